# Optimizing a Trainium2 kernel written in Bass

```python
import jax, jax.numpy as jnp
from jax import lax
import numpy as np

D_MODEL = 2048
BATCH = 2
SEQ = 8192
DEPTH = 1

GRID_W = 64
CTX_LEN = 256
MIX_WIDTH = D_MODEL
RET_WIDTH = MIX_WIDTH // 2
RET_HEADS = 8
RET_HEAD_DIM = RET_WIDTH // RET_HEADS
CONV_CH = MIX_WIDTH - RET_WIDTH
CONV_TAPS = 3
IN_WIDTH = 4 * RET_WIDTH + 3 * CONV_CH
D_FF = ((8 * D_MODEL // 3 + 255) // 256) * 256
CHUNK = 128
ROPE_THETA = 10000.0
EPS = 1e-6
N_MOD = 6

kernel_name = 'hybrid_retention_shortconv_flow_block'


def rms_norm(x, gain):
    xf = x.astype(jnp.float32)
    y = xf * lax.rsqrt(jnp.mean(xf * xf, axis=-1, keepdims=True) + EPS)
    return (y * gain.astype(jnp.float32)).astype(x.dtype)


def modulate(h, shift, scale):
    return h * (1 + scale) + shift


def split_heads(t):
    b, n, _ = t.shape
    return t.reshape(b, n, RET_HEADS, RET_HEAD_DIM).transpose(0, 2, 1, 3)


def merge_heads(t):
    b, h, n, d = t.shape
    return t.transpose(0, 2, 1, 3).reshape(b, n, h * d)


def head_rms_norm(o):
    of = o.astype(jnp.float32)
    return (of * lax.rsqrt(jnp.mean(of * of, axis=-1, keepdims=True) + EPS)).astype(o.dtype)


def rope1d(x, pos):
    d = x.shape[-1]
    half = d // 2
    freqs = ROPE_THETA ** (-jnp.arange(0, d, 2, dtype=jnp.float32) / d)
    ang = pos[:, None] * freqs[None, :]
    cos = jnp.cos(ang).astype(x.dtype)
    sin = jnp.sin(ang).astype(x.dtype)
    x1, x2 = x[..., :half], x[..., half:]
    return jnp.concatenate([x1 * cos - x2 * sin, x1 * sin + x2 * cos], axis=-1)


def rope2d(x, row, col):
    half = x.shape[-1] // 2
    return jnp.concatenate([rope1d(x[..., :half], row), rope1d(x[..., half:], col)], axis=-1)


def conv3_along(x, w, axis):
    n = x.shape[axis]
    pad = [(0, 0)] * x.ndim
    pad[axis] = (1, 1)
    xp = jnp.pad(x, pad)
    taps = [lax.slice_in_dim(xp, i, i + n, axis=axis) for i in range(CONV_TAPS)]
    return taps[0] * w[0] + taps[1] * w[1] + taps[2] * w[2]


def dwconv3(x, w, rows, vertical):
    if rows is None:
        return conv3_along(x, w, 1)
    b, n, ch = x.shape
    xg = x.reshape(b, rows, GRID_W, ch)
    return conv3_along(xg, w, 1 if vertical else 2).reshape(b, n, ch)


def retention_dir(q, k, v, log_gamma, s0, strict):
    b, h, n, dk = q.shape
    dv = v.shape[-1]
    nc = n // CHUNK
    dt = q.dtype
    qc = q.reshape(b, h, nc, CHUNK, dk)
    kc = k.reshape(b, h, nc, CHUNK, dk)
    vc = v.reshape(b, h, nc, CHUNK, dv)
    idx = jnp.arange(CHUNK, dtype=jnp.float32)
    rel = idx[:, None] - idx[None, :]
    mask = (rel > 0) if strict else (rel >= 0)
    lg = log_gamma[:, None, None]
    dmat = jnp.where(mask[None], jnp.exp(lg * jnp.maximum(rel, 0.0)[None]), 0.0).astype(dt)
    scores = jnp.einsum('bhnid,bhnjd->bhnij', qc, kc) * dmat[None, :, None]
    inner = jnp.einsum('bhnij,bhnje->bhnie', scores, vc)
    k_decay = jnp.exp(log_gamma[:, None] * (CHUNK - 1.0 - idx)[None]).astype(dt)
    kv = jnp.einsum('bhnjd,hj,bhnje->nbhde', kc, k_decay, vc)
    chunk_decay = jnp.exp(log_gamma * CHUNK).astype(dt)[None, :, None, None]

    def step(s, kv_n):
        return chunk_decay * s + kv_n, s

    s_final, s_prev = lax.scan(step, s0, kv)
    q_decay = jnp.exp(log_gamma[:, None] * (idx + 1.0)[None]).astype(dt)
    cross = jnp.einsum('bhnid,nbhde->bhnie', qc, s_prev) * q_decay[None, :, None, :, None]
    return (inner + cross).reshape(b, h, n, dv), s_final


def context_final_state(k, v, log_gamma, reverse):
    l = k.shape[2]
    t = jnp.arange(l, dtype=jnp.float32)
    expo = t if reverse else (l - 1.0 - t)
    w = jnp.exp(log_gamma[:, None] * expo[None]).astype(k.dtype)
    return jnp.einsum('bhtd,ht,bhte->bhde', k, w, v)


def token_mixers(p, s0_f, s0_b, log_gf, log_gb, conv_w, w_out, rows, grid_pos):
    r, cc = RET_WIDTH, CONV_CH
    q = split_heads(p[..., 0:r])
    k = split_heads(p[..., r:2 * r])
    v = split_heads(p[..., 2 * r:3 * r])
    g = p[..., 3 * r:4 * r]
    bg = p[..., 4 * r:4 * r + cc]
    cg = p[..., 4 * r + cc:4 * r + 2 * cc]
    hv = p[..., 4 * r + 2 * cc:4 * r + 3 * cc]
    if grid_pos is not None:
        q = rope2d(q, grid_pos[0], grid_pos[1])
        k = rope2d(k, grid_pos[0], grid_pos[1])
    k = k * (RET_HEAD_DIM ** -0.5)
    o_f, s_f = retention_dir(q, k, v, log_gf, s0_f, False)
    o_b_rev, s_b = retention_dir(jnp.flip(q, 2), jnp.flip(k, 2), jnp.flip(v, 2), log_gb, s0_b, True)
    o = head_rms_norm(o_f + jnp.flip(o_b_rev, 2))
    ret = merge_heads(o) * jax.nn.silu(g)
    conv = bg * dwconv3(cg * hv, conv_w, rows, vertical=False)
    mix = jnp.concatenate([ret, conv], axis=-1) @ w_out
    return mix, s_f, s_b


def conv_ffn(h, w_up, conv_w, conv_b, w_down, rows):
    u = h @ w_up
    a, b = u[..., :D_FF], u[..., D_FF:]
    a = dwconv3(a, conv_w, rows, vertical=True) + conv_b
    return (jax.nn.silu(a) * b) @ w_down


def setup_inputs(seed: int = 0) -> dict:
    key = jax.random.key(seed)
    ks = jax.random.split(key, 20)
    f32 = jnp.float32

    def nrm(k, shape, scale):
        return jax.random.normal(k, shape, f32) * scale

    decay_logit = jnp.asarray(np.log(2.0 ** (5 + np.arange(RET_HEADS)) - 1.0).astype(np.float32))
    return {
        'x': nrm(ks[0], (BATCH, SEQ, D_MODEL), 1.0),
        'c': nrm(ks[1], (BATCH, D_MODEL), 1.0),
        'ctx': nrm(ks[2], (BATCH, CTX_LEN, D_MODEL), 1.0),
        'c_ctx': nrm(ks[3], (D_MODEL,), 1.0),
        'w_mod': nrm(ks[4], (DEPTH, D_MODEL, N_MOD * D_MODEL), D_MODEL ** -0.5),
        'b_mod': nrm(ks[5], (DEPTH, N_MOD * D_MODEL), 0.01),
        'norm1_g': 1.0 + nrm(ks[6], (DEPTH, D_MODEL), 0.02),
        'w_in': nrm(ks[7], (DEPTH, D_MODEL, IN_WIDTH), D_MODEL ** -0.5),
        'ret_decay_fwd': decay_logit[None] + nrm(ks[8], (DEPTH, RET_HEADS), 0.1),
        'ret_decay_bwd': decay_logit[None] + nrm(ks[9], (DEPTH, RET_HEADS), 0.1),
        'conv_w': nrm(ks[10], (DEPTH, CONV_TAPS, CONV_CH), CONV_TAPS ** -0.5),
        'w_out': nrm(ks[11], (DEPTH, MIX_WIDTH, D_MODEL), MIX_WIDTH ** -0.5),
        'norm2_g': 1.0 + nrm(ks[12], (DEPTH, D_MODEL), 0.02),
        'w_up': nrm(ks[13], (DEPTH, D_MODEL, 2 * D_FF), D_MODEL ** -0.5),
        'ffn_conv_w': nrm(ks[14], (DEPTH, CONV_TAPS, D_FF), CONV_TAPS ** -0.5),
        'ffn_conv_b': nrm(ks[15], (DEPTH, D_FF), 0.01),
        'w_down': nrm(ks[16], (DEPTH, D_FF, D_MODEL), D_FF ** -0.5),
        'final_g': 1.0 + nrm(ks[17], (D_MODEL,), 0.02),
    }


def reference(x, c, ctx, c_ctx, w_mod, b_mod, norm1_g, w_in, ret_decay_fwd, ret_decay_bwd,
              conv_w, w_out, norm2_g, w_up, ffn_conv_w, ffn_conv_b, w_down, final_g):
    b, n, _ = x.shape
    rows = n // GRID_W
    pos = jnp.arange(n, dtype=jnp.int32)
    grid_pos = ((pos // GRID_W).astype(jnp.float32), (pos % GRID_W).astype(jnp.float32))
    r = RET_WIDTH

    for layer in range(DEPTH):
        last = layer == DEPTH - 1
        mod = (jax.nn.silu(c) @ w_mod[layer] + b_mod[layer])[:, None, :]
        mod_c = jax.nn.silu(c_ctx) @ w_mod[layer] + b_mod[layer]
        sh1, sc1, g1, sh2, sc2, g2 = jnp.split(mod, N_MOD, axis=-1)
        csh1, csc1, cg1, csh2, csc2, cg2 = jnp.split(mod_c, N_MOD, axis=-1)
        log_gf = jax.nn.log_sigmoid(ret_decay_fwd[layer].astype(jnp.float32))
        log_gb = jax.nn.log_sigmoid(ret_decay_bwd[layer].astype(jnp.float32))

        hc = modulate(rms_norm(ctx, norm1_g[layer]), csh1, csc1)
        if last:
            pc = hc @ w_in[layer][:, r:3 * r]
            kc = split_heads(pc[..., :r]) * (RET_HEAD_DIM ** -0.5)
            vc = split_heads(pc[..., r:])
            s_f = context_final_state(kc, vc, log_gf, reverse=False)
            s_b = context_final_state(kc, vc, log_gb, reverse=True)
        else:
            zeros = jnp.zeros((b, RET_HEADS, RET_HEAD_DIM, RET_HEAD_DIM), hc.dtype)
            mix_c, s_f, s_b = token_mixers(hc @ w_in[layer], zeros, zeros, log_gf, log_gb,
                                           conv_w[layer], w_out[layer], None, None)
            ctx_mid = ctx + cg1 * mix_c
            hc2 = modulate(rms_norm(ctx_mid, norm2_g[layer]), csh2, csc2)
            ctx_next = ctx_mid + cg2 * conv_ffn(hc2, w_up[layer], ffn_conv_w[layer],
                                                ffn_conv_b[layer], w_down[layer], None)

        h = modulate(rms_norm(x, norm1_g[layer]), sh1, sc1)
        mix, _, _ = token_mixers(h @ w_in[layer], s_f, s_b, log_gf, log_gb,
                                 conv_w[layer], w_out[layer], rows, grid_pos)
        x = x + g1 * mix
        h2 = modulate(rms_norm(x, norm2_g[layer]), sh2, sc2)
        x = x + g2 * conv_ffn(h2, w_up[layer], ffn_conv_w[layer], ffn_conv_b[layer],
                              w_down[layer], rows)
        if not last:
            ctx = ctx_next

    return rms_norm(x, final_g)
```

```python
import numpy as np
import concourse.bass as bass
import concourse.mybir as mybir
from concourse.bass_utils import run_bass_kernel_spmd
from contextlib import ExitStack

F32 = mybir.dt.float32
BF16 = mybir.dt.bfloat16
AF = mybir.ActivationFunctionType
ALU = mybir.AluOpType
AX = mybir.AxisListType

D = 2048
KT = 16
NH = 8
DFF = 5632
NFT = 44
NO = 49
NE = 18
EPS = 1e-6
DEBUG = False
DVE_EVAC = False
STRICT = False


class Buf:
    __slots__ = ("name", "writers", "readers", "prev")

    def __init__(self, name):
        self.name = name
        self.writers = []
        self.readers = []
        self.prev = []


class Op:
    __slots__ = ("eng", "fn", "deps", "signal", "sem", "val", "is_dma")


def _prune(lst, o):
    lst[:] = [p for p in lst if p.sem != o.sem]
    lst.append(o)


class Prog:
    def __init__(self, nc):
        self.nc = nc
        self.ops = []
        self.engs = dict(pe=nc.tensor, act=nc.scalar, dve=nc.vector, pool=nc.gpsimd, sp=nc.sync)
        self.sems = []
        self.esem = {}
        for e in ("pe", "act", "dve", "pool"):
            self.esem[e] = self._newsem("p_" + e)
        self.dsem = {}
        self.dpool = []
        self.bar = []
        self.last = {}

    def _newsem(self, name):
        self.sems.append(self.nc.alloc_semaphore(name=name))
        return len(self.sems) - 1

    def op(self, eng, fn, r=(), w=(), wp=(), dma=None):
        o = Op()
        o.eng = eng
        o.fn = fn
        o.is_dma = dma is not None
        o.signal = o.is_dma
        o.val = None
        if o.is_dma:
            if dma not in self.dsem:
                k = len(self.dsem)
                if k >= len(self.dpool):
                    self.dpool.append(self._newsem("d%d" % k))
                self.dsem[dma] = self.dpool[k]
            o.sem = self.dsem[dma]
        else:
            o.sem = self.esem[eng]
        deps = []
        for p in self.bar:
            deps.append((p, "raw"))
        for b in r:
            for p in b.writers:
                deps.append((p, "raw"))
        for b, partial in [(x, False) for x in w] + [(x, True) for x in wp]:
            if b.readers:
                b.prev = b.readers + b.writers
                b.readers = []
                b.writers = []
            for p in b.prev:
                deps.append((p, "war"))
            if not partial:
                for p in b.writers:
                    deps.append((p, "waw"))
        for b in r:
            _prune(b.readers, o)
        for b in list(w) + list(wp):
            _prune(b.writers, o)
        o.deps = []
        for p, kind in deps:
            if p is o:
                continue
            if (not p.is_dma) and p.eng == eng and (eng == "pe" or (kind == "war" and not STRICT)):
                continue
            p.signal = True
            o.deps.append(p)
        self.ops.append(o)
        self.last[(eng, o.sem)] = o
        return o

    def barrier(self):
        self.bar = list(self.last.values())

    def new_phase(self):
        self.barrier()
        self.dsem = {}

    def emit(self):
        cnt = [0] * len(self.sems)
        known = {}
        for o in self.ops:
            E = self.engs[o.eng]
            need = {}
            for p in o.deps:
                if need.get(p.sem, 0) < p.val:
                    need[p.sem] = p.val
            for s, v in need.items():
                if known.get((o.eng, s), 0) < v:
                    E.wait_ge(self.sems[s], v)
                    known[(o.eng, s)] = v
            ins = o.fn(E)
            if o.signal:
                inc = 16 if o.is_dma else 1
                cnt[o.sem] += inc
                o.val = cnt[o.sem]
                ins.then_inc(self.sems[o.sem], inc)
            else:
                o.val = cnt[o.sem] + 1
        for s in self.dpool:
            if cnt[s] > 0:
                self.nc.sync.wait_ge(self.sems[s], cnt[s])
        assert max(cnt) < 60000, max(cnt)


def build_nc():
    nc = bass.Bass("TRN2", target_bir_lowering=False)
    P = Prog(nc)

    def din(name, shape, dt=F32):
        return nc.dram_tensor(name, list(shape), dt, kind="ExternalInput").ap()

    def dscr(name, shape, dt):
        kind = "ExternalOutput" if (DEBUG and name in DEBUG_OUT) else "Internal"
        return nc.dram_tensor(name, list(shape), dt, kind=kind).ap()

    xo = din("xo", [NO, 128, D])
    xe = din("xe", [NE, 128, D])
    ropek = din("ropek", [NO + NE, 128, 128])
    ropeq = din("ropeq", [NE, 128, 128])
    efb = din("efb", [128, 2, NO])
    mfb = din("mfb", [128, 2, NO])
    vfl = din("vfl", [128, NO + NE])
    afl = din("afl", [128, 2])
    cT = din("cT", [128, KT, 33])
    w_mod = din("w_mod", [D, 6 * D])
    bm = din("bm", [33, 6 * D])
    gn = din("gn", [128, 2, KT])
    fgb = din("fgb", [128, D])
    w_in = din("w_in", [D, 7168])
    w_out = din("w_out", [D, D])
    w_up = din("w_up", [D, 2 * DFF])
    w_down = din("w_down", [DFF, D])
    dlog = din("dlog", [128, 16])
    cw = din("cw", [128, 8, 3])
    cw2 = din("cw2", [128, NFT, 4])
    ctab = din("ctab", [128, 6, 128])
    pcol = din("pcol", [128, 2])
    ident = din("ident", [128, 128])
    out = nc.dram_tensor("out", [16, 128, D], F32, kind="ExternalOutput").ap()

    HT = dscr("HT", [128, KT, NE * 128], BF16)
    KTs = dscr("KTs", [NE, 128, NH, 128], BF16)
    Vs = dscr("Vs", [NE, 128, 1024], BF16)
    KVB = dscr("KVB", [NE, 128, 1024], F32)
    QTs = dscr("QTs", [NE, 128, NH, 128], BF16)
    Gs = dscr("Gs", [NE, 128, 1024], BF16)
    CTs = dscr("CTs", [8, 128, NE * 128], BF16)
    SFs = dscr("SFs", [NE, 128, 1024], BF16)
    SBs = dscr("SBs", [NE, 128, 1024], BF16)
    XM = dscr("XM", [NE, 128, D], F32)
    H2T = dscr("H2T", [128, KT, NE * 128], BF16)
    UT = dscr("UT", [16, 128, NFT, 128], BF16)
    XO = dscr("XO", [16, 128, D], F32)
    dbuf = {}

    def DB(name):
        if name not in dbuf:
            dbuf[name] = Buf(name)
        return dbuf[name]

    TMT = dscr("TMT", [128, 1024], F32)
    TQD = dscr("TQD", [128, 2048], F32)
    TG1 = dscr("TG1", [128, D], F32)
    TG2 = dscr("TG2", [128, D], F32)
    RTs = dscr("RTs", [NE, 128, NH, 128], BF16)

    gstack = ExitStack()
    pstack = [ExitStack()]

    class Tile:
        def __init__(self, name, shape, dt, st):
            self.t = st.enter_context(nc.sbuf_tensor(name, list(shape), dt))
            self.b = Buf(name)

        def __getitem__(self, idx):
            return self.t[idx]

    def sb(name, shape, dt=F32, persist=False, st=None):
        if st is None:
            st = gstack if persist else pstack[0]
        return Tile(name, shape, dt, st)

    def end_phase():
        tick()
        tick()
        P.new_phase()
        pstack[0].close()
        pstack[0] = ExitStack()

    PS = []
    for i in range(4):
        t = gstack.enter_context(nc.psum_tensor("ps%d" % i, [128, 1024], F32))
        PS.append((t, [Buf("ps%da" % i), Buf("ps%db" % i)]))

    def ps_f32(i, half=None):
        t, bs = PS[i]
        if half is None:
            return t[:, :], list(bs)
        return t[:, half * 512:(half + 1) * 512], [bs[half]]

    def ps_bf16(i):
        t, bs = PS[i]
        return t[:, :].bitcast(BF16), list(bs)

    def dma(q, o, i, r, w, key, wp=()):
        return P.op(q, lambda E: E.dma_start(out=o, in_=i), r=r, w=w, wp=wp, dma=key)

    pend = [[], []]
    STQ = "sp"
    MODQ = "act"

    def store(o, i, r, wbufs, key):
        pend[1].append(lambda: dma(STQ, o, i, r, [], key, wp=wbufs))

    def tick():
        for f in pend[0]:
            f()
        pend[0] = pend[1]
        pend[1] = []

    def act(o, i, func, r, w, wp=(), **kw):
        return P.op("act", lambda E: E.activation(out=o, in_=i, func=func, **kw), r=r, w=w, wp=wp)

    def tt(eng, o, a, b, op, r, w, wp=()):
        return P.op(eng, lambda E: E.tensor_tensor(out=o, in0=a, in1=b, op=op), r=r, w=w, wp=wp)

    def ts(eng, o, a, s1, s2, op0, op1, r, w, wp=()):
        if op1 is None:
            return P.op(eng, lambda E: E.tensor_scalar(out=o, in0=a, scalar1=s1, scalar2=None, op0=op0),
                        r=r, w=w, wp=wp)
        return P.op(eng, lambda E: E.tensor_scalar(out=o, in0=a, scalar1=s1, scalar2=s2, op0=op0, op1=op1),
                    r=r, w=w, wp=wp)

    def stt(o, a, s, b, op0, op1, r, w, wp=()):
        return P.op("dve", lambda E: E.scalar_tensor_tensor(out=o, in0=a, scalar=s, in1=b, op0=op0, op1=op1),
                    r=r, w=w, wp=wp)

    def cp(eng, o, i, r, w, wp=()):
        if eng == "act":
            return act(o, i, AF.Copy, r, w, wp)
        return P.op(eng, lambda E: E.tensor_copy(out=o, in_=i), r=r, w=w, wp=wp)

    def mm(o, l, rh, st, sp_, r, wp):
        return P.op("pe", lambda E: E.matmul(o, lhsT=l, rhs=rh, start=st, stop=sp_), r=r, wp=wp)

    def tr(o, i, r, wp):
        return P.op("pe", lambda E: E.transpose(out=o, in_=i, identity=IDB[:, :]), r=list(r) + [IDB.b], wp=wp)

    def rsq(o, oi, ob, scale):
        ts("dve", o, oi, scale, EPS, ALU.mult, ALU.add, [ob], [ob])
        act(o, o, AF.Sqrt, [ob], [ob])
        P.op("dve", lambda E: E.reciprocal(out=o, in_=o), r=[ob], w=[ob])

    def h3(ap):
        return ap.rearrange("p (h e) -> p h e", h=NH)

    IDB = sb("idb", [128, 128], BF16, True)
    LG = sb("lg", [128, 16], F32, True)
    KD = sb("kd", [128, 16], F32, True)
    G128 = sb("g128", [128, 16], F32, True)
    COEF = sb("coef", [128, 2, NO, NH], F32, True)
    MODF = sb("modf", [128, 6, KT], F32, True)
    GN = sb("gn_s", [128, 2, KT], F32, True)
    VFL = sb("vfl_s", [128, NO + NE], F32, True)
    AFL = sb("afl_s", [128, 2], F32, True)
    CW = sb("cw_s", [128, 8, 3], F32, True)
    CW2 = sb("cw2_s", [128, NFT, 4], F32, True)
    SNAP = [sb("snapf", [128, 1024], F32, True), sb("snapb", [128, 1024], F32, True)]
    ONES = sb("ones", [33, 128], F32, True)
    ST_ = sb("sT_s", [128, KT, 33], F32, True)
    STB = sb("sT_b", [128, KT, 33], BF16, True)
    wst_n = [0]

    def make_wload(nst, engines, st=None):
        WST = [sb("wst%d_%d" % (wst_n[0], i), [128, 2048], F32, False, st) for i in range(nst)]
        wst_n[0] += 1
        cnt = [0]
        tag = wst_n[0]

        def wload(dst, src, dstbuf, a=None):
            k = cnt[0]
            cnt[0] += 1
            stg = WST[k % nst]
            sv = stg[:, :] if a is None else stg[:, :].rearrange("p (a b) -> p a b", a=a)
            dma("sp", sv, src, [], [stg.b], "wst%d_%d" % (tag, k % nst))
            cp(engines[k % len(engines)], dst, sv, [stg.b], [], wp=[dstbuf])
        return wload

    winv = w_in.rearrange("(kt p) c -> p kt c", p=128)

    stA = ExitStack()
    WKV = sb("wkv", [128, KT, 2048], BF16, False, stA)
    WKVb = [Buf("wkv%d" % i) for i in range(KT)]
    wlA = make_wload(2, ("pool", "dve"), stA)

    MT = sb("mt", [128, NH, 128])
    QDT = sb("qdt", [128, 2, NH, 128])
    CTAB = sb("ctab_s", [128, 6, 128])
    PCOL = sb("pcol_s", [128, 2])
    ID32 = sb("id32", [128, 128])
    DL = sb("dl", [128, 16])
    EFB = sb("efb_s", [128, 2, NO])
    MFB = sb("mfb_s", [128, 2, NO])
    CT_ = sb("cT_s", [128, KT, 33])
    for (t, src) in [(ID32, ident), (DL, dlog), (EFB, efb), (MFB, mfb), (GN, gn), (VFL, vfl), (AFL, afl),
                     (CW, cw), (CW2, cw2), (CTAB, ctab), (PCOL, pcol), (CT_, cT)]:
        full = tuple([slice(None)] * len(src.shape))
        dma("sp", t[full], src[full], [], [t.b], "cst")
    P.barrier()
    act(IDB[:, :], ID32[:, :], AF.Copy, [ID32.b], [IDB.b])
    P.op("pool", lambda E: E.memset(ONES[:, :], 1.0), w=[ONES.b])
    P.op("pool", lambda E: E.memset(SNAP[0][:, :], 0.0), w=[SNAP[0].b])
    P.op("pool", lambda E: E.memset(SNAP[1][:, :], 0.0), w=[SNAP[1].b])
    act(ST_[:, :, :], CT_[:, :, :], AF.Silu, [CT_.b], [ST_.b])
    act(STB[:, :, :], ST_[:, :, :], AF.Copy, [ST_.b], [STB.b])

    wmv = w_mod.rearrange("(kt p) c -> p kt c", p=128)

    def mod_items(vs, WM, BMB, ROW, GB_, psa, psv, nq=2, WMB=None):
        step = [0]
        seq = [(v, j) for v in vs for j in range(4)]
        issued = set()

        ncol = 512 // nq

        def issue(idx, hf):
            if idx >= len(seq) or (idx, hf) in issued:
                return
            issued.add((idx, hf))
            v, j = seq[idx]
            c0 = v * D + j * 512 + hf * ncol
            wm = WM[hf % len(WM)]
            dma(MODQ, wm[:, :, :], wmv[:, :, c0:c0 + ncol], [], [wm.b], "wm%d" % (hf % len(WM)))

        for idx, (v, j) in enumerate(seq):
            def item(idx=idx, v=v, j=j):
                col0 = v * D + j * 512
                pt, pb = ps_f32(*psa[step[0] % len(psa)])
                bmb = BMB[step[0] % 2]
                dma("sp", bmb[:, :], bm[:, col0:col0 + 512], [], [bmb.b], "bmb%d" % (step[0] % 2))
                for hf in range(nq):
                    wm = WM[hf % len(WM)]
                    issue(idx, hf)
                    lhs = ST_
                    if WMB is not None:
                        wmb = WMB[hf % len(WMB)]
                        cp(("act", "dve")[hf % 2], wmb[:, :, :], wm[:, :, :], [wm.b], [wmb.b])
                        wm = wmb
                        lhs = STB
                    for kt in range(KT):
                        mm(pt[0:33, hf * ncol:(hf + 1) * ncol], lhs[:, kt, :], wm[:, kt, :], kt == 0,
                           kt == KT - 1, [lhs.b, wm.b], pb)
                tt("dve", ROW[:, j * 512:(j + 1) * 512], pt[0:33, :], bmb[:, :], ALU.add, pb + [bmb.b], [],
                   wp=[ROW.b])
                step[0] += 1
            yield item
            if j != 3:
                continue

            def fin(v=v):
                if v in (2, 5):
                    for j in range(4):
                        pt, pb = ps_f32(*psa[j % len(psa)])
                        mm(pt[:, :], ONES[0:1, :], ROW[0:1, j * 512:(j + 1) * 512], True, True, [ONES.b, ROW.b], pb)
                        act(GB_[:, j * 512:(j + 1) * 512], pt[:, :], AF.Copy, pb, [], wp=[GB_.b])
                    dma(STQ, (TG1 if v == 2 else TG2)[:, :], GB_[:, :], [GB_.b], [], "st_gb", wp=[DB("TG%d" % v)])
                else:
                    rows = [(0, {0: 0, 1: 1, 3: 2, 4: 3}[v])]
                    if v in (0, 1):
                        rows.append((32, 4 + v))
                    for (rw, slot) in rows:
                        pt, pb = ps_f32(*psv)
                        for kt in range(KT):
                            mm(pt[:, kt:kt + 1], ROW[rw:rw + 1, kt * 128:(kt + 1) * 128], ONES[rw:rw + 1, 0:1],
                               True, True, [ROW.b, ONES.b], pb)
                        if v in (1, 4):
                            gi = 0 if v == 1 else 1
                            stt(MODF[:, slot, :], pt[:, 0:KT], 1.0, GN[:, gi, :], ALU.add, ALU.mult, pb + [GN.b], [],
                                wp=[MODF.b])
                        else:
                            act(MODF[:, slot, :], pt[:, 0:KT], AF.Copy, pb, [], wp=[MODF.b])
            yield fin

    WM = [sb("wm%d" % i, [128, KT, 256]) for i in range(2)]
    BMB = [sb("bmb%d" % i, [33, 512]) for i in range(2)]
    ROW = sb("row", [33, D])
    for it_ in mod_items((0, 1), WM, BMB, ROW, None, [(0, 0), (1, 0), (2, 0)], (3, 1)):
        it_()
    for kt in range(KT):
        wlA(WKV[:, kt, :], winv[:, kt, 1024:3072], WKVb[kt])

    TMPS = sb("tmps", [128, 16])
    act(TMPS[:, :], DL[:, :], AF.Exp, [DL.b], [TMPS.b], scale=-1.0)
    ts("dve", TMPS[:, :], TMPS[:, :], 1.0, None, ALU.add, None, [TMPS.b], [TMPS.b])
    act(LG[:, :], TMPS[:, :], AF.Ln, [TMPS.b], [LG.b])
    ts("dve", LG[:, :], LG[:, :], -1.0, None, ALU.mult, None, [LG.b], [LG.b])
    TA = sb("ta", [128, 128])
    TB = sb("tb", [128, 128])
    for h in range(NH):
        act(TA[:, :], CTAB[:, 0, :], AF.Exp, [CTAB.b, LG.b], [TA.b], scale=LG[:, h:h + 1])
        tt("dve", TA[:, :], TA[:, :], CTAB[:, 2, :], ALU.mult, [TA.b, CTAB.b], [TA.b])
        act(TB[:, :], CTAB[:, 1, :], AF.Exp, [CTAB.b, LG.b], [TB.b], scale=LG[:, 8 + h:9 + h])
        tt("dve", TB[:, :], TB[:, :], CTAB[:, 3, :], ALU.mult, [TB.b, CTAB.b], [TB.b])
        tt("dve", MT[:, h, :], TA[:, :], TB[:, :], ALU.add, [TA.b, TB.b], [], wp=[MT.b])
        act(QDT[:, 0, h, :], CTAB[:, 4, :], AF.Exp, [CTAB.b, LG.b], [], wp=[QDT.b], scale=LG[:, h:h + 1])
        act(QDT[:, 1, h, :], CTAB[:, 5, :], AF.Exp, [CTAB.b, LG.b], [], wp=[QDT.b], scale=LG[:, 8 + h:9 + h])
    dma(STQ, TMT[:, :], MT[:, :, :].rearrange("p h i -> p (h i)"), [MT.b], [], "st_mt", wp=[DB("TMT")])
    dma(STQ, TQD[:, :], QDT[:, :, :, :].rearrange("p a h i -> p (a h i)"), [QDT.b], [], "st_qd", wp=[DB("TQD")])
    act(KD[:, 0:8], LG[:, 0:8], AF.Exp, [LG.b, PCOL.b], [], wp=[KD.b], scale=PCOL[:, 0:1])
    act(KD[:, 8:16], LG[:, 8:16], AF.Exp, [LG.b, PCOL.b], [], wp=[KD.b], scale=PCOL[:, 1:2])
    act(G128[:, :], LG[:, :], AF.Exp, [LG.b], [G128.b], scale=128.0)
    TC = sb("tc", [128, NO])
    for d_ in range(2):
        for h in range(NH):
            act(TC[:, :], EFB[:, d_, :], AF.Exp, [EFB.b, LG.b], [TC.b], scale=LG[:, 8 * d_ + h:8 * d_ + h + 1])
            tt("dve", COEF[:, d_, :, h], TC[:, :], MFB[:, d_, :], ALU.mult, [TC.b, MFB.b], [], wp=[COEF.b])
    end_phase()

    def rms_n(src_t, ssq, rstd, xn, sqj):
        act(sqj[:, :], src_t[:, :], AF.Square, [src_t.b], [ssq.b], wp=[sqj.b], accum_out=ssq[:, 0:1])
        ts("dve", rstd[:, :], ssq[:, :], 1.0 / D, EPS, ALU.mult, ALU.add, [ssq.b], [rstd.b])
        act(rstd[:, :], rstd[:, :], AF.Sqrt, [rstd.b], [rstd.b])
        P.op("dve", lambda E: E.reciprocal(out=rstd[:, :], in_=rstd[:, :]), r=[rstd.b], w=[rstd.b])
        act(xn[:, :], src_t[:, :], AF.Copy, [src_t.b, rstd.b], [xn.b], scale=rstd[:, 0:1])

    def tr16(xn, psi, ht, so, sh):
        pst, psb = ps_bf16(psi)
        pv = pst.rearrange("p (k t) -> p k t", t=128)
        for kt in range(KT):
            tr(pv[:, kt, :], xn[:, kt * 128:(kt + 1) * 128], [xn.b], [psb[kt // 8]])
        for kt in range(KT):
            if kt % 2 == 0 or not DVE_EVAC:
                act(ht[:, kt, :], pv[:, kt, :], AF.Identity, [psb[kt // 8], MODF.b], [], wp=[ht.b],
                    scale=MODF[:, so, kt:kt + 1], bias=MODF[:, sh, kt:kt + 1])
            else:
                ts("dve", ht[:, kt, :], pv[:, kt, :], MODF[:, so, kt:kt + 1], MODF[:, sh, kt:kt + 1],
                   ALU.mult, ALU.add, [psb[kt // 8], MODF.b], [], wp=[ht.b])

    def rope(pt, pb, rt, outt, outb, nh, R1, R2):
        x4 = pt.rearrange("p (h a b f) -> p h a b f", h=nh, a=2, b=2, f=32)
        o4 = outt.rearrange("p (h a b f) -> p h a b f", h=nh, a=2, b=2, f=32)
        cosv = rt[:, 0:64].rearrange("p (a f) -> p a f", a=2).unsqueeze(1).broadcast_to([128, nh, 2, 32])
        sinv = rt[:, 64:128].rearrange("p (a f) -> p a f", a=2).unsqueeze(1).broadcast_to([128, nh, 2, 32])
        n = nh * 64
        t1 = R1[:, 0:n].rearrange("p (h a f) -> p h a f", h=nh, a=2)
        t2 = R2[:, 0:n].rearrange("p (h a f) -> p h a f", h=nh, a=2)
        x1 = x4[:, :, :, 0, :]
        x2 = x4[:, :, :, 1, :]
        tt("dve", t1, x1, cosv, ALU.mult, pb + [rt.b], [R1.b])
        tt("dve", t2, x2, sinv, ALU.mult, pb + [rt.b], [R2.b])
        tt("dve", o4[:, :, :, 0, :], t1, t2, ALU.subtract, [R1.b, R2.b], [], wp=[outb])
        tt("dve", t1, x1, sinv, ALU.mult, pb + [rt.b], [R1.b])
        tt("dve", t2, x2, cosv, ALU.mult, pb + [rt.b], [R2.b])
        tt("dve", o4[:, :, :, 1, :], t1, t2, ALU.add, [R1.b, R2.b], [], wp=[outb])

    XB = [sb("xb%d" % i, [128, D]) for i in range(2)]
    RT = [sb("rt%d" % i, [128, 128]) for i in range(2)]
    SQJ = sb("sqj", [128, D], BF16)
    SSQ = [sb("ssq%d" % i, [128, 1]) for i in range(2)]
    RSTD = [sb("rstd%d" % i, [128, 1]) for i in range(2)]
    XN = [sb("xn%d" % i, [128, D], BF16) for i in range(2)]
    HTt = [sb("ht%d" % i, [128, KT, 128], BF16) for i in range(2)]
    KR = [sb("kr%d" % i, [128, 1024], BF16) for i in range(2)]
    KF = [sb("kf%d" % i, [128, 1024], BF16) for i in range(2)]
    KB_ = [sb("kb%d" % i, [128, 1024], BF16) for i in range(2)]
    VB = [sb("vb%d" % i, [128, 1024], BF16) for i in range(2)]
    R1 = sb("r1", [128, 512])
    R2 = sb("r2", [128, 512])
    STMP = sb("stmp", [128, 1024])
    KTS = [sb("kts%d" % i, [128, NH, 128], BF16) for i in range(2)]
    SFB = [sb("sfb%d" % i, [128, 1024], BF16) for i in range(2)]
    KVST = [sb("kvst%d" % i, [128, 1024]) for i in range(2)]
    NS = NO + NE

    def A_N(i):
        if i >= NS:
            return
        xb = XB[i % 2]
        src = xo[i] if i < NO else xe[i - NO]
        dma("sp", xb[:, :], src, [], [xb.b], "xb%d" % (i % 2))
        rms_n(xb, SSQ[i % 2], RSTD[i % 2], XN[i % 2], SQJ)

    def A_T(i):
        if i >= NS:
            return
        ht = HTt[i % 2]
        so, sh = (5, 4) if i < 2 else (1, 0)
        tr16(XN[i % 2], 0, ht, so, sh)
        if i >= NO:
            m = i - NO
            store(HT[:, :, m * 128:(m + 1) * 128], ht[:, :, :], [ht.b], [DB("HT")], "st_ht%d" % (i % 2))

    def A_RT(i):
        if i >= NS:
            return
        rt = RT[i % 2]
        dma("sp", rt[:, :], ropek[i], [], [rt.b], "rt%d" % (i % 2))

    def A_MM(i, which):
        if i >= NS:
            return
        ht = HTt[i % 2]
        pt, pb = ps_f32(1 if which == 0 else 2)
        for cgi in range(2):
            cg = which * 2 + cgi
            for kt in range(KT):
                mm(pt[:, cgi * 512:(cgi + 1) * 512], ht[:, kt, :], WKV[:, kt, cg * 512:(cg + 1) * 512],
                   kt == 0, kt == KT - 1, [ht.b, WKVb[kt]], [pb[cgi]])

    def A_s3(i):
        rt = RT[i % 2]
        kr, kf, kb, vb = KR[i % 2], KF[i % 2], KB_[i % 2], VB[i % 2]
        pk, pkb = ps_f32(1)
        pv_, pvb = ps_f32(2)
        act(vb[:, :], pv_, AF.Copy, pvb + [VFL.b], [vb.b], scale=VFL[:, i:i + 1])
        rope(pk, pkb, rt, kr[:, :], kr.b, NH, R1, R2)
        tt("pool", h3(kf[:, :]), h3(kr[:, :]), KD[:, 0:8].to_broadcast([128, NH, 128]), ALU.mult,
           [kr.b, KD.b], [kf.b])
        tt("pool", h3(kb[:, :]), h3(kr[:, :]), KD[:, 8:16].to_broadcast([128, NH, 128]), ALU.mult,
           [kr.b, KD.b], [kb.b])
        if i >= NO:
            m = i - NO
            pst, psb = ps_bf16(0)
            pv3 = pst.rearrange("p (k t) -> p k t", t=128)
            for h in range(NH):
                tr(pv3[:, h, :], kr[:, h * 128:(h + 1) * 128], [kr.b], [psb[0]])
            kts = KTS[i % 2]
            act(kts[:, :, :], pv3[:, 0:NH, :], AF.Copy, [psb[0]], [kts.b])
            store(KTs[m], kts[:, :, :], [kts.b], [DB("KT")], "st_kts%d" % (i % 2))
            store(Vs[m], vb[:, :], [vb.b], [DB("V")], "st_vb%d" % (i % 2))

    def A_s4(i, d_):
        kx = (KF if d_ == 0 else KB_)[i % 2]
        vb = VB[i % 2]
        pt, pb = ps_f32(3)
        for h in range(NH):
            mm(pt[:, h * 128:(h + 1) * 128], kx[:, h * 128:(h + 1) * 128], vb[:, h * 128:(h + 1) * 128], True, True,
               [kx.b, vb.b], [pb[h // 4]])
        if i < NO:
            tt("dve", h3(STMP[:, :]), h3(pt), COEF[:, d_, i, :].to_broadcast([128, NH, 128]), ALU.mult,
               pb + [COEF.b], [STMP.b])
            tt("dve", SNAP[d_][:, :], SNAP[d_][:, :], STMP[:, :], ALU.add, [SNAP[d_].b, STMP.b], [SNAP[d_].b])
        else:
            m = i - NO
            if d_ == 0:
                sfb = SFB[i % 2]
                act(sfb[:, :], SNAP[0][:, :], AF.Copy, [SNAP[0].b], [sfb.b])
                store(SFs[m], sfb[:, :], [sfb.b], [DB("SF")], "st_sfb%d" % (i % 2))
                tt("dve", h3(STMP[:, :]), h3(SNAP[0][:, :]), G128[:, 0:8].to_broadcast([128, NH, 128]), ALU.mult,
                   [SNAP[0].b, G128.b], [STMP.b])
                tt("dve", SNAP[0][:, :], STMP[:, :], pt, ALU.add, [STMP.b] + pb, [SNAP[0].b])
            else:
                kv = KVST[i % 2]
                act(kv[:, :], pt, AF.Copy, pb, [kv.b])
                store(KVB[m], kv[:, :], [kv.b], [DB("KVB%d" % m)], "st_kv%d" % (i % 2))

    A_N(0)
    A_N(1)
    A_T(0)
    A_RT(0)
    A_MM(0, 0)
    A_MM(0, 1)
    A_N(2)
    A_T(1)
    for i in range(NS):
        A_RT(i + 1)
        A_s3(i)
        A_N(i + 3)
        A_T(i + 2)
        A_s4(i, 0)
        A_MM(i + 1, 0)
        A_s4(i, 1)
        A_MM(i + 1, 1)
        tick()
    end_phase()
    stA.close()

    stB = ExitStack()
    HTO = sb("hto", [128, KT, NE * 128], BF16, False, stB)
    for q in range(4):
        dma("sp", HTO[:, q * 4:(q + 1) * 4, :], HT[:, q * 4:(q + 1) * 4, :], [DB("HT")], [], "hto", wp=[HTO.b])
    WB = [sb("wb%d" % i, [128, KT, 512], BF16) for i in range(2)]
    wl = make_wload(3, ("act", "pool", "act", "dve"))
    RT = [sb("rtq%d" % i, [128, 128]) for i in range(2)]
    QR = [sb("qr%d" % i, [128, 512], BF16) for i in range(2)]
    QTS_ = [sb("qts%d" % i, [128, 4, 128], BF16) for i in range(2)]
    GS = [sb("gs%d" % i, [128, 512], BF16) for i in range(2)]
    R1 = sb("r1b", [128, 256])
    R2 = sb("r2b", [128, 256])
    WM = [sb("wmb%d" % i, [128, KT, 128]) for i in range(2)]
    WMB = [sb("wmbb%d" % i, [128, KT, 128], BF16) for i in range(2)]
    BMB = [sb("bmbb%d" % i, [33, 512]) for i in range(2)]
    ROW = sb("rowb", [33, D])
    GB_ = sb("gbb", [128, D])
    modgen = mod_items((2, 3, 4), WM, BMB, ROW, GB_, [(2, 0), (2, 1), (3, 0)], (3, 1), nq=4, WMB=WMB)

    def B_wl(g):
        wb = WB[g % 2]
        col0 = g * 512 if g < 2 else 3072 + (g - 2) * 512
        for q4 in range(4):
            yield (lambda q4=q4: wl(wb[:, q4 * 4:(q4 + 1) * 4, :],
                                    winv[:, q4 * 4:(q4 + 1) * 4, col0:col0 + 512], wb.b, a=4))

    for f in B_wl(0):
        f()
    itn = 0
    for g in range(4):
        wb = WB[g % 2]
        nxt = list(B_wl(g + 1)) if g + 1 < 4 else []
        for m in range(NE):
            if m in (2, 6, 10, 14) and nxt:
                nxt.pop(0)()
            if g < 2:
                rt = RT[m % 2]
                dma("sp", rt[:, :], ropeq[m], [], [rt.b], "rtq%d" % (m % 2))
            pt, pb = ps_f32(1, m % 2)
            for kt in range(KT):
                mm(pt, HTO[:, kt, m * 128:(m + 1) * 128], wb[:, kt, :], kt == 0, kt == KT - 1, [HTO.b, wb.b], pb)
            if g < 2:
                qr = QR[m % 2]
                rope(pt, pb, rt, qr[:, :], qr.b, 4, R1, R2)
                pst, psb = ps_bf16(0)
                pv3 = pst.rearrange("p (k t) -> p k t", t=128)
                o8 = (m % 2) * 8
                for h in range(4):
                    tr(pv3[:, o8 + h, :], qr[:, h * 128:(h + 1) * 128], [qr.b], [psb[m % 2]])
                qts = QTS_[m % 2]
                act(qts[:, :, :], pv3[:, o8:o8 + 4, :], AF.Copy, [psb[m % 2]], [qts.b])
                store(QTs[m][:, g * 4:(g + 1) * 4, :], qts[:, :, :], [qts.b], [DB("QT")], "st_qts%d" % (m % 2))
            else:
                gs = GS[m % 2]
                act(gs[:, :], pt, AF.Silu, pb, [gs.b])
                store(Gs[m][:, (g - 2) * 512:(g - 1) * 512], gs[:, :], [gs.b], [DB("G")], "st_gs%d" % (m % 2))
            if itn % 4 == 1:
                nx = next(modgen, None)
                if nx is not None:
                    nx()
            itn += 1
            tick()
    for nx in modgen:
        nx()
    end_phase()
    WC = [sb("wc%d" % i, [128, 3, KT, 128], BF16) for i in range(2)]
    wl = make_wload(3, ("pool",))
    CSB = [sb("csb%d" % i, [128, 384]) for i in range(2)]
    UU = [sb("uu%d" % i, [128, 384]) for i in range(2)]
    YY = [sb("yy%d" % i, [128, 384]) for i in range(2)]
    CVT = [sb("cvt%d" % i, [128, 384], BF16) for i in range(2)]
    sets = [((2, 0), (2, 1), (3, 0)), ((3, 1), (1, 0), (1, 1))]
    SFBb = [sb("sfbb%d" % i, [128, 1024], BF16) for i in range(2)]
    KVSb = [sb("kvsb%d" % i, [128, 1024]) for i in range(2)]
    STMPb = sb("stmpb", [128, 1024])

    def bwd_steps():
        for m in range(NE - 1, -1, -1):
            def stp(m=m):
                sfb = SFBb[m % 2]
                act(sfb[:, :], SNAP[1][:, :], AF.Copy, [SNAP[1].b], [sfb.b])
                store(SBs[m], sfb[:, :], [sfb.b], [DB("SB")], "st_sfbb%d" % (m % 2))
                if m > 0:
                    kv = KVSb[m % 2]
                    dma("sp", kv[:, :], KVB[m], [DB("KVB%d" % m)], [kv.b], "ld_kvb%d" % (m % 2))
                    tt("dve", h3(STMPb[:, :]), h3(SNAP[1][:, :]), G128[:, 8:16].to_broadcast([128, NH, 128]),
                       ALU.mult, [SNAP[1].b, G128.b], [STMPb.b])
                    tt("dve", SNAP[1][:, :], STMPb[:, :], kv[:, :], ALU.add, [STMPb.b, kv.b], [SNAP[1].b])
            yield stp
    bwdgen = bwd_steps()

    def B2_wl(c):
        wc = WC[c % 2]
        for j3 in range(3):
            cb = 4096 + j3 * 1024 + c * 128
            yield (lambda j3=j3, cb=cb: wl(wc[:, j3, :, :], winv[:, :, cb:cb + 128], wc.b, a=KT))

    for f in B2_wl(0):
        f()
    it = 0
    for c in range(8):
        wc = WC[c % 2]
        nxt = list(B2_wl(c + 1)) if c + 1 < 8 else []
        for tb in range(6):
            if tb in (1, 2, 3) and nxt:
                nxt.pop(0)()
            bk = [ps_f32(a_, b_) for (a_, b_) in sets[it % 2]]
            for j3 in range(3):
                pt, pb = bk[j3]
                for kt in range(KT):
                    mm(pt[:, 0:384], wc[:, j3, kt, :], HTO[:, kt, tb * 384:(tb + 1) * 384], kt == 0, kt == KT - 1,
                       [wc.b, HTO.b], pb)
            (pB, pBb), (pC, pCb), (pH, pHb) = bk
            csb, uu, yy, cvt = CSB[it % 2], UU[it % 2], YY[it % 2], CVT[it % 2]
            act(csb[:, :], pC[:, 0:384], AF.Copy, pCb, [csb.b])
            tt("dve", uu[:, :], csb[:, :], pH[:, 0:384], ALU.mult, [csb.b] + pHb, [uu.b])
            act(yy[:, :], uu[:, :], AF.Copy, [uu.b, CW.b], [yy.b], scale=CW[:, c, 1:2])
            u3 = uu[:, :].rearrange("p (r c) -> p r c", c=64)
            y3 = yy[:, :].rearrange("p (r c) -> p r c", c=64)
            stt(y3[:, :, 1:64], u3[:, :, 0:63], CW[:, c, 0:1], y3[:, :, 1:64], ALU.mult, ALU.add,
                [uu.b, yy.b, CW.b], [yy.b])
            stt(y3[:, :, 0:63], u3[:, :, 1:64], CW[:, c, 2:3], y3[:, :, 0:63], ALU.mult, ALU.add,
                [uu.b, yy.b, CW.b], [yy.b])
            tt("dve", cvt[:, :], pB[:, 0:384], yy[:, :], ALU.mult, pBb + [yy.b], [cvt.b])
            store(CTs[c][:, tb * 384:(tb + 1) * 384], cvt[:, :], [cvt.b], [DB("CT")], "st_cvt%d" % (it % 2))
            it += 1
            nx = next(bwdgen, None)
            if nx is not None:
                nx()
            tick()
    for nx in bwdgen:
        nx()
    end_phase()
    stB.close()

    stC = ExitStack()
    WO = sb("wo", [128, KT, D], BF16, False, stC)
    WOb = [Buf("wo%d" % i) for i in range(KT)]
    wov = w_out.rearrange("(kt p) c -> p kt c", p=128)
    wlC = make_wload(2, ("act",), stC)
    MT = sb("mt2", [128, 1024])
    QDT = sb("qdt2", [128, 2, NH, 128])
    dma("sp", MT[:, :], TMT[:, :], [DB("TMT")], [MT.b], "ld_mt")
    dma("sp", QDT[:, :, :, :].rearrange("p a h i -> p (a h i)"), TQD[:, :], [DB("TQD")], [QDT.b], "ld_qd")
    L = {}
    for nm, shp, dt in [("qt", [128, NH, 128], BF16), ("kt", [128, NH, 128], BF16), ("v", [128, 1024], BF16),
                        ("g", [128, 1024], BF16), ("sf", [128, 1024], BF16), ("sb", [128, 1024], BF16),
                        ("qf", [128, NH, 128], BF16), ("qb", [128, NH, 128], BF16), ("pt", [128, 1024], BF16),
                        ("ret", [128, 1024], BF16), ("rett", [128, NH, 128], BF16)]:
        L[nm] = [sb("c1%s%d" % (nm, i), shp, dt) for i in range(2)]
    OSQ = sb("osq", [128, 1024])
    R1c = sb("r1c", [128, 1024])
    SSO = [sb("sso%d" % i, [128, NH]) for i in range(2)]

    def C1_a(m):
        if m >= NE:
            return
        j = m % 2
        qt, kt_, vv, gg, sf, sbb = L["qt"][j], L["kt"][j], L["v"][j], L["g"][j], L["sf"][j], L["sb"][j]
        dma("sp", qt[:, :, :], QTs[m], [DB("QT")], [qt.b], "l_qt%d" % j)
        dma("sp", kt_[:, :, :], KTs[m], [DB("KT")], [kt_.b], "l_kt%d" % j)
        dma("sp", vv[:, :], Vs[m], [DB("V")], [vv.b], "l_v%d" % j)
        dma("sp", gg[:, :], Gs[m], [DB("G")], [gg.b], "l_g%d" % j)
        dma("sp", sf[:, :], SFs[m], [DB("SF")], [sf.b], "l_sf%d" % j)
        dma("sp", sbb[:, :], SBs[m], [DB("SB")], [sbb.b], "l_sb%d" % j)
        qf, qb = L["qf"][j], L["qb"][j]
        tt("pool", qf[:, :, :], qt[:, :, :], QDT[:, 0, :, :], ALU.mult, [qt.b, QDT.b], [qf.b])
        tt("pool", qb[:, :, :], qt[:, :, :], QDT[:, 1, :, :], ALU.mult, [qt.b, QDT.b], [qb.b])
        pA, pAb = ps_f32(0)
        for h in range(NH):
            mm(pA[:, h * 128:(h + 1) * 128], kt_[:, h, :], qt[:, h, :], True, True, [kt_.b, qt.b], [pAb[h // 4]])
        ptt = L["pt"][j]
        tt("dve", ptt[:, :], pA, MT[:, :], ALU.mult, pAb + [MT.b], [ptt.b])

    def C1_b(m):
        j = m % 2
        vv, gg, sf, sbb = L["v"][j], L["g"][j], L["sf"][j], L["sb"][j]
        qf, qb, ptt = L["qf"][j], L["qb"][j], L["pt"][j]
        pB, pBb = ps_f32(1 + j)
        for h in range(NH):
            hs = slice(h * 128, (h + 1) * 128)
            mm(pB[:, hs], ptt[:, hs], vv[:, hs], True, False, [ptt.b, vv.b], [pBb[h // 4]])
            mm(pB[:, hs], qf[:, h, :], sf[:, hs], False, False, [qf.b, sf.b], [pBb[h // 4]])
            mm(pB[:, hs], qb[:, h, :], sbb[:, hs], False, True, [qb.b, sbb.b], [pBb[h // 4]])
        act(OSQ[:, :], pB, AF.Square, pBb, [OSQ.b])
        sso = SSO[j]
        P.op("dve", lambda E, sso=sso: E.tensor_reduce(out=sso[:, :], in_=h3(OSQ[:, :]), axis=AX.X, op=ALU.add),
             r=[OSQ.b], w=[sso.b])
        rsq(sso[:, :], sso[:, :], sso.b, 1.0 / 128)

    def C1_b2(m):
        if m < 0:
            return
        j = m % 2
        gg = L["g"][j]
        sso = SSO[j]
        pB, pBb = ps_f32(1 + j)
        tt("dve", h3(R1c[:, :]), h3(pB), sso[:, :].to_broadcast([128, NH, 128]), ALU.mult, pBb + [sso.b], [R1c.b])
        ret = L["ret"][j]
        tt("dve", ret[:, :], R1c[:, :], gg[:, :], ALU.mult, [R1c.b, gg.b], [ret.b])

    def C1_c(m):
        if m < 0:
            return
        j = m % 2
        ret = L["ret"][j]
        pst, psb = ps_bf16(3)
        pv3 = pst.rearrange("p (k t) -> p k t", t=128)
        for h in range(NH):
            tr(pv3[:, h, :], ret[:, h * 128:(h + 1) * 128], [ret.b], [psb[0]])
        rett = L["rett"][j]
        act(rett[:, :, :], pv3[:, 0:NH, :], AF.Copy, [psb[0]], [rett.b])
        store(RTs[m], rett[:, :, :], [rett.b], [DB("RT")], "st_rett%d" % j)

    WM = [sb("wmc%d" % i, [128, KT, 128]) for i in range(2)]
    BMB = [sb("bmbc%d" % i, [33, 512]) for i in range(2)]
    ROW = sb("rowc", [33, D])
    GB_ = sb("gbc", [128, D])
    WMB = [sb("wmcb%d" % i, [128, KT, 128], BF16) for i in range(2)]
    modgen = mod_items((5,), WM, BMB, ROW, GB_, [(3, 1)], (3, 1), nq=4, WMB=WMB)
    C1_a(0)
    for m in range(NE + 2):
        C1_b2(m - 1 if m - 1 < NE else -1)
        C1_c(m - 2)
        C1_a(m + 1)
        if m < NE:
            C1_b(m)
        if m < KT:
            wlC(WO[:, m, :], wov[:, m, :], WOb[m])
        nx = next(modgen, None)
        if nx is not None:
            nx()
        tick()
    for nx in modgen:
        nx()
    end_phase()

    G1B = sb("g1b", [128, D])
    dma("sp", G1B[:, :], TG1[:, :], [DB("TG2")], [G1B.b], "ld_g1")
    MXT = [sb("mxt%d" % i, [128, KT, 128], BF16) for i in range(2)]
    XMc = [sb("xmc%d" % i, [128, D]) for i in range(3)]
    TMPX = sb("tmpx", [128, D])
    SQJ = sb("sqj2", [128, D], BF16)
    SSQ = [sb("ssq2%d" % i, [128, 1]) for i in range(2)]
    RSTD = [sb("rstd2%d" % i, [128, 1]) for i in range(2)]
    XN = [sb("xn2%d" % i, [128, D], BF16) for i in range(2)]
    H2c = [sb("h2c%d" % i, [128, KT, 128], BF16) for i in range(2)]
    ctv = CTs.rearrange("c p t -> p c t")

    def C2_l(m):
        if m >= NE:
            return
        j = m % 2
        mxt, xm = MXT[j], XMc[m % 3]
        dma("sp", mxt[:, 0:8, :], RTs[m], [DB("RT")], [], "l_mxa%d" % j, wp=[mxt.b])
        dma("sp", mxt[:, 8:16, :], ctv[:, :, m * 128:(m + 1) * 128], [DB("CT")], [], "l_mxa%d" % j, wp=[mxt.b])
        dma("sp", xm[:, :], xe[m], [], [xm.b], "l_xm%d" % (m % 3))

    def C2_a(m):
        if m >= NE:
            return
        j = m % 2
        mxt, xm = MXT[j], XMc[m % 3]
        pC, pCb = ps_f32(2)
        pD, pDb = ps_f32(3)
        for cg in range(4):
            tgt = (pC if cg < 2 else pD)[:, (cg % 2) * 512:(cg % 2 + 1) * 512]
            tb_ = (pCb if cg < 2 else pDb)[cg % 2]
            for kt in range(KT):
                mm(tgt, mxt[:, kt, :], WO[:, kt, cg * 512:(cg + 1) * 512], kt == 0, kt == KT - 1,
                   [mxt.b, WOb[kt]], [tb_])
        tt("dve", TMPX[:, 0:1024], pC, G1B[:, 0:1024], ALU.mult, pCb + [G1B.b], [], wp=[TMPX.b])
        tt("dve", TMPX[:, 1024:2048], pD, G1B[:, 1024:2048], ALU.mult, pDb + [G1B.b], [], wp=[TMPX.b])
        tt("pool", xm[:, :], xm[:, :], TMPX[:, :], ALU.add, [xm.b, TMPX.b], [xm.b])
        store(XM[m], xm[:, :], [xm.b], [DB("XM")], "st_xm%d" % (m % 3))
        rms_n(xm, SSQ[j], RSTD[j], XN[j], SQJ)

    def C2_b(m):
        if m < 0:
            return
        j = m % 2
        h2 = H2c[j]
        tr16(XN[j], j, h2, 3, 2)
        store(H2T[:, :, m * 128:(m + 1) * 128], h2[:, :, :], [h2.b], [DB("H2T")], "st_h2%d" % j)

    C2_l(0)
    C2_l(1)
    for m in range(NE):
        C2_a(m)
        C2_b(m - 1)
        tick()
        C2_l(m + 2)
    C2_b(NE - 1)
    end_phase()
    stC.close()

    H2O = sb("h2o", [128, KT, NE * 128], BF16)
    for q in range(4):
        dma("sp", H2O[:, q * 4:(q + 1) * 4, :], H2T[:, q * 4:(q + 1) * 4, :], [DB("H2T")], [], "h2o", wp=[H2O.b])
    WU = [sb("wu%d" % i, [128, 2, KT, 128], BF16) for i in range(2)]
    wl = make_wload(3, ("pool",))
    wuv = w_up.rearrange("(kt p) c -> p kt c", p=128)
    ASB = [sb("asb%d" % i, [128, 2176]) for i in range(2)]
    YD = [sb("yd%d" % i, [128, 2048]) for i in range(2)]
    SD = [sb("sd%d" % i, [128, 2048], BF16) for i in range(2)]
    UD = [sb("ud%d" % i, [128, 2048], BF16) for i in range(2)]
    utv = UT.rearrange("n p c t -> p n c t")
    ablk = [(64, 512), (576, 512), (1088, 512), (1600, 512), (2112, 128)]
    pa_i = 0
    pb_i = 0

    def D_wl(c):
        if c >= NFT:
            return
        wu = WU[c % 2]
        wl(wu[:, 0, :, :], wuv[:, :, c * 128:(c + 1) * 128], wu.b, a=KT)
        wl(wu[:, 1, :, :], wuv[:, :, DFF + c * 128:DFF + (c + 1) * 128], wu.b, a=KT)

    D_wl(0)
    for c in range(NFT):
        wu = WU[c % 2]
        D_wl(c + 1)
        asb, yd, sd, ud = ASB[c % 2], YD[c % 2], SD[c % 2], UD[c % 2]
        for bi, (t0, n) in enumerate(ablk):
            pt, pb = ps_f32(pa_i % 2, (pa_i // 2) % 2)
            pa_i += 1
            for kt in range(KT):
                mm(pt[:, 0:n], wu[:, 0, kt, :], H2O[:, kt, t0:t0 + n], kt == 0, kt == KT - 1, [wu.b, H2O.b], pb)
            a0 = t0 - 64
            if bi == 0:
                act(asb[:, 0:64], pt[:, 0:64], AF.Copy, pb + [AFL.b], [], wp=[asb.b], scale=AFL[:, 0:1])
                act(asb[:, 64:512], pt[:, 64:512], AF.Copy, pb, [], wp=[asb.b])
            elif bi == 4:
                act(asb[:, a0:a0 + 64], pt[:, 0:64], AF.Copy, pb, [], wp=[asb.b])
                act(asb[:, a0 + 64:a0 + 128], pt[:, 64:128], AF.Copy, pb + [AFL.b], [], wp=[asb.b],
                    scale=AFL[:, 1:2])
            else:
                act(asb[:, a0:a0 + n], pt[:, 0:n], AF.Copy, pb, [], wp=[asb.b])
        act(yd[:, :], asb[:, 64:2112], AF.Identity, [asb.b, CW2.b], [yd.b], scale=CW2[:, c, 1:2], bias=CW2[:, c, 3:4])
        stt(yd[:, :], asb[:, 0:2048], CW2[:, c, 0:1], yd[:, :], ALU.mult, ALU.add, [asb.b, yd.b, CW2.b], [yd.b])
        stt(yd[:, :], asb[:, 128:2176], CW2[:, c, 2:3], yd[:, :], ALU.mult, ALU.add, [asb.b, yd.b, CW2.b], [yd.b])
        act(sd[:, :], yd[:, :], AF.Silu, [yd.b], [sd.b])
        for bi in range(4):
            pt, pb = ps_f32(2 + pb_i % 2, (pb_i // 2) % 2)
            pb_i += 1
            t0 = 128 + bi * 512
            for kt in range(KT):
                mm(pt, wu[:, 1, kt, :], H2O[:, kt, t0:t0 + 512], kt == 0, kt == KT - 1, [wu.b, H2O.b], pb)
            tt("dve", ud[:, bi * 512:(bi + 1) * 512], sd[:, bi * 512:(bi + 1) * 512], pt, ALU.mult, [sd.b] + pb, [],
               wp=[ud.b])
        store(utv[:, :, c, :], ud[:, :].rearrange("p (n t) -> p n t", t=128), [ud.b], [DB("UT")], "st_ud%d" % (c % 2))
        tick()
    end_phase()

    WD = [sb("wd%d" % i, [128, NFT, 512], BF16) for i in range(2)]
    wl = make_wload(3, ("act", "dve", "act", "pool"))
    wdv = w_down.rearrange("(kt p) c -> p kt c", p=128)
    G2B = sb("g2b", [128, D])
    FGB = sb("fgb_s", [128, D])
    dma("sp", G2B[:, :], TG2[:, :], [DB("TG5")], [G2B.b], "ld_g2")
    dma("sp", FGB[:, :], fgb[:, :], [], [FGB.b], "ld_fg")
    UTc = [sb("utc%d" % i, [128, NFT, 128], BF16) for i in range(2)]
    XMq = [sb("xmq%d" % i, [128, 512]) for i in range(2)]
    ZC = [sb("zc%d" % i, [128, 512]) for i in range(2)]
    ZJ = sb("zj", [128, 512], BF16)
    SSF = sb("ssf", [128, 16, 4])

    def E_wl(cg):
        wd = WD[cg % 2]
        for q in range(NFT // 4):
            yield (lambda q=q: wl(wd[:, q * 4:(q + 1) * 4, :], wdv[:, q * 4:(q + 1) * 4, cg * 512:(cg + 1) * 512],
                                  wd.b, a=4))

    def E_l(it):
        if it >= 64:
            return
        cg, n = it // 16, it % 16
        j = it % 2
        utc, xmq = UTc[j], XMq[j]
        dma("sp", utc[:, :, :], UT[n], [DB("UT")], [utc.b], "l_utc%d" % j)
        dma("sp", xmq[:, :], XM[n + 1][:, cg * 512:(cg + 1) * 512], [DB("XM")], [xmq.b], "l_xmq%d" % j)

    for f in E_wl(0):
        f()
    E_l(0)
    it = 0
    for cg in range(4):
        wd = WD[cg % 2]
        nxt = list(E_wl(cg + 1)) if cg + 1 < 4 else []
        for n in range(16):
            E_l(it + 1)
            if nxt and n >= 2:
                nxt.pop(0)()
            j = it % 2
            utc, xmq, zc = UTc[j], XMq[j], ZC[j]
            pt, pb = ps_f32((it // 2) % 4, it % 2)
            for kt in range(NFT):
                mm(pt, utc[:, kt, :], wd[:, kt, :], kt == 0, kt == NFT - 1, [utc.b, wd.b], pb)
            tt("dve", zc[:, :], pt, G2B[:, cg * 512:(cg + 1) * 512], ALU.mult, pb + [G2B.b], [zc.b])
            tt("dve", zc[:, :], zc[:, :], xmq[:, :], ALU.add, [zc.b, xmq.b], [zc.b])
            act(ZJ[:, :], zc[:, :], AF.Square, [zc.b], [], wp=[ZJ.b, SSF.b], accum_out=SSF[:, n, cg:cg + 1])
            store(XO[n][:, cg * 512:(cg + 1) * 512], zc[:, :], [zc.b], [DB("XO%d" % n)], "st_zc%d" % j)
            it += 1
            tick()
        while nxt:
            nxt.pop(0)()
    tick()
    tick()
    RF = sb("rf", [128, 16])
    P.op("dve", lambda E: E.tensor_reduce(out=RF[:, :], in_=SSF[:, :, :], axis=AX.X, op=ALU.add), r=[SSF.b], w=[RF.b])
    rsq(RF[:, :], RF[:, :], RF.b, 1.0 / D)
    ZF = [sb("zf%d" % i, [128, D]) for i in range(2)]
    OF = [sb("of%d" % i, [128, 1024]) for i in range(2)]
    dma("sp", ZF[0][:, :], XO[0], [DB("XO0")], [ZF[0].b], "l_zf0")
    k = 0
    for n in range(16):
        zf = ZF[n % 2]
        if n + 1 < 16:
            dma("sp", ZF[(n + 1) % 2][:, :], XO[n + 1], [DB("XO%d" % (n + 1))], [ZF[(n + 1) % 2].b],
                "l_zf%d" % ((n + 1) % 2))
        for hf in range(2):
            of = OF[k % 2]
            stt(of[:, :], zf[:, hf * 1024:(hf + 1) * 1024], RF[:, n:n + 1], FGB[:, hf * 1024:(hf + 1) * 1024],
                ALU.mult, ALU.mult, [zf.b, RF.b, FGB.b], [of.b])
            dma(STQ, out[n][:, hf * 1024:(hf + 1) * 1024], of[:, :], [of.b], [], "st_of%d" % (k % 2),
                wp=[DB("out")])
            k += 1
    P.barrier()
    P.emit()
    pstack[0].close()
    gstack.close()
    return nc


DEBUG_OUT = ()
_NC_CACHE = {}


def _host_tables(s):
    f32 = np.float32
    G0 = 16 * s - 1
    G1 = 16 * s + 16
    ext = [G0 + m for m in range(NE)]
    extset = set(g for g in ext if 0 <= g < 64)
    others = [g for g in range(64) if g not in extset]
    efb = np.zeros((2, NO), f32)
    mfb = np.zeros((2, NO), f32)
    for c in range(2):
        efb[0, c] = 128.0 * G0 + 128.0 * (1 - c)
        efb[1, c] = 128.0 * (63 - G1) + 128.0 * c
        mfb[:, c] = 1.0
    olist = []
    for k in range(NO - 2):
        if k < len(others):
            g = others[k]
            olist.append(g)
            if g < G0:
                efb[0, 2 + k] = 128.0 * (G0 - 1 - g)
                mfb[0, 2 + k] = 1.0
            if g > G1:
                efb[1, 2 + k] = 128.0 * (g - G1 - 1)
                mfb[1, 2 + k] = 1.0
        else:
            olist.append(others[0])
    freqs = (np.float32(10000.0) ** (-np.arange(0, 64, 2, dtype=f32) / np.float32(64))).astype(f32)

    def table(g, scale):
        g = min(max(g, 0), 63)
        pos = g * 128 + np.arange(128)
        row = (pos // 64).astype(f32)
        col = (pos % 64).astype(f32)
        ar = row[:, None] * freqs[None, :]
        ac = col[:, None] * freqs[None, :]
        t = np.concatenate([np.cos(ar), np.cos(ac), np.sin(ar), np.sin(ac)], axis=1).astype(f32)
        return (t * f32(scale)).astype(f32)

    ks = f32(128.0 ** -0.5)
    ctx_t = np.concatenate([np.full((128, 64), ks, f32), np.zeros((128, 64), f32)], axis=1)
    ropek = np.stack([ctx_t, ctx_t] + [table(g, ks) for g in olist] + [table(g, ks) for g in ext]).astype(f32)
    ropeq = np.stack([table(g, 1.0) for g in ext]).astype(f32)
    vfl = np.ones((NO + NE,), f32)
    for m, g in enumerate(ext):
        if not (0 <= g < 64):
            vfl[NO + m] = 0.0
    afl = np.array([0.0 if s == 0 else 1.0, 0.0 if s == 3 else 1.0], f32)
    return ext, olist, efb, mfb, ropek, ropeq, vfl, afl


def _fm(v, nt):
    return np.ascontiguousarray(np.asarray(v, np.float32).reshape(nt, 128).T)


def kernel(x, c, ctx, c_ctx, w_mod, b_mod, norm1_g, w_in, ret_decay_fwd, ret_decay_bwd,
           conv_w, w_out, norm2_g, w_up, ffn_conv_w, ffn_conv_b, w_down, final_g):
    f32 = np.float32
    x = np.asarray(x, f32)
    ctx = np.asarray(ctx, f32)
    c = np.asarray(c, f32)
    c_ctx = np.asarray(c_ctx, f32)
    if "nc" not in _NC_CACHE:
        _NC_CACHE["nc"] = build_nc()
    nc = _NC_CACHE["nc"]
    rep = lambda v: np.ascontiguousarray(np.broadcast_to(np.asarray(v, f32)[None], (128,) + np.asarray(v).shape))
    jj = np.arange(128, dtype=f32)
    rel = jj[None, :] - jj[:, None]
    ctab = np.stack([np.maximum(rel, 0), np.maximum(-rel, 0), (rel >= 0).astype(f32), (rel < 0).astype(f32),
                     np.broadcast_to(jj[None, :] + 1, (128, 128)), np.broadcast_to(128 - jj[None, :], (128, 128))],
                    axis=1).astype(f32)
    pcol = np.stack([127 - jj, jj], axis=1).astype(f32)
    shared = {
        "w_mod": np.ascontiguousarray(np.asarray(w_mod, f32)[0]),
        "w_in": np.ascontiguousarray(np.asarray(w_in, f32)[0]),
        "w_out": np.ascontiguousarray(np.asarray(w_out, f32)[0]),
        "w_up": np.ascontiguousarray(np.asarray(w_up, f32)[0]),
        "w_down": np.ascontiguousarray(np.asarray(w_down, f32)[0]),
        "gn": np.ascontiguousarray(np.stack([_fm(norm1_g[0], KT), _fm(norm2_g[0], KT)], axis=1)),
        "fgb": rep(final_g),
        "dlog": rep(np.concatenate([np.asarray(ret_decay_fwd, f32)[0], np.asarray(ret_decay_bwd, f32)[0]])),
        "cw": np.ascontiguousarray(np.stack([_fm(np.asarray(conv_w, f32)[0, t], 8) for t in range(3)], axis=2)),
        "cw2": np.ascontiguousarray(np.stack([_fm(np.asarray(ffn_conv_w, f32)[0, t], NFT) for t in range(3)]
                                             + [_fm(np.asarray(ffn_conv_b, f32)[0], NFT)], axis=2)),
        "ctab": ctab, "pcol": pcol, "ident": np.eye(128, dtype=f32),
    }
    bm = np.zeros((33, 6 * D), f32)
    bm[0] = np.asarray(b_mod, f32)[0]
    bm[32] = np.asarray(b_mod, f32)[0]
    shared["bm"] = bm
    in_maps = []
    for core in range(8):
        b, s = core // 4, core % 4
        ext, olist, efb, mfb, ropek, ropeq, vfl, afl = _host_tables(s)
        xc = x[b].reshape(64, 128, D)
        xo = np.concatenate([ctx[b].reshape(2, 128, D), xc[olist]], axis=0)
        xe = np.zeros((NE, 128, D), f32)
        for m, g in enumerate(ext):
            if 0 <= g < 64:
                xe[m] = xc[g]
        cT = np.zeros((128, KT, 33), f32)
        cT[:, :, 0] = _fm(c[b], KT)
        cT[:, :, 32] = _fm(c_ctx, KT)
        d = dict(shared)
        d.update({"xo": np.ascontiguousarray(xo), "xe": xe, "ropek": ropek, "ropeq": ropeq,
                  "efb": rep(efb), "mfb": rep(mfb), "vfl": rep(vfl), "afl": rep(afl), "cT": cT})
        in_maps.append(d)
    res = run_bass_kernel_spmd(nc, in_maps, core_ids=list(range(8)))
    _NC_CACHE["res"] = res
    outp = np.zeros((2, 8192, D), f32)
    for core in range(8):
        b, s = core // 4, core % 4
        outp[b, s * 2048:(s + 1) * 2048] = np.asarray(res.results[core]["out"]).reshape(2048, D)
    return outp
```

```python
import numpy as np
import concourse.bass as bass
import concourse.mybir as mybir
from concourse.bass_utils import run_bass_kernel_spmd
from contextlib import ExitStack

F32 = mybir.dt.float32
BF16 = mybir.dt.bfloat16
AF = mybir.ActivationFunctionType
ALU = mybir.AluOpType
AX = mybir.AxisListType

D = 2048
KT = 16
NH = 8
DFF = 5632
NFT = 44
NO = 49
NE = 18
EPS = 1e-6
DEBUG = False
DVE_EVAC = False
STRICT = False


class Buf:
    __slots__ = ("name", "writers", "readers", "prev")

    def __init__(self, name):
        self.name = name
        self.writers = []
        self.readers = []
        self.prev = []


class Op:
    __slots__ = ("eng", "fn", "deps", "signal", "sem", "val", "is_dma")


def _prune(lst, o):
    lst[:] = [p for p in lst if p.sem != o.sem]
    lst.append(o)


class Prog:
    def __init__(self, nc):
        self.nc = nc
        self.ops = []
        self.engs = dict(pe=nc.tensor, act=nc.scalar, dve=nc.vector, pool=nc.gpsimd, sp=nc.sync)
        self.sems = []
        self.esem = {}
        for e in ("pe", "act", "dve", "pool"):
            self.esem[e] = self._newsem("p_" + e)
        self.dsem = {}
        self.dpool = []
        self.bar = []
        self.last = {}

    def _newsem(self, name):
        self.sems.append(self.nc.alloc_semaphore(name=name))
        return len(self.sems) - 1

    def op(self, eng, fn, r=(), w=(), wp=(), dma=None):
        o = Op()
        o.eng = eng
        o.fn = fn
        o.is_dma = dma is not None
        o.signal = o.is_dma
        o.val = None
        if o.is_dma:
            if dma not in self.dsem:
                k = len(self.dsem)
                if k >= len(self.dpool):
                    self.dpool.append(self._newsem("d%d" % k))
                self.dsem[dma] = self.dpool[k]
            o.sem = self.dsem[dma]
        else:
            o.sem = self.esem[eng]
        deps = []
        for p in self.bar:
            deps.append((p, "raw"))
        for b in r:
            for p in b.writers:
                deps.append((p, "raw"))
        for b, partial in [(x, False) for x in w] + [(x, True) for x in wp]:
            if b.readers:
                b.prev = b.readers + b.writers
                b.readers = []
                b.writers = []
            for p in b.prev:
                deps.append((p, "war"))
            if not partial:
                for p in b.writers:
                    deps.append((p, "waw"))
        for b in r:
            _prune(b.readers, o)
        for b in list(w) + list(wp):
            _prune(b.writers, o)
        o.deps = []
        for p, kind in deps:
            if p is o:
                continue
            if (not p.is_dma) and p.eng == eng and (eng == "pe" or (kind == "war" and not STRICT)):
                continue
            p.signal = True
            o.deps.append(p)
        self.ops.append(o)
        self.last[(eng, o.sem)] = o
        return o

    def barrier(self):
        self.bar = list(self.last.values())

    def new_phase(self):
        self.barrier()
        self.dsem = {}

    def emit(self):
        cnt = [0] * len(self.sems)
        known = {}
        for o in self.ops:
            E = self.engs[o.eng]
            need = {}
            for p in o.deps:
                if need.get(p.sem, 0) < p.val:
                    need[p.sem] = p.val
            for s, v in need.items():
                if known.get((o.eng, s), 0) < v:
                    E.wait_ge(self.sems[s], v)
                    known[(o.eng, s)] = v
            ins = o.fn(E)
            if o.signal:
                inc = 16 if o.is_dma else 1
                cnt[o.sem] += inc
                o.val = cnt[o.sem]
                ins.then_inc(self.sems[o.sem], inc)
            else:
                o.val = cnt[o.sem] + 1
        for s in self.dpool:
            if cnt[s] > 0:
                self.nc.sync.wait_ge(self.sems[s], cnt[s])
        assert max(cnt) < 60000, max(cnt)


def build_nc():
    nc = bass.Bass("TRN2", target_bir_lowering=False)
    P = Prog(nc)

    def din(name, shape, dt=F32):
        return nc.dram_tensor(name, list(shape), dt, kind="ExternalInput").ap()

    def dscr(name, shape, dt):
        kind = "ExternalOutput" if (DEBUG and name in DEBUG_OUT) else "Internal"
        return nc.dram_tensor(name, list(shape), dt, kind=kind).ap()

    xo = din("xo", [NO, 128, D])
    xe = din("xe", [NE, 128, D])
    ropek = din("ropek", [NO + NE, 128, 128])
    ropeq = din("ropeq", [NE, 128, 128])
    efb = din("efb", [128, 2, NO])
    mfb = din("mfb", [128, 2, NO])
    vfl = din("vfl", [128, NO + NE])
    afl = din("afl", [128, 2])
    cT = din("cT", [128, KT, 33])
    w_mod = din("w_mod", [D, 6 * D])
    bm = din("bm", [33, 6 * D])
    gn = din("gn", [128, 2, KT])
    fgb = din("fgb", [128, D])
    w_in = din("w_in", [D, 7168])
    w_out = din("w_out", [D, D])
    w_up = din("w_up", [D, 2 * DFF])
    w_down = din("w_down", [DFF, D])
    dlog = din("dlog", [128, 16])
    cw = din("cw", [128, 8, 3])
    cw2 = din("cw2", [128, NFT, 4])
    ctab = din("ctab", [128, 6, 128])
    pcol = din("pcol", [128, 2])
    ident = din("ident", [128, 128])
    out = nc.dram_tensor("out", [16, 128, D], F32, kind="ExternalOutput").ap()

    HT = dscr("HT", [128, KT, NE * 128], BF16)
    KTs = dscr("KTs", [NE, 128, NH, 128], BF16)
    Vs = dscr("Vs", [NE, 128, 1024], BF16)
    KVB = dscr("KVB", [NE, 128, 1024], F32)
    QTs = dscr("QTs", [NE, 128, NH, 128], BF16)
    Gs = dscr("Gs", [NE, 128, 1024], BF16)
    CTs = dscr("CTs", [8, 128, NE * 128], BF16)
    SFs = dscr("SFs", [NE, 128, 1024], BF16)
    SBs = dscr("SBs", [NE, 128, 1024], BF16)
    XM = dscr("XM", [NE, 128, D], F32)
    H2T = dscr("H2T", [128, KT, NE * 128], BF16)
    UT = dscr("UT", [16, 128, NFT, 128], BF16)
    XO = dscr("XO", [16, 128, D], F32)
    dbuf = {}

    def DB(name):
        if name not in dbuf:
            dbuf[name] = Buf(name)
        return dbuf[name]

    TMT = dscr("TMT", [128, 1024], F32)
    TQD = dscr("TQD", [128, 2048], F32)
    TG1 = dscr("TG1", [128, D], F32)
    TG2 = dscr("TG2", [128, D], F32)
    RTs = dscr("RTs", [NE, 128, NH, 128], BF16)

    gstack = ExitStack()
    pstack = [ExitStack()]

    class Tile:
        def __init__(self, name, shape, dt, st):
            self.t = st.enter_context(nc.sbuf_tensor(name, list(shape), dt))
            self.b = Buf(name)

        def __getitem__(self, idx):
            return self.t[idx]

    def sb(name, shape, dt=F32, persist=False, st=None):
        if st is None:
            st = gstack if persist else pstack[0]
        return Tile(name, shape, dt, st)

    def end_phase():
        tick()
        tick()
        P.new_phase()
        pstack[0].close()
        pstack[0] = ExitStack()

    PS = []
    for i in range(4):
        t = gstack.enter_context(nc.psum_tensor("ps%d" % i, [128, 1024], F32))
        PS.append((t, [Buf("ps%da" % i), Buf("ps%db" % i)]))

    def ps_f32(i, half=None):
        t, bs = PS[i]
        if half is None:
            return t[:, :], list(bs)
        return t[:, half * 512:(half + 1) * 512], [bs[half]]

    def ps_bf16(i):
        t, bs = PS[i]
        return t[:, :].bitcast(BF16), list(bs)

    def dma(q, o, i, r, w, key, wp=()):
        return P.op(q, lambda E: E.dma_start(out=o, in_=i), r=r, w=w, wp=wp, dma=key)

    pend = [[], []]
    STQ = "sp"

    def store(o, i, r, wbufs, key):
        pend[1].append(lambda: dma(STQ, o, i, r, [], key, wp=wbufs))

    def tick():
        for f in pend[0]:
            f()
        pend[0] = pend[1]
        pend[1] = []

    def act(o, i, func, r, w, wp=(), **kw):
        return P.op("act", lambda E: E.activation(out=o, in_=i, func=func, **kw), r=r, w=w, wp=wp)

    def tt(eng, o, a, b, op, r, w, wp=()):
        return P.op(eng, lambda E: E.tensor_tensor(out=o, in0=a, in1=b, op=op), r=r, w=w, wp=wp)

    def ts(eng, o, a, s1, s2, op0, op1, r, w, wp=()):
        if op1 is None:
            return P.op(eng, lambda E: E.tensor_scalar(out=o, in0=a, scalar1=s1, scalar2=None, op0=op0),
                        r=r, w=w, wp=wp)
        return P.op(eng, lambda E: E.tensor_scalar(out=o, in0=a, scalar1=s1, scalar2=s2, op0=op0, op1=op1),
                    r=r, w=w, wp=wp)

    def stt(o, a, s, b, op0, op1, r, w, wp=()):
        return P.op("dve", lambda E: E.scalar_tensor_tensor(out=o, in0=a, scalar=s, in1=b, op0=op0, op1=op1),
                    r=r, w=w, wp=wp)

    def cp(eng, o, i, r, w, wp=()):
        if eng == "act":
            return act(o, i, AF.Copy, r, w, wp)
        return P.op(eng, lambda E: E.tensor_copy(out=o, in_=i), r=r, w=w, wp=wp)

    def mm(o, l, rh, st, sp_, r, wp):
        return P.op("pe", lambda E: E.matmul(o, lhsT=l, rhs=rh, start=st, stop=sp_), r=r, wp=wp)

    def tr(o, i, r, wp):
        return P.op("pe", lambda E: E.transpose(out=o, in_=i, identity=IDB[:, :]), r=list(r) + [IDB.b], wp=wp)

    def rsq(o, oi, ob, scale):
        ts("dve", o, oi, scale, EPS, ALU.mult, ALU.add, [ob], [ob])
        act(o, o, AF.Sqrt, [ob], [ob])
        P.op("dve", lambda E: E.reciprocal(out=o, in_=o), r=[ob], w=[ob])

    def h3(ap):
        return ap.rearrange("p (h e) -> p h e", h=NH)

    IDB = sb("idb", [128, 128], BF16, True)
    LG = sb("lg", [128, 16], F32, True)
    KD = sb("kd", [128, 16], F32, True)
    G128 = sb("g128", [128, 16], F32, True)
    COEF = sb("coef", [128, 2, NO, NH], F32, True)
    MODF = sb("modf", [128, 6, KT], F32, True)
    GN = sb("gn_s", [128, 2, KT], F32, True)
    VFL = sb("vfl_s", [128, NO + NE], F32, True)
    AFL = sb("afl_s", [128, 2], F32, True)
    CW = sb("cw_s", [128, 8, 3], F32, True)
    CW2 = sb("cw2_s", [128, NFT, 4], F32, True)
    SNAP = [sb("snapf", [128, 1024], F32, True), sb("snapb", [128, 1024], F32, True)]
    ONES = sb("ones", [33, 128], F32, True)
    ST_ = sb("sT_s", [128, KT, 33], F32, True)
    STB = sb("sT_b", [128, KT, 33], BF16, True)
    wst_n = [0]

    def make_wload(nst, engines, st=None):
        WST = [sb("wst%d_%d" % (wst_n[0], i), [128, 2048], F32, False, st) for i in range(nst)]
        wst_n[0] += 1
        cnt = [0]
        tag = wst_n[0]

        def wload(dst, src, dstbuf, a=None):
            k = cnt[0]
            cnt[0] += 1
            stg = WST[k % nst]
            sv = stg[:, :] if a is None else stg[:, :].rearrange("p (a b) -> p a b", a=a)
            dma("sp", sv, src, [], [stg.b], "wst%d_%d" % (tag, k % nst))
            cp(engines[k % len(engines)], dst, sv, [stg.b], [], wp=[dstbuf])
        return wload

    winv = w_in.rearrange("(kt p) c -> p kt c", p=128)

    stA = ExitStack()
    WKV = sb("wkv", [128, KT, 2048], BF16, False, stA)
    WKVb = [Buf("wkv%d" % i) for i in range(KT)]
    wlA = make_wload(2, ("pool", "dve"), stA)

    MT = sb("mt", [128, NH, 128])
    QDT = sb("qdt", [128, 2, NH, 128])
    CTAB = sb("ctab_s", [128, 6, 128])
    PCOL = sb("pcol_s", [128, 2])
    ID32 = sb("id32", [128, 128])
    DL = sb("dl", [128, 16])
    EFB = sb("efb_s", [128, 2, NO])
    MFB = sb("mfb_s", [128, 2, NO])
    CT_ = sb("cT_s", [128, KT, 33])
    for (t, src) in [(ID32, ident), (DL, dlog), (EFB, efb), (MFB, mfb), (GN, gn), (VFL, vfl), (AFL, afl),
                     (CW, cw), (CW2, cw2), (CTAB, ctab), (PCOL, pcol), (CT_, cT)]:
        full = tuple([slice(None)] * len(src.shape))
        dma("sp", t[full], src[full], [], [t.b], "cst")
    P.barrier()
    act(IDB[:, :], ID32[:, :], AF.Copy, [ID32.b], [IDB.b])
    P.op("pool", lambda E: E.memset(ONES[:, :], 1.0), w=[ONES.b])
    P.op("pool", lambda E: E.memset(SNAP[0][:, :], 0.0), w=[SNAP[0].b])
    P.op("pool", lambda E: E.memset(SNAP[1][:, :], 0.0), w=[SNAP[1].b])
    act(ST_[:, :, :], CT_[:, :, :], AF.Silu, [CT_.b], [ST_.b])
    act(STB[:, :, :], ST_[:, :, :], AF.Copy, [ST_.b], [STB.b])

    wmv = w_mod.rearrange("(kt p) c -> p kt c", p=128)

    def mod_items(vs, WM, BMB, ROW, GB_, psa, psv, nq=2, WMB=None):
        step = [0]
        seq = [(v, j) for v in vs for j in range(4)]
        issued = set()

        ktn = KT // nq

        def issue(idx, hf):
            if idx >= len(seq) or (idx, hf) in issued:
                return
            issued.add((idx, hf))
            v, j = seq[idx]
            c0 = v * D + j * 512
            wm = WM[hf % len(WM)]
            dma("sp", wm[:, :, :], wmv[:, hf * ktn:(hf + 1) * ktn, c0:c0 + 512], [], [wm.b],
                "wm%d" % (hf % len(WM)))

        for idx, (v, j) in enumerate(seq):
            def item(idx=idx, v=v, j=j):
                col0 = v * D + j * 512
                pt, pb = ps_f32(*psa[step[0] % len(psa)])
                bmb = BMB[step[0] % 2]
                dma("sp", bmb[:, :], bm[:, col0:col0 + 512], [], [bmb.b], "bmb%d" % (step[0] % 2))
                for hf in range(nq):
                    wm = WM[hf % len(WM)]
                    issue(idx, hf)
                    lhs = ST_
                    if WMB is not None:
                        wmb = WMB[hf % len(WMB)]
                        cp(("act", "dve")[hf % 2], wmb[:, :, :], wm[:, :, :], [wm.b], [wmb.b])
                        wm = wmb
                        lhs = STB
                    for k in range(ktn):
                        kt = hf * ktn + k
                        mm(pt[0:33, :], lhs[:, kt, :], wm[:, k, :], kt == 0, kt == KT - 1, [lhs.b, wm.b], pb)
                tt("dve", ROW[:, j * 512:(j + 1) * 512], pt[0:33, :], bmb[:, :], ALU.add, pb + [bmb.b], [],
                   wp=[ROW.b])
                step[0] += 1
            yield item
            if j != 3:
                continue

            def fin(v=v):
                if v in (2, 5):
                    for j in range(4):
                        pt, pb = ps_f32(*psa[j % len(psa)])
                        mm(pt[:, :], ONES[0:1, :], ROW[0:1, j * 512:(j + 1) * 512], True, True, [ONES.b, ROW.b], pb)
                        act(GB_[:, j * 512:(j + 1) * 512], pt[:, :], AF.Copy, pb, [], wp=[GB_.b])
                    dma(STQ, (TG1 if v == 2 else TG2)[:, :], GB_[:, :], [GB_.b], [], "st_gb", wp=[DB("TG%d" % v)])
                else:
                    rows = [(0, {0: 0, 1: 1, 3: 2, 4: 3}[v])]
                    if v in (0, 1):
                        rows.append((32, 4 + v))
                    for (rw, slot) in rows:
                        pt, pb = ps_f32(*psv)
                        for kt in range(KT):
                            mm(pt[:, kt:kt + 1], ROW[rw:rw + 1, kt * 128:(kt + 1) * 128], ONES[rw:rw + 1, 0:1],
                               True, True, [ROW.b, ONES.b], pb)
                        if v in (1, 4):
                            gi = 0 if v == 1 else 1
                            stt(MODF[:, slot, :], pt[:, 0:KT], 1.0, GN[:, gi, :], ALU.add, ALU.mult, pb + [GN.b], [],
                                wp=[MODF.b])
                        else:
                            act(MODF[:, slot, :], pt[:, 0:KT], AF.Copy, pb, [], wp=[MODF.b])
            yield fin

    WM = [sb("wm%d" % i, [128, KT // 2, 512]) for i in range(2)]
    BMB = [sb("bmb%d" % i, [33, 512]) for i in range(2)]
    ROW = sb("row", [33, D])
    for it_ in mod_items((0, 1), WM, BMB, ROW, None, [(0, 0), (1, 0), (2, 0)], (3, 1)):
        it_()
    for kt in range(KT):
        wlA(WKV[:, kt, :], winv[:, kt, 1024:3072], WKVb[kt])

    TMPS = sb("tmps", [128, 16])
    act(TMPS[:, :], DL[:, :], AF.Exp, [DL.b], [TMPS.b], scale=-1.0)
    ts("dve", TMPS[:, :], TMPS[:, :], 1.0, None, ALU.add, None, [TMPS.b], [TMPS.b])
    act(LG[:, :], TMPS[:, :], AF.Ln, [TMPS.b], [LG.b])
    ts("dve", LG[:, :], LG[:, :], -1.0, None, ALU.mult, None, [LG.b], [LG.b])
    TA = sb("ta", [128, 128])
    TB = sb("tb", [128, 128])
    for h in range(NH):
        act(TA[:, :], CTAB[:, 0, :], AF.Exp, [CTAB.b, LG.b], [TA.b], scale=LG[:, h:h + 1])
        tt("dve", TA[:, :], TA[:, :], CTAB[:, 2, :], ALU.mult, [TA.b, CTAB.b], [TA.b])
        act(TB[:, :], CTAB[:, 1, :], AF.Exp, [CTAB.b, LG.b], [TB.b], scale=LG[:, 8 + h:9 + h])
        tt("dve", TB[:, :], TB[:, :], CTAB[:, 3, :], ALU.mult, [TB.b, CTAB.b], [TB.b])
        tt("dve", MT[:, h, :], TA[:, :], TB[:, :], ALU.add, [TA.b, TB.b], [], wp=[MT.b])
        act(QDT[:, 0, h, :], CTAB[:, 4, :], AF.Exp, [CTAB.b, LG.b], [], wp=[QDT.b], scale=LG[:, h:h + 1])
        act(QDT[:, 1, h, :], CTAB[:, 5, :], AF.Exp, [CTAB.b, LG.b], [], wp=[QDT.b], scale=LG[:, 8 + h:9 + h])
    dma(STQ, TMT[:, :], MT[:, :, :].rearrange("p h i -> p (h i)"), [MT.b], [], "st_mt", wp=[DB("TMT")])
    dma(STQ, TQD[:, :], QDT[:, :, :, :].rearrange("p a h i -> p (a h i)"), [QDT.b], [], "st_qd", wp=[DB("TQD")])
    act(KD[:, 0:8], LG[:, 0:8], AF.Exp, [LG.b, PCOL.b], [], wp=[KD.b], scale=PCOL[:, 0:1])
    act(KD[:, 8:16], LG[:, 8:16], AF.Exp, [LG.b, PCOL.b], [], wp=[KD.b], scale=PCOL[:, 1:2])
    act(G128[:, :], LG[:, :], AF.Exp, [LG.b], [G128.b], scale=128.0)
    TC = sb("tc", [128, NO])
    for d_ in range(2):
        for h in range(NH):
            act(TC[:, :], EFB[:, d_, :], AF.Exp, [EFB.b, LG.b], [TC.b], scale=LG[:, 8 * d_ + h:8 * d_ + h + 1])
            tt("dve", COEF[:, d_, :, h], TC[:, :], MFB[:, d_, :], ALU.mult, [TC.b, MFB.b], [], wp=[COEF.b])
    end_phase()

    def rms_n(src_t, ssq, rstd, xn, sqj):
        act(sqj[:, :], src_t[:, :], AF.Square, [src_t.b], [ssq.b], wp=[sqj.b], accum_out=ssq[:, 0:1])
        ts("dve", rstd[:, :], ssq[:, :], 1.0 / D, EPS, ALU.mult, ALU.add, [ssq.b], [rstd.b])
        act(rstd[:, :], rstd[:, :], AF.Sqrt, [rstd.b], [rstd.b])
        P.op("dve", lambda E: E.reciprocal(out=rstd[:, :], in_=rstd[:, :]), r=[rstd.b], w=[rstd.b])
        act(xn[:, :], src_t[:, :], AF.Copy, [src_t.b, rstd.b], [xn.b], scale=rstd[:, 0:1])

    def tr16(xn, psi, ht, so, sh):
        pst, psb = ps_bf16(psi)
        pv = pst.rearrange("p (k t) -> p k t", t=128)
        for kt in range(KT):
            tr(pv[:, kt, :], xn[:, kt * 128:(kt + 1) * 128], [xn.b], [psb[kt // 8]])
        for kt in range(KT):
            if kt % 2 == 0 or not DVE_EVAC:
                act(ht[:, kt, :], pv[:, kt, :], AF.Identity, [psb[kt // 8], MODF.b], [], wp=[ht.b],
                    scale=MODF[:, so, kt:kt + 1], bias=MODF[:, sh, kt:kt + 1])
            else:
                ts("dve", ht[:, kt, :], pv[:, kt, :], MODF[:, so, kt:kt + 1], MODF[:, sh, kt:kt + 1],
                   ALU.mult, ALU.add, [psb[kt // 8], MODF.b], [], wp=[ht.b])

    def rope(pt, pb, rt, outt, outb, nh, R1, R2):
        x4 = pt.rearrange("p (h a b f) -> p h a b f", h=nh, a=2, b=2, f=32)
        o4 = outt.rearrange("p (h a b f) -> p h a b f", h=nh, a=2, b=2, f=32)
        cosv = rt[:, 0:64].rearrange("p (a f) -> p a f", a=2).unsqueeze(1).broadcast_to([128, nh, 2, 32])
        sinv = rt[:, 64:128].rearrange("p (a f) -> p a f", a=2).unsqueeze(1).broadcast_to([128, nh, 2, 32])
        n = nh * 64
        t1 = R1[:, 0:n].rearrange("p (h a f) -> p h a f", h=nh, a=2)
        t2 = R2[:, 0:n].rearrange("p (h a f) -> p h a f", h=nh, a=2)
        x1 = x4[:, :, :, 0, :]
        x2 = x4[:, :, :, 1, :]
        tt("dve", t1, x1, cosv, ALU.mult, pb + [rt.b], [R1.b])
        tt("dve", t2, x2, sinv, ALU.mult, pb + [rt.b], [R2.b])
        tt("dve", o4[:, :, :, 0, :], t1, t2, ALU.subtract, [R1.b, R2.b], [], wp=[outb])
        tt("dve", t1, x1, sinv, ALU.mult, pb + [rt.b], [R1.b])
        tt("dve", t2, x2, cosv, ALU.mult, pb + [rt.b], [R2.b])
        tt("dve", o4[:, :, :, 1, :], t1, t2, ALU.add, [R1.b, R2.b], [], wp=[outb])

    XB = [sb("xb%d" % i, [128, D]) for i in range(2)]
    RT = [sb("rt%d" % i, [128, 128]) for i in range(2)]
    SQJ = sb("sqj", [128, D], BF16)
    SSQ = [sb("ssq%d" % i, [128, 1]) for i in range(2)]
    RSTD = [sb("rstd%d" % i, [128, 1]) for i in range(2)]
    XN = [sb("xn%d" % i, [128, D], BF16) for i in range(2)]
    HTt = [sb("ht%d" % i, [128, KT, 128], BF16) for i in range(2)]
    KR = [sb("kr%d" % i, [128, 1024], BF16) for i in range(2)]
    KF = [sb("kf%d" % i, [128, 1024], BF16) for i in range(2)]
    KB_ = [sb("kb%d" % i, [128, 1024], BF16) for i in range(2)]
    VB = [sb("vb%d" % i, [128, 1024], BF16) for i in range(2)]
    R1 = sb("r1", [128, 512])
    R2 = sb("r2", [128, 512])
    STMP = sb("stmp", [128, 1024])
    KTS = [sb("kts%d" % i, [128, NH, 128], BF16) for i in range(2)]
    SFB = [sb("sfb%d" % i, [128, 1024], BF16) for i in range(2)]
    KVST = [sb("kvst%d" % i, [128, 1024]) for i in range(2)]
    NS = NO + NE

    def A_N(i):
        if i >= NS:
            return
        xb = XB[i % 2]
        src = xo[i] if i < NO else xe[i - NO]
        dma("sp", xb[:, :], src, [], [xb.b], "xb%d" % (i % 2))
        rms_n(xb, SSQ[i % 2], RSTD[i % 2], XN[i % 2], SQJ)

    def A_T(i):
        if i >= NS:
            return
        ht = HTt[i % 2]
        so, sh = (5, 4) if i < 2 else (1, 0)
        tr16(XN[i % 2], 0, ht, so, sh)
        if i >= NO:
            m = i - NO
            store(HT[:, :, m * 128:(m + 1) * 128], ht[:, :, :], [ht.b], [DB("HT")], "st_ht%d" % (i % 2))

    def A_RT(i):
        if i >= NS:
            return
        rt = RT[i % 2]
        dma("sp", rt[:, :], ropek[i], [], [rt.b], "rt%d" % (i % 2))

    def A_MM(i, which):
        if i >= NS:
            return
        ht = HTt[i % 2]
        pt, pb = ps_f32(1 if which == 0 else 2)
        for cgi in range(2):
            cg = which * 2 + cgi
            for kt in range(KT):
                mm(pt[:, cgi * 512:(cgi + 1) * 512], ht[:, kt, :], WKV[:, kt, cg * 512:(cg + 1) * 512],
                   kt == 0, kt == KT - 1, [ht.b, WKVb[kt]], [pb[cgi]])

    def A_s3(i):
        rt = RT[i % 2]
        kr, kf, kb, vb = KR[i % 2], KF[i % 2], KB_[i % 2], VB[i % 2]
        pk, pkb = ps_f32(1)
        pv_, pvb = ps_f32(2)
        act(vb[:, :], pv_, AF.Copy, pvb + [VFL.b], [vb.b], scale=VFL[:, i:i + 1])
        rope(pk, pkb, rt, kr[:, :], kr.b, NH, R1, R2)
        tt("pool", h3(kf[:, :]), h3(kr[:, :]), KD[:, 0:8].to_broadcast([128, NH, 128]), ALU.mult,
           [kr.b, KD.b], [kf.b])
        tt("pool", h3(kb[:, :]), h3(kr[:, :]), KD[:, 8:16].to_broadcast([128, NH, 128]), ALU.mult,
           [kr.b, KD.b], [kb.b])
        if i >= NO:
            m = i - NO
            pst, psb = ps_bf16(0)
            pv3 = pst.rearrange("p (k t) -> p k t", t=128)
            for h in range(NH):
                tr(pv3[:, h, :], kr[:, h * 128:(h + 1) * 128], [kr.b], [psb[0]])
            kts = KTS[i % 2]
            act(kts[:, :, :], pv3[:, 0:NH, :], AF.Copy, [psb[0]], [kts.b])
            store(KTs[m], kts[:, :, :], [kts.b], [DB("KT")], "st_kts%d" % (i % 2))
            store(Vs[m], vb[:, :], [vb.b], [DB("V")], "st_vb%d" % (i % 2))

    def A_s4(i, d_):
        kx = (KF if d_ == 0 else KB_)[i % 2]
        vb = VB[i % 2]
        pt, pb = ps_f32(3)
        for h in range(NH):
            mm(pt[:, h * 128:(h + 1) * 128], kx[:, h * 128:(h + 1) * 128], vb[:, h * 128:(h + 1) * 128], True, True,
               [kx.b, vb.b], [pb[h // 4]])
        if i < NO:
            tt("dve", h3(STMP[:, :]), h3(pt), COEF[:, d_, i, :].to_broadcast([128, NH, 128]), ALU.mult,
               pb + [COEF.b], [STMP.b])
            tt("dve", SNAP[d_][:, :], SNAP[d_][:, :], STMP[:, :], ALU.add, [SNAP[d_].b, STMP.b], [SNAP[d_].b])
        else:
            m = i - NO
            if d_ == 0:
                sfb = SFB[i % 2]
                act(sfb[:, :], SNAP[0][:, :], AF.Copy, [SNAP[0].b], [sfb.b])
                store(SFs[m], sfb[:, :], [sfb.b], [DB("SF")], "st_sfb%d" % (i % 2))
                tt("dve", h3(STMP[:, :]), h3(SNAP[0][:, :]), G128[:, 0:8].to_broadcast([128, NH, 128]), ALU.mult,
                   [SNAP[0].b, G128.b], [STMP.b])
                tt("dve", SNAP[0][:, :], STMP[:, :], pt, ALU.add, [STMP.b] + pb, [SNAP[0].b])
            else:
                kv = KVST[i % 2]
                act(kv[:, :], pt, AF.Copy, pb, [kv.b])
                store(KVB[m], kv[:, :], [kv.b], [DB("KVB%d" % m)], "st_kv%d" % (i % 2))

    A_N(0)
    A_N(1)
    A_T(0)
    A_RT(0)
    A_MM(0, 0)
    A_MM(0, 1)
    A_N(2)
    A_T(1)
    for i in range(NS):
        A_RT(i + 1)
        A_s3(i)
        A_N(i + 3)
        A_T(i + 2)
        A_s4(i, 0)
        A_MM(i + 1, 0)
        A_s4(i, 1)
        A_MM(i + 1, 1)
        tick()
    end_phase()
    stA.close()

    stB = ExitStack()
    HTO = sb("hto", [128, KT, NE * 128], BF16, False, stB)
    for q in range(4):
        dma("sp", HTO[:, q * 4:(q + 1) * 4, :], HT[:, q * 4:(q + 1) * 4, :], [DB("HT")], [], "hto", wp=[HTO.b])
    WB = [sb("wb%d" % i, [128, KT, 512], BF16) for i in range(2)]
    wl = make_wload(3, ("act", "pool", "act", "dve"))
    RT = [sb("rtq%d" % i, [128, 128]) for i in range(2)]
    QR = [sb("qr%d" % i, [128, 512], BF16) for i in range(2)]
    QTS_ = [sb("qts%d" % i, [128, 4, 128], BF16) for i in range(2)]
    GS = [sb("gs%d" % i, [128, 512], BF16) for i in range(2)]
    R1 = sb("r1b", [128, 256])
    R2 = sb("r2b", [128, 256])
    WM = [sb("wmb%d" % i, [128, KT // 4, 512]) for i in range(2)]
    WMB = [sb("wmbb%d" % i, [128, KT // 4, 512], BF16) for i in range(2)]
    BMB = [sb("bmbb%d" % i, [33, 512]) for i in range(2)]
    ROW = sb("rowb", [33, D])
    GB_ = sb("gbb", [128, D])
    modgen = mod_items((2, 3, 4), WM, BMB, ROW, GB_, [(2, 0), (2, 1), (3, 0)], (3, 1), nq=4, WMB=WMB)

    def B_wl(g):
        wb = WB[g % 2]
        col0 = g * 512 if g < 2 else 3072 + (g - 2) * 512
        for q4 in range(4):
            yield (lambda q4=q4: wl(wb[:, q4 * 4:(q4 + 1) * 4, :],
                                    winv[:, q4 * 4:(q4 + 1) * 4, col0:col0 + 512], wb.b, a=4))

    for f in B_wl(0):
        f()
    itn = 0
    for g in range(4):
        wb = WB[g % 2]
        nxt = list(B_wl(g + 1)) if g + 1 < 4 else []
        for m in range(NE):
            if m in (2, 6, 10, 14) and nxt:
                nxt.pop(0)()
            if g < 2:
                rt = RT[m % 2]
                dma("sp", rt[:, :], ropeq[m], [], [rt.b], "rtq%d" % (m % 2))
            pt, pb = ps_f32(1, m % 2)
            for kt in range(KT):
                mm(pt, HTO[:, kt, m * 128:(m + 1) * 128], wb[:, kt, :], kt == 0, kt == KT - 1, [HTO.b, wb.b], pb)
            if g < 2:
                qr = QR[m % 2]
                rope(pt, pb, rt, qr[:, :], qr.b, 4, R1, R2)
                pst, psb = ps_bf16(0)
                pv3 = pst.rearrange("p (k t) -> p k t", t=128)
                o8 = (m % 2) * 8
                for h in range(4):
                    tr(pv3[:, o8 + h, :], qr[:, h * 128:(h + 1) * 128], [qr.b], [psb[m % 2]])
                qts = QTS_[m % 2]
                act(qts[:, :, :], pv3[:, o8:o8 + 4, :], AF.Copy, [psb[m % 2]], [qts.b])
                store(QTs[m][:, g * 4:(g + 1) * 4, :], qts[:, :, :], [qts.b], [DB("QT")], "st_qts%d" % (m % 2))
            else:
                gs = GS[m % 2]
                act(gs[:, :], pt, AF.Silu, pb, [gs.b])
                store(Gs[m][:, (g - 2) * 512:(g - 1) * 512], gs[:, :], [gs.b], [DB("G")], "st_gs%d" % (m % 2))
            if itn % 4 == 1:
                nx = next(modgen, None)
                if nx is not None:
                    nx()
            itn += 1
            tick()
    for nx in modgen:
        nx()
    end_phase()
    WC = [sb("wc%d" % i, [128, 3, KT, 128], BF16) for i in range(2)]
    wl = make_wload(3, ("pool",))
    CSB = [sb("csb%d" % i, [128, 384]) for i in range(2)]
    UU = [sb("uu%d" % i, [128, 384]) for i in range(2)]
    YY = [sb("yy%d" % i, [128, 384]) for i in range(2)]
    CVT = [sb("cvt%d" % i, [128, 384], BF16) for i in range(2)]
    sets = [((2, 0), (2, 1), (3, 0)), ((3, 1), (1, 0), (1, 1))]
    SFBb = [sb("sfbb%d" % i, [128, 1024], BF16) for i in range(2)]
    KVSb = [sb("kvsb%d" % i, [128, 1024]) for i in range(2)]
    STMPb = sb("stmpb", [128, 1024])

    def bwd_steps():
        for m in range(NE - 1, -1, -1):
            def stp(m=m):
                sfb = SFBb[m % 2]
                act(sfb[:, :], SNAP[1][:, :], AF.Copy, [SNAP[1].b], [sfb.b])
                store(SBs[m], sfb[:, :], [sfb.b], [DB("SB")], "st_sfbb%d" % (m % 2))
                if m > 0:
                    kv = KVSb[m % 2]
                    dma("sp", kv[:, :], KVB[m], [DB("KVB%d" % m)], [kv.b], "ld_kvb%d" % (m % 2))
                    tt("dve", h3(STMPb[:, :]), h3(SNAP[1][:, :]), G128[:, 8:16].to_broadcast([128, NH, 128]),
                       ALU.mult, [SNAP[1].b, G128.b], [STMPb.b])
                    tt("dve", SNAP[1][:, :], STMPb[:, :], kv[:, :], ALU.add, [STMPb.b, kv.b], [SNAP[1].b])
            yield stp
    bwdgen = bwd_steps()

    def B2_wl(c):
        wc = WC[c % 2]
        for j3 in range(3):
            cb = 4096 + j3 * 1024 + c * 128
            yield (lambda j3=j3, cb=cb: wl(wc[:, j3, :, :], winv[:, :, cb:cb + 128], wc.b, a=KT))

    for f in B2_wl(0):
        f()
    it = 0
    for c in range(8):
        wc = WC[c % 2]
        nxt = list(B2_wl(c + 1)) if c + 1 < 8 else []
        for tb in range(6):
            if tb in (1, 2, 3) and nxt:
                nxt.pop(0)()
            bk = [ps_f32(a_, b_) for (a_, b_) in sets[it % 2]]
            for j3 in range(3):
                pt, pb = bk[j3]
                for kt in range(KT):
                    mm(pt[:, 0:384], wc[:, j3, kt, :], HTO[:, kt, tb * 384:(tb + 1) * 384], kt == 0, kt == KT - 1,
                       [wc.b, HTO.b], pb)
            (pB, pBb), (pC, pCb), (pH, pHb) = bk
            csb, uu, yy, cvt = CSB[it % 2], UU[it % 2], YY[it % 2], CVT[it % 2]
            act(csb[:, :], pC[:, 0:384], AF.Copy, pCb, [csb.b])
            tt("dve", uu[:, :], csb[:, :], pH[:, 0:384], ALU.mult, [csb.b] + pHb, [uu.b])
            act(yy[:, :], uu[:, :], AF.Copy, [uu.b, CW.b], [yy.b], scale=CW[:, c, 1:2])
            u3 = uu[:, :].rearrange("p (r c) -> p r c", c=64)
            y3 = yy[:, :].rearrange("p (r c) -> p r c", c=64)
            stt(y3[:, :, 1:64], u3[:, :, 0:63], CW[:, c, 0:1], y3[:, :, 1:64], ALU.mult, ALU.add,
                [uu.b, yy.b, CW.b], [yy.b])
            stt(y3[:, :, 0:63], u3[:, :, 1:64], CW[:, c, 2:3], y3[:, :, 0:63], ALU.mult, ALU.add,
                [uu.b, yy.b, CW.b], [yy.b])
            tt("dve", cvt[:, :], pB[:, 0:384], yy[:, :], ALU.mult, pBb + [yy.b], [cvt.b])
            store(CTs[c][:, tb * 384:(tb + 1) * 384], cvt[:, :], [cvt.b], [DB("CT")], "st_cvt%d" % (it % 2))
            it += 1
            nx = next(bwdgen, None)
            if nx is not None:
                nx()
            tick()
    for nx in bwdgen:
        nx()
    end_phase()
    stB.close()

    stC = ExitStack()
    WO = sb("wo", [128, KT, D], BF16, False, stC)
    WOb = [Buf("wo%d" % i) for i in range(KT)]
    wov = w_out.rearrange("(kt p) c -> p kt c", p=128)
    wlC = make_wload(2, ("act",), stC)
    MT = sb("mt2", [128, 1024])
    QDT = sb("qdt2", [128, 2, NH, 128])
    dma("sp", MT[:, :], TMT[:, :], [DB("TMT")], [MT.b], "ld_mt")
    dma("sp", QDT[:, :, :, :].rearrange("p a h i -> p (a h i)"), TQD[:, :], [DB("TQD")], [QDT.b], "ld_qd")
    L = {}
    for nm, shp, dt in [("qt", [128, NH, 128], BF16), ("kt", [128, NH, 128], BF16), ("v", [128, 1024], BF16),
                        ("g", [128, 1024], BF16), ("sf", [128, 1024], BF16), ("sb", [128, 1024], BF16),
                        ("qf", [128, NH, 128], BF16), ("qb", [128, NH, 128], BF16), ("pt", [128, 1024], BF16),
                        ("ret", [128, 1024], BF16), ("rett", [128, NH, 128], BF16)]:
        L[nm] = [sb("c1%s%d" % (nm, i), shp, dt) for i in range(2)]
    OSQ = sb("osq", [128, 1024])
    R1c = sb("r1c", [128, 1024])
    SSO = [sb("sso%d" % i, [128, NH]) for i in range(2)]

    def C1_a(m):
        if m >= NE:
            return
        j = m % 2
        qt, kt_, vv, gg, sf, sbb = L["qt"][j], L["kt"][j], L["v"][j], L["g"][j], L["sf"][j], L["sb"][j]
        dma("sp", qt[:, :, :], QTs[m], [DB("QT")], [qt.b], "l_qt%d" % j)
        dma("sp", kt_[:, :, :], KTs[m], [DB("KT")], [kt_.b], "l_kt%d" % j)
        dma("sp", vv[:, :], Vs[m], [DB("V")], [vv.b], "l_v%d" % j)
        dma("sp", gg[:, :], Gs[m], [DB("G")], [gg.b], "l_g%d" % j)
        dma("sp", sf[:, :], SFs[m], [DB("SF")], [sf.b], "l_sf%d" % j)
        dma("sp", sbb[:, :], SBs[m], [DB("SB")], [sbb.b], "l_sb%d" % j)
        qf, qb = L["qf"][j], L["qb"][j]
        tt("pool", qf[:, :, :], qt[:, :, :], QDT[:, 0, :, :], ALU.mult, [qt.b, QDT.b], [qf.b])
        tt("pool", qb[:, :, :], qt[:, :, :], QDT[:, 1, :, :], ALU.mult, [qt.b, QDT.b], [qb.b])
        pA, pAb = ps_f32(0)
        for h in range(NH):
            mm(pA[:, h * 128:(h + 1) * 128], kt_[:, h, :], qt[:, h, :], True, True, [kt_.b, qt.b], [pAb[h // 4]])
        ptt = L["pt"][j]
        tt("dve", ptt[:, :], pA, MT[:, :], ALU.mult, pAb + [MT.b], [ptt.b])

    def C1_b(m):
        j = m % 2
        vv, gg, sf, sbb = L["v"][j], L["g"][j], L["sf"][j], L["sb"][j]
        qf, qb, ptt = L["qf"][j], L["qb"][j], L["pt"][j]
        pB, pBb = ps_f32(1 + j)
        for h in range(NH):
            hs = slice(h * 128, (h + 1) * 128)
            mm(pB[:, hs], ptt[:, hs], vv[:, hs], True, False, [ptt.b, vv.b], [pBb[h // 4]])
            mm(pB[:, hs], qf[:, h, :], sf[:, hs], False, False, [qf.b, sf.b], [pBb[h // 4]])
            mm(pB[:, hs], qb[:, h, :], sbb[:, hs], False, True, [qb.b, sbb.b], [pBb[h // 4]])
        act(OSQ[:, :], pB, AF.Square, pBb, [OSQ.b])
        sso = SSO[j]
        P.op("dve", lambda E, sso=sso: E.tensor_reduce(out=sso[:, :], in_=h3(OSQ[:, :]), axis=AX.X, op=ALU.add),
             r=[OSQ.b], w=[sso.b])
        rsq(sso[:, :], sso[:, :], sso.b, 1.0 / 128)

    def C1_b2(m):
        if m < 0:
            return
        j = m % 2
        gg = L["g"][j]
        sso = SSO[j]
        pB, pBb = ps_f32(1 + j)
        tt("dve", h3(R1c[:, :]), h3(pB), sso[:, :].to_broadcast([128, NH, 128]), ALU.mult, pBb + [sso.b], [R1c.b])
        ret = L["ret"][j]
        tt("dve", ret[:, :], R1c[:, :], gg[:, :], ALU.mult, [R1c.b, gg.b], [ret.b])

    def C1_c(m):
        if m < 0:
            return
        j = m % 2
        ret = L["ret"][j]
        pst, psb = ps_bf16(3)
        pv3 = pst.rearrange("p (k t) -> p k t", t=128)
        for h in range(NH):
            tr(pv3[:, h, :], ret[:, h * 128:(h + 1) * 128], [ret.b], [psb[0]])
        rett = L["rett"][j]
        act(rett[:, :, :], pv3[:, 0:NH, :], AF.Copy, [psb[0]], [rett.b])
        store(RTs[m], rett[:, :, :], [rett.b], [DB("RT")], "st_rett%d" % j)

    WM = [sb("wmc%d" % i, [128, KT // 4, 512]) for i in range(2)]
    BMB = [sb("bmbc%d" % i, [33, 512]) for i in range(2)]
    ROW = sb("rowc", [33, D])
    GB_ = sb("gbc", [128, D])
    WMB = [sb("wmcb%d" % i, [128, KT // 4, 512], BF16) for i in range(2)]
    modgen = mod_items((5,), WM, BMB, ROW, GB_, [(3, 1)], (3, 1), nq=4, WMB=WMB)
    C1_a(0)
    for m in range(NE + 2):
        C1_b2(m - 1 if m - 1 < NE else -1)
        C1_c(m - 2)
        C1_a(m + 1)
        if m < NE:
            C1_b(m)
        if m < KT:
            wlC(WO[:, m, :], wov[:, m, :], WOb[m])
        nx = next(modgen, None)
        if nx is not None:
            nx()
        tick()
    for nx in modgen:
        nx()
    end_phase()

    G1B = sb("g1b", [128, D])
    dma("sp", G1B[:, :], TG1[:, :], [DB("TG2")], [G1B.b], "ld_g1")
    MXT = [sb("mxt%d" % i, [128, KT, 128], BF16) for i in range(2)]
    XMc = [sb("xmc%d" % i, [128, D]) for i in range(3)]
    TMPX = sb("tmpx", [128, D])
    SQJ = sb("sqj2", [128, D], BF16)
    SSQ = [sb("ssq2%d" % i, [128, 1]) for i in range(2)]
    RSTD = [sb("rstd2%d" % i, [128, 1]) for i in range(2)]
    XN = [sb("xn2%d" % i, [128, D], BF16) for i in range(2)]
    H2c = [sb("h2c%d" % i, [128, KT, 128], BF16) for i in range(2)]
    ctv = CTs.rearrange("c p t -> p c t")

    def C2_l(m):
        if m >= NE:
            return
        j = m % 2
        mxt, xm = MXT[j], XMc[m % 3]
        dma("sp", mxt[:, 0:8, :], RTs[m], [DB("RT")], [], "l_mxa%d" % j, wp=[mxt.b])
        dma("sp", mxt[:, 8:16, :], ctv[:, :, m * 128:(m + 1) * 128], [DB("CT")], [], "l_mxa%d" % j, wp=[mxt.b])
        dma("sp", xm[:, :], xe[m], [], [xm.b], "l_xm%d" % (m % 3))

    def C2_a(m):
        if m >= NE:
            return
        j = m % 2
        mxt, xm = MXT[j], XMc[m % 3]
        pC, pCb = ps_f32(2)
        pD, pDb = ps_f32(3)
        for cg in range(4):
            tgt = (pC if cg < 2 else pD)[:, (cg % 2) * 512:(cg % 2 + 1) * 512]
            tb_ = (pCb if cg < 2 else pDb)[cg % 2]
            for kt in range(KT):
                mm(tgt, mxt[:, kt, :], WO[:, kt, cg * 512:(cg + 1) * 512], kt == 0, kt == KT - 1,
                   [mxt.b, WOb[kt]], [tb_])
        tt("dve", TMPX[:, 0:1024], pC, G1B[:, 0:1024], ALU.mult, pCb + [G1B.b], [], wp=[TMPX.b])
        tt("dve", TMPX[:, 1024:2048], pD, G1B[:, 1024:2048], ALU.mult, pDb + [G1B.b], [], wp=[TMPX.b])
        tt("pool", xm[:, :], xm[:, :], TMPX[:, :], ALU.add, [xm.b, TMPX.b], [xm.b])
        store(XM[m], xm[:, :], [xm.b], [DB("XM")], "st_xm%d" % (m % 3))
        rms_n(xm, SSQ[j], RSTD[j], XN[j], SQJ)

    def C2_b(m):
        if m < 0:
            return
        j = m % 2
        h2 = H2c[j]
        tr16(XN[j], j, h2, 3, 2)
        store(H2T[:, :, m * 128:(m + 1) * 128], h2[:, :, :], [h2.b], [DB("H2T")], "st_h2%d" % j)

    C2_l(0)
    C2_l(1)
    for m in range(NE):
        C2_a(m)
        C2_b(m - 1)
        tick()
        C2_l(m + 2)
    C2_b(NE - 1)
    end_phase()
    stC.close()

    H2O = sb("h2o", [128, KT, NE * 128], BF16)
    for q in range(4):
        dma("sp", H2O[:, q * 4:(q + 1) * 4, :], H2T[:, q * 4:(q + 1) * 4, :], [DB("H2T")], [], "h2o", wp=[H2O.b])
    WU = [sb("wu%d" % i, [128, 2, KT, 128], BF16) for i in range(2)]
    wl = make_wload(3, ("pool",))
    wuv = w_up.rearrange("(kt p) c -> p kt c", p=128)
    ASB = [sb("asb%d" % i, [128, 2176]) for i in range(2)]
    YD = [sb("yd%d" % i, [128, 2048]) for i in range(2)]
    SD = [sb("sd%d" % i, [128, 2048], BF16) for i in range(2)]
    UD = [sb("ud%d" % i, [128, 2048], BF16) for i in range(2)]
    utv = UT.rearrange("n p c t -> p n c t")
    ablk = [(64, 512), (576, 512), (1088, 512), (1600, 512), (2112, 128)]
    pa_i = 0
    pb_i = 0

    def D_wl(c):
        if c >= NFT:
            return
        wu = WU[c % 2]
        wl(wu[:, 0, :, :], wuv[:, :, c * 128:(c + 1) * 128], wu.b, a=KT)
        wl(wu[:, 1, :, :], wuv[:, :, DFF + c * 128:DFF + (c + 1) * 128], wu.b, a=KT)

    D_wl(0)
    for c in range(NFT):
        wu = WU[c % 2]
        D_wl(c + 1)
        asb, yd, sd, ud = ASB[c % 2], YD[c % 2], SD[c % 2], UD[c % 2]
        for bi, (t0, n) in enumerate(ablk):
            pt, pb = ps_f32(pa_i % 2, (pa_i // 2) % 2)
            pa_i += 1
            for kt in range(KT):
                mm(pt[:, 0:n], wu[:, 0, kt, :], H2O[:, kt, t0:t0 + n], kt == 0, kt == KT - 1, [wu.b, H2O.b], pb)
            a0 = t0 - 64
            if bi == 0:
                act(asb[:, 0:64], pt[:, 0:64], AF.Copy, pb + [AFL.b], [], wp=[asb.b], scale=AFL[:, 0:1])
                act(asb[:, 64:512], pt[:, 64:512], AF.Copy, pb, [], wp=[asb.b])
            elif bi == 4:
                act(asb[:, a0:a0 + 64], pt[:, 0:64], AF.Copy, pb, [], wp=[asb.b])
                act(asb[:, a0 + 64:a0 + 128], pt[:, 64:128], AF.Copy, pb + [AFL.b], [], wp=[asb.b],
                    scale=AFL[:, 1:2])
            else:
                act(asb[:, a0:a0 + n], pt[:, 0:n], AF.Copy, pb, [], wp=[asb.b])
        act(yd[:, :], asb[:, 64:2112], AF.Identity, [asb.b, CW2.b], [yd.b], scale=CW2[:, c, 1:2], bias=CW2[:, c, 3:4])
        stt(yd[:, :], asb[:, 0:2048], CW2[:, c, 0:1], yd[:, :], ALU.mult, ALU.add, [asb.b, yd.b, CW2.b], [yd.b])
        stt(yd[:, :], asb[:, 128:2176], CW2[:, c, 2:3], yd[:, :], ALU.mult, ALU.add, [asb.b, yd.b, CW2.b], [yd.b])
        act(sd[:, :], yd[:, :], AF.Silu, [yd.b], [sd.b])
        for bi in range(4):
            pt, pb = ps_f32(2 + pb_i % 2, (pb_i // 2) % 2)
            pb_i += 1
            t0 = 128 + bi * 512
            for kt in range(KT):
                mm(pt, wu[:, 1, kt, :], H2O[:, kt, t0:t0 + 512], kt == 0, kt == KT - 1, [wu.b, H2O.b], pb)
            tt("dve", ud[:, bi * 512:(bi + 1) * 512], sd[:, bi * 512:(bi + 1) * 512], pt, ALU.mult, [sd.b] + pb, [],
               wp=[ud.b])
        store(utv[:, :, c, :], ud[:, :].rearrange("p (n t) -> p n t", t=128), [ud.b], [DB("UT")], "st_ud%d" % (c % 2))
        tick()
    end_phase()

    WD = [sb("wd%d" % i, [128, NFT, 512], BF16) for i in range(2)]
    wl = make_wload(3, ("act", "dve", "act", "pool"))
    wdv = w_down.rearrange("(kt p) c -> p kt c", p=128)
    G2B = sb("g2b", [128, D])
    FGB = sb("fgb_s", [128, D])
    dma("sp", G2B[:, :], TG2[:, :], [DB("TG5")], [G2B.b], "ld_g2")
    dma("sp", FGB[:, :], fgb[:, :], [], [FGB.b], "ld_fg")
    UTc = [sb("utc%d" % i, [128, NFT, 128], BF16) for i in range(2)]
    XMq = [sb("xmq%d" % i, [128, 512]) for i in range(2)]
    ZC = [sb("zc%d" % i, [128, 512]) for i in range(2)]
    ZJ = sb("zj", [128, 512], BF16)
    SSF = sb("ssf", [128, 16, 4])

    def E_wl(cg):
        wd = WD[cg % 2]
        for q in range(NFT // 4):
            yield (lambda q=q: wl(wd[:, q * 4:(q + 1) * 4, :], wdv[:, q * 4:(q + 1) * 4, cg * 512:(cg + 1) * 512],
                                  wd.b, a=4))

    def E_l(it):
        if it >= 64:
            return
        cg, n = it // 16, it % 16
        j = it % 2
        utc, xmq = UTc[j], XMq[j]
        dma("sp", utc[:, :, :], UT[n], [DB("UT")], [utc.b], "l_utc%d" % j)
        dma("sp", xmq[:, :], XM[n + 1][:, cg * 512:(cg + 1) * 512], [DB("XM")], [xmq.b], "l_xmq%d" % j)

    for f in E_wl(0):
        f()
    E_l(0)
    it = 0
    for cg in range(4):
        wd = WD[cg % 2]
        nxt = list(E_wl(cg + 1)) if cg + 1 < 4 else []
        for n in range(16):
            E_l(it + 1)
            if nxt and n >= 2:
                nxt.pop(0)()
            j = it % 2
            utc, xmq, zc = UTc[j], XMq[j], ZC[j]
            pt, pb = ps_f32((it // 2) % 4, it % 2)
            for kt in range(NFT):
                mm(pt, utc[:, kt, :], wd[:, kt, :], kt == 0, kt == NFT - 1, [utc.b, wd.b], pb)
            tt("dve", zc[:, :], pt, G2B[:, cg * 512:(cg + 1) * 512], ALU.mult, pb + [G2B.b], [zc.b])
            tt("dve", zc[:, :], zc[:, :], xmq[:, :], ALU.add, [zc.b, xmq.b], [zc.b])
            act(ZJ[:, :], zc[:, :], AF.Square, [zc.b], [], wp=[ZJ.b, SSF.b], accum_out=SSF[:, n, cg:cg + 1])
            store(XO[n][:, cg * 512:(cg + 1) * 512], zc[:, :], [zc.b], [DB("XO%d" % n)], "st_zc%d" % j)
            it += 1
            tick()
        while nxt:
            nxt.pop(0)()
    tick()
    tick()
    RF = sb("rf", [128, 16])
    P.op("dve", lambda E: E.tensor_reduce(out=RF[:, :], in_=SSF[:, :, :], axis=AX.X, op=ALU.add), r=[SSF.b], w=[RF.b])
    rsq(RF[:, :], RF[:, :], RF.b, 1.0 / D)
    ZF = [sb("zf%d" % i, [128, D]) for i in range(2)]
    OF = [sb("of%d" % i, [128, 1024]) for i in range(2)]
    dma("sp", ZF[0][:, :], XO[0], [DB("XO0")], [ZF[0].b], "l_zf0")
    k = 0
    for n in range(16):
        zf = ZF[n % 2]
        if n + 1 < 16:
            dma("sp", ZF[(n + 1) % 2][:, :], XO[n + 1], [DB("XO%d" % (n + 1))], [ZF[(n + 1) % 2].b],
                "l_zf%d" % ((n + 1) % 2))
        for hf in range(2):
            of = OF[k % 2]
            stt(of[:, :], zf[:, hf * 1024:(hf + 1) * 1024], RF[:, n:n + 1], FGB[:, hf * 1024:(hf + 1) * 1024],
                ALU.mult, ALU.mult, [zf.b, RF.b, FGB.b], [of.b])
            dma(STQ, out[n][:, hf * 1024:(hf + 1) * 1024], of[:, :], [of.b], [], "st_of%d" % (k % 2),
                wp=[DB("out")])
            k += 1
    P.barrier()
    P.emit()
    pstack[0].close()
    gstack.close()
    return nc


DEBUG_OUT = ()
_NC_CACHE = {}


def _host_tables(s):
    f32 = np.float32
    G0 = 16 * s - 1
    G1 = 16 * s + 16
    ext = [G0 + m for m in range(NE)]
    extset = set(g for g in ext if 0 <= g < 64)
    others = [g for g in range(64) if g not in extset]
    efb = np.zeros((2, NO), f32)
    mfb = np.zeros((2, NO), f32)
    for c in range(2):
        efb[0, c] = 128.0 * G0 + 128.0 * (1 - c)
        efb[1, c] = 128.0 * (63 - G1) + 128.0 * c
        mfb[:, c] = 1.0
    olist = []
    for k in range(NO - 2):
        if k < len(others):
            g = others[k]
            olist.append(g)
            if g < G0:
                efb[0, 2 + k] = 128.0 * (G0 - 1 - g)
                mfb[0, 2 + k] = 1.0
            if g > G1:
                efb[1, 2 + k] = 128.0 * (g - G1 - 1)
                mfb[1, 2 + k] = 1.0
        else:
            olist.append(others[0])
    freqs = (np.float32(10000.0) ** (-np.arange(0, 64, 2, dtype=f32) / np.float32(64))).astype(f32)

    def table(g, scale):
        g = min(max(g, 0), 63)
        pos = g * 128 + np.arange(128)
        row = (pos // 64).astype(f32)
        col = (pos % 64).astype(f32)
        ar = row[:, None] * freqs[None, :]
        ac = col[:, None] * freqs[None, :]
        t = np.concatenate([np.cos(ar), np.cos(ac), np.sin(ar), np.sin(ac)], axis=1).astype(f32)
        return (t * f32(scale)).astype(f32)

    ks = f32(128.0 ** -0.5)
    ctx_t = np.concatenate([np.full((128, 64), ks, f32), np.zeros((128, 64), f32)], axis=1)
    ropek = np.stack([ctx_t, ctx_t] + [table(g, ks) for g in olist] + [table(g, ks) for g in ext]).astype(f32)
    ropeq = np.stack([table(g, 1.0) for g in ext]).astype(f32)
    vfl = np.ones((NO + NE,), f32)
    for m, g in enumerate(ext):
        if not (0 <= g < 64):
            vfl[NO + m] = 0.0
    afl = np.array([0.0 if s == 0 else 1.0, 0.0 if s == 3 else 1.0], f32)
    return ext, olist, efb, mfb, ropek, ropeq, vfl, afl


def _fm(v, nt):
    return np.ascontiguousarray(np.asarray(v, np.float32).reshape(nt, 128).T)


def kernel(x, c, ctx, c_ctx, w_mod, b_mod, norm1_g, w_in, ret_decay_fwd, ret_decay_bwd,
           conv_w, w_out, norm2_g, w_up, ffn_conv_w, ffn_conv_b, w_down, final_g):
    f32 = np.float32
    x = np.asarray(x, f32)
    ctx = np.asarray(ctx, f32)
    c = np.asarray(c, f32)
    c_ctx = np.asarray(c_ctx, f32)
    if "nc" not in _NC_CACHE:
        _NC_CACHE["nc"] = build_nc()
    nc = _NC_CACHE["nc"]
    rep = lambda v: np.ascontiguousarray(np.broadcast_to(np.asarray(v, f32)[None], (128,) + np.asarray(v).shape))
    jj = np.arange(128, dtype=f32)
    rel = jj[None, :] - jj[:, None]
    ctab = np.stack([np.maximum(rel, 0), np.maximum(-rel, 0), (rel >= 0).astype(f32), (rel < 0).astype(f32),
                     np.broadcast_to(jj[None, :] + 1, (128, 128)), np.broadcast_to(128 - jj[None, :], (128, 128))],
                    axis=1).astype(f32)
    pcol = np.stack([127 - jj, jj], axis=1).astype(f32)
    shared = {
        "w_mod": np.ascontiguousarray(np.asarray(w_mod, f32)[0]),
        "w_in": np.ascontiguousarray(np.asarray(w_in, f32)[0]),
        "w_out": np.ascontiguousarray(np.asarray(w_out, f32)[0]),
        "w_up": np.ascontiguousarray(np.asarray(w_up, f32)[0]),
        "w_down": np.ascontiguousarray(np.asarray(w_down, f32)[0]),
        "gn": np.ascontiguousarray(np.stack([_fm(norm1_g[0], KT), _fm(norm2_g[0], KT)], axis=1)),
        "fgb": rep(final_g),
        "dlog": rep(np.concatenate([np.asarray(ret_decay_fwd, f32)[0], np.asarray(ret_decay_bwd, f32)[0]])),
        "cw": np.ascontiguousarray(np.stack([_fm(np.asarray(conv_w, f32)[0, t], 8) for t in range(3)], axis=2)),
        "cw2": np.ascontiguousarray(np.stack([_fm(np.asarray(ffn_conv_w, f32)[0, t], NFT) for t in range(3)]
                                             + [_fm(np.asarray(ffn_conv_b, f32)[0], NFT)], axis=2)),
        "ctab": ctab, "pcol": pcol, "ident": np.eye(128, dtype=f32),
    }
    bm = np.zeros((33, 6 * D), f32)
    bm[0] = np.asarray(b_mod, f32)[0]
    bm[32] = np.asarray(b_mod, f32)[0]
    shared["bm"] = bm
    in_maps = []
    for core in range(8):
        b, s = core // 4, core % 4
        ext, olist, efb, mfb, ropek, ropeq, vfl, afl = _host_tables(s)
        xc = x[b].reshape(64, 128, D)
        xo = np.concatenate([ctx[b].reshape(2, 128, D), xc[olist]], axis=0)
        xe = np.zeros((NE, 128, D), f32)
        for m, g in enumerate(ext):
            if 0 <= g < 64:
                xe[m] = xc[g]
        cT = np.zeros((128, KT, 33), f32)
        cT[:, :, 0] = _fm(c[b], KT)
        cT[:, :, 32] = _fm(c_ctx, KT)
        d = dict(shared)
        d.update({"xo": np.ascontiguousarray(xo), "xe": xe, "ropek": ropek, "ropeq": ropeq,
                  "efb": rep(efb), "mfb": rep(mfb), "vfl": rep(vfl), "afl": rep(afl), "cT": cT})
        in_maps.append(d)
    res = run_bass_kernel_spmd(nc, in_maps, core_ids=list(range(8)))
    _NC_CACHE["res"] = res
    outp = np.zeros((2, 8192, D), f32)
    for core in range(8):
        b, s = core // 4, core % 4
        outp[b, s * 2048:(s + 1) * 2048] = np.asarray(res.results[core]["out"]).reshape(2048, D)
    return outp
```

```python
import numpy as np
import concourse.bass as bass
import concourse.mybir as mybir
from concourse.bass_utils import run_bass_kernel_spmd
from contextlib import ExitStack

F32 = mybir.dt.float32
BF16 = mybir.dt.bfloat16
AF = mybir.ActivationFunctionType
ALU = mybir.AluOpType
AX = mybir.AxisListType

D = 2048
KT = 16
NH = 8
DFF = 5632
NFT = 44
NO = 49
NE = 18
EPS = 1e-6
DEBUG = False
DVE_EVAC = False
STRICT = False


class Buf:
    __slots__ = ("name", "writers", "readers", "prev")

    def __init__(self, name):
        self.name = name
        self.writers = []
        self.readers = []
        self.prev = []


class Op:
    __slots__ = ("eng", "fn", "deps", "signal", "sem", "val", "is_dma")


def _prune(lst, o):
    lst[:] = [p for p in lst if p.sem != o.sem]
    lst.append(o)


class Prog:
    def __init__(self, nc):
        self.nc = nc
        self.ops = []
        self.engs = dict(pe=nc.tensor, act=nc.scalar, dve=nc.vector, pool=nc.gpsimd, sp=nc.sync)
        self.sems = []
        self.esem = {}
        for e in ("pe", "act", "dve", "pool"):
            self.esem[e] = self._newsem("p_" + e)
        self.dsem = {}
        self.dpool = []
        self.bar = []
        self.last = {}

    def _newsem(self, name):
        self.sems.append(self.nc.alloc_semaphore(name=name))
        return len(self.sems) - 1

    def op(self, eng, fn, r=(), w=(), wp=(), dma=None):
        o = Op()
        o.eng = eng
        o.fn = fn
        o.is_dma = dma is not None
        o.signal = o.is_dma
        o.val = None
        if o.is_dma:
            if dma not in self.dsem:
                k = len(self.dsem)
                if k >= len(self.dpool):
                    self.dpool.append(self._newsem("d%d" % k))
                self.dsem[dma] = self.dpool[k]
            o.sem = self.dsem[dma]
        else:
            o.sem = self.esem[eng]
        deps = []
        for p in self.bar:
            deps.append((p, "raw"))
        for b in r:
            for p in b.writers:
                deps.append((p, "raw"))
        for b, partial in [(x, False) for x in w] + [(x, True) for x in wp]:
            if b.readers:
                b.prev = b.readers + b.writers
                b.readers = []
                b.writers = []
            for p in b.prev:
                deps.append((p, "war"))
            if not partial:
                for p in b.writers:
                    deps.append((p, "waw"))
        for b in r:
            _prune(b.readers, o)
        for b in list(w) + list(wp):
            _prune(b.writers, o)
        o.deps = []
        for p, kind in deps:
            if p is o:
                continue
            if (not p.is_dma) and (not o.is_dma) and p.eng == eng and (eng == "pe" or (kind == "war" and not STRICT)):
                continue
            p.signal = True
            o.deps.append(p)
        self.ops.append(o)
        self.last[(eng, o.sem)] = o
        return o

    def barrier(self):
        self.bar = list(self.last.values())

    def new_phase(self):
        self.barrier()
        self.dsem = {}

    def emit(self):
        cnt = [0] * len(self.sems)
        known = {}
        for o in self.ops:
            E = self.engs[o.eng]
            need = {}
            for p in o.deps:
                if need.get(p.sem, 0) < p.val:
                    need[p.sem] = p.val
            for s, v in need.items():
                if known.get((o.eng, s), 0) < v:
                    E.wait_ge(self.sems[s], v)
                    known[(o.eng, s)] = v
            ins = o.fn(E)
            if o.signal:
                inc = 16 if o.is_dma else 1
                cnt[o.sem] += inc
                o.val = cnt[o.sem]
                ins.then_inc(self.sems[o.sem], inc)
            else:
                o.val = cnt[o.sem] + 1
        for s in self.dpool:
            if cnt[s] > 0:
                self.nc.sync.wait_ge(self.sems[s], cnt[s])
        assert max(cnt) < 60000, max(cnt)


def build_nc():
    nc = bass.Bass("TRN2", target_bir_lowering=False)
    P = Prog(nc)

    def din(name, shape, dt=F32):
        return nc.dram_tensor(name, list(shape), dt, kind="ExternalInput").ap()

    def dscr(name, shape, dt):
        kind = "ExternalOutput" if (DEBUG and name in DEBUG_OUT) else "Internal"
        return nc.dram_tensor(name, list(shape), dt, kind=kind).ap()

    xo = din("xo", [NO, 128, D])
    xe = din("xe", [NE, 128, D])
    ropek = din("ropek", [NO + NE, 128, 128])
    ropeq = din("ropeq", [NE, 128, 128])
    efb = din("efb", [128, 2, NO])
    mfb = din("mfb", [128, 2, NO])
    vfl = din("vfl", [128, NO + NE])
    afl = din("afl", [128, 2])
    cT = din("cT", [128, KT, 33])
    w_mod = din("w_mod", [D, 6 * D])
    bm = din("bm", [33, 6 * D])
    gn = din("gn", [128, 2, KT])
    fgb = din("fgb", [128, D])
    w_in = din("w_in", [D, 7168])
    w_out = din("w_out", [D, D])
    w_up = din("w_up", [D, 2 * DFF])
    w_down = din("w_down", [DFF, D])
    dlog = din("dlog", [128, 16])
    cw = din("cw", [128, 8, 3])
    cw2 = din("cw2", [128, NFT, 4])
    ctab = din("ctab", [128, 6, 128])
    pcol = din("pcol", [128, 2])
    ident = din("ident", [128, 128])
    out = nc.dram_tensor("out", [16, 128, D], F32, kind="ExternalOutput").ap()

    HT = dscr("HT", [128, KT, NE * 128], BF16)
    KTs = dscr("KTs", [NE, 128, NH, 128], BF16)
    Vs = dscr("Vs", [NE, 128, 1024], BF16)
    KVB = dscr("KVB", [NE, 128, 1024], F32)
    QTs = dscr("QTs", [NE, 128, NH, 128], BF16)
    Gs = dscr("Gs", [NE, 128, 1024], BF16)
    CTs = dscr("CTs", [8, 128, NE * 128], BF16)
    SFs = dscr("SFs", [NE, 128, 1024], BF16)
    SBs = dscr("SBs", [NE, 128, 1024], BF16)
    XM = dscr("XM", [NE, 128, D], F32)
    H2T = dscr("H2T", [128, KT, NE * 128], BF16)
    UT = dscr("UT", [16, 128, NFT, 128], BF16)
    XO = dscr("XO", [16, 128, D], F32)
    dbuf = {}

    def DB(name):
        if name not in dbuf:
            dbuf[name] = Buf(name)
        return dbuf[name]

    TMT = dscr("TMT", [128, 1024], F32)
    TQD = dscr("TQD", [128, 2048], F32)
    TG1 = dscr("TG1", [128, D], F32)
    TG2 = dscr("TG2", [128, D], F32)
    RTs = dscr("RTs", [NE, 128, NH, 128], BF16)

    gstack = ExitStack()
    pstack = [ExitStack()]

    class Tile:
        def __init__(self, name, shape, dt, st):
            self.t = st.enter_context(nc.sbuf_tensor(name, list(shape), dt))
            self.b = Buf(name)

        def __getitem__(self, idx):
            return self.t[idx]

    def sb(name, shape, dt=F32, persist=False, st=None):
        if st is None:
            st = gstack if persist else pstack[0]
        return Tile(name, shape, dt, st)

    def end_phase():
        tick()
        tick()
        P.new_phase()
        pstack[0].close()
        pstack[0] = ExitStack()

    PS = []
    for i in range(4):
        t = gstack.enter_context(nc.psum_tensor("ps%d" % i, [128, 1024], F32))
        PS.append((t, [Buf("ps%da" % i), Buf("ps%db" % i)]))

    def ps_f32(i, half=None):
        t, bs = PS[i]
        if half is None:
            return t[:, :], list(bs)
        return t[:, half * 512:(half + 1) * 512], [bs[half]]

    def ps_bf16(i):
        t, bs = PS[i]
        return t[:, :].bitcast(BF16), list(bs)

    def dma(q, o, i, r, w, key, wp=()):
        return P.op(q, lambda E: E.dma_start(out=o, in_=i), r=r, w=w, wp=wp, dma=key)

    pend = [[], []]
    STQ = "sp"

    def store(o, i, r, wbufs, key):
        pend[1].append(lambda: dma(STQ, o, i, r, [], key, wp=wbufs))

    def tick():
        for f in pend[0]:
            f()
        pend[0] = pend[1]
        pend[1] = []

    def act(o, i, func, r, w, wp=(), **kw):
        return P.op("act", lambda E: E.activation(out=o, in_=i, func=func, **kw), r=r, w=w, wp=wp)

    def tt(eng, o, a, b, op, r, w, wp=()):
        return P.op(eng, lambda E: E.tensor_tensor(out=o, in0=a, in1=b, op=op), r=r, w=w, wp=wp)

    def ts(eng, o, a, s1, s2, op0, op1, r, w, wp=()):
        if op1 is None:
            return P.op(eng, lambda E: E.tensor_scalar(out=o, in0=a, scalar1=s1, scalar2=None, op0=op0),
                        r=r, w=w, wp=wp)
        return P.op(eng, lambda E: E.tensor_scalar(out=o, in0=a, scalar1=s1, scalar2=s2, op0=op0, op1=op1),
                    r=r, w=w, wp=wp)

    def stt(o, a, s, b, op0, op1, r, w, wp=()):
        return P.op("dve", lambda E: E.scalar_tensor_tensor(out=o, in0=a, scalar=s, in1=b, op0=op0, op1=op1),
                    r=r, w=w, wp=wp)

    def cp(eng, o, i, r, w, wp=()):
        if eng == "act":
            return act(o, i, AF.Copy, r, w, wp)
        return P.op(eng, lambda E: E.tensor_copy(out=o, in_=i), r=r, w=w, wp=wp)

    def mm(o, l, rh, st, sp_, r, wp):
        return P.op("pe", lambda E: E.matmul(o, lhsT=l, rhs=rh, start=st, stop=sp_), r=r, wp=wp)

    def tr(o, i, r, wp):
        return P.op("pe", lambda E: E.transpose(out=o, in_=i, identity=IDB[:, :]), r=list(r) + [IDB.b], wp=wp)

    def rsq(o, oi, ob, scale):
        ts("dve", o, oi, scale, EPS, ALU.mult, ALU.add, [ob], [ob])
        act(o, o, AF.Sqrt, [ob], [ob])
        P.op("dve", lambda E: E.reciprocal(out=o, in_=o), r=[ob], w=[ob])

    def h3(ap):
        return ap.rearrange("p (h e) -> p h e", h=NH)

    IDB = sb("idb", [128, 128], BF16, True)
    LG = sb("lg", [128, 16], F32, True)
    KD = sb("kd", [128, 16], F32, True)
    G128 = sb("g128", [128, 16], F32, True)
    COEF = sb("coef", [128, 2, NO, NH], F32, True)
    MODF = sb("modf", [128, 6, KT], F32, True)
    GN = sb("gn_s", [128, 2, KT], F32, True)
    VFL = sb("vfl_s", [128, NO + NE], F32, True)
    AFL = sb("afl_s", [128, 2], F32, True)
    CW = sb("cw_s", [128, 8, 3], F32, True)
    CW2 = sb("cw2_s", [128, NFT, 4], F32, True)
    SNAP = [sb("snapf", [128, 1024], F32, True), sb("snapb", [128, 1024], F32, True)]
    ONES = sb("ones", [33, 128], F32, True)
    ST_ = sb("sT_s", [128, KT, 33], F32, True)
    STB = sb("sT_b", [128, KT, 33], BF16, True)
    wst_n = [0]

    def make_wload(nst, engines, st=None):
        WST = [sb("wst%d_%d" % (wst_n[0], i), [128, 2048], F32, False, st) for i in range(nst)]
        wst_n[0] += 1
        cnt = [0]
        tag = wst_n[0]

        def wload(dst, src, dstbuf, a=None):
            k = cnt[0]
            cnt[0] += 1
            stg = WST[k % nst]
            sv = stg[:, :] if a is None else stg[:, :].rearrange("p (a b) -> p a b", a=a)
            dma("sp", sv, src, [], [stg.b], "wst%d_%d" % (tag, k % nst))
            cp(engines[k % len(engines)], dst, sv, [stg.b], [], wp=[dstbuf])
        return wload

    winv = w_in.rearrange("(kt p) c -> p kt c", p=128)

    stA = ExitStack()
    WKV = sb("wkv", [128, KT, 2048], BF16, False, stA)
    WKVb = [Buf("wkv%d" % i) for i in range(KT)]
    wlA = make_wload(2, ("pool", "dve"), stA)

    MT = sb("mt", [128, NH, 128])
    QDT = sb("qdt", [128, 2, NH, 128])
    CTAB = sb("ctab_s", [128, 6, 128])
    PCOL = sb("pcol_s", [128, 2])
    ID32 = sb("id32", [128, 128])
    DL = sb("dl", [128, 16])
    EFB = sb("efb_s", [128, 2, NO])
    MFB = sb("mfb_s", [128, 2, NO])
    CT_ = sb("cT_s", [128, KT, 33])
    for (t, src) in [(ID32, ident), (DL, dlog), (EFB, efb), (MFB, mfb), (GN, gn), (VFL, vfl), (AFL, afl),
                     (CW, cw), (CW2, cw2), (CTAB, ctab), (PCOL, pcol), (CT_, cT)]:
        full = tuple([slice(None)] * len(src.shape))
        dma("sp", t[full], src[full], [], [t.b], "cst")
    P.barrier()
    act(IDB[:, :], ID32[:, :], AF.Copy, [ID32.b], [IDB.b])
    P.op("pool", lambda E: E.memset(ONES[:, :], 1.0), w=[ONES.b])
    P.op("pool", lambda E: E.memset(SNAP[0][:, :], 0.0), w=[SNAP[0].b])
    P.op("pool", lambda E: E.memset(SNAP[1][:, :], 0.0), w=[SNAP[1].b])
    act(ST_[:, :, :], CT_[:, :, :], AF.Silu, [CT_.b], [ST_.b])
    act(STB[:, :, :], ST_[:, :, :], AF.Copy, [ST_.b], [STB.b])

    wmv = w_mod.rearrange("(kt p) c -> p kt c", p=128)

    def mod_items(vs, WM, BMB, ROW, GB_, psa, psv, nq=2, WMB=None):
        step = [0]
        seq = [(v, j) for v in vs for j in range(4)]
        issued = set()

        ktn = KT // nq

        def issue(idx, hf):
            if idx >= len(seq) or (idx, hf) in issued:
                return
            issued.add((idx, hf))
            v, j = seq[idx]
            c0 = v * D + j * 512
            wm = WM[hf % len(WM)]
            dma("sp", wm[:, :, :], wmv[:, hf * ktn:(hf + 1) * ktn, c0:c0 + 512], [], [wm.b],
                "wm%d" % (hf % len(WM)))

        ntile = len(seq) * nq
        done_ld = set()
        done_cs = set()

        def ld(t, q):
            if t >= ntile or t in done_ld:
                return
            done_ld.add(t)
            i_, hf_ = divmod(t, nq)
            v_, j_ = seq[i_]
            c0 = v_ * D + j_ * 512
            wm = WM[t % len(WM)]
            dma(q, wm[:, :, :], wmv[:, hf_ * ktn:(hf_ + 1) * ktn, c0:c0 + 512], [], [wm.b],
                "wm%d" % (t % len(WM)))

        def cs(t, eng):
            if t >= ntile or t in done_cs:
                return
            done_cs.add(t)
            wm = WM[t % len(WM)]
            wmb = WMB[t % len(WMB)]
            cp(eng, wmb[:, :, :], wm[:, :, :], [wm.b], [wmb.b])

        for idx, (v, j) in enumerate(seq):
            def item(idx=idx, v=v, j=j):
                col0 = v * D + j * 512
                pt, pb = ps_f32(*psa[step[0] % len(psa)])
                bmb = BMB[step[0] % 2]
                dma("sp", bmb[:, :], bm[:, col0:col0 + 512], [], [bmb.b], "bmb%d" % (step[0] % 2))
                for hf in range(nq):
                    wm = WM[hf % len(WM)]
                    lhs = ST_
                    if WMB is not None:
                        t = idx * nq + hf
                        ld(t, "sp")
                        cs(t, ("act", "dve")[hf % 2])
                        wm = WMB[t % len(WMB)]
                        lhs = STB
                    else:
                        issue(idx, hf)
                    for k in range(ktn):
                        kt = hf * ktn + k
                        mm(pt[0:33, :], lhs[:, kt, :], wm[:, k, :], kt == 0, kt == KT - 1, [lhs.b, wm.b], pb)
                if WMB is not None:
                    n0 = (idx + 1) * nq
                    ld(n0, "pool")
                    cs(n0, "pool")
                    ld(n0 + 1, "pool")
                    cs(n0 + 1, "pool")
                    ld(n0 + 2, "pool")
                    ld(n0 + 3, "pool")
                tt("dve", ROW[:, j * 512:(j + 1) * 512], pt[0:33, :], bmb[:, :], ALU.add, pb + [bmb.b], [],
                   wp=[ROW.b])
                step[0] += 1
            yield item
            if j != 3:
                continue

            def fin(v=v):
                if v in (2, 5):
                    for j in range(4):
                        pt, pb = ps_f32(*psa[j % len(psa)])
                        mm(pt[:, :], ONES[0:1, :], ROW[0:1, j * 512:(j + 1) * 512], True, True, [ONES.b, ROW.b], pb)
                        act(GB_[:, j * 512:(j + 1) * 512], pt[:, :], AF.Copy, pb, [], wp=[GB_.b])
                    dma(STQ, (TG1 if v == 2 else TG2)[:, :], GB_[:, :], [GB_.b], [], "st_gb", wp=[DB("TG%d" % v)])
                else:
                    rows = [(0, {0: 0, 1: 1, 3: 2, 4: 3}[v])]
                    if v in (0, 1):
                        rows.append((32, 4 + v))
                    for (rw, slot) in rows:
                        pt, pb = ps_f32(*psv)
                        for kt in range(KT):
                            mm(pt[:, kt:kt + 1], ROW[rw:rw + 1, kt * 128:(kt + 1) * 128], ONES[rw:rw + 1, 0:1],
                               True, True, [ROW.b, ONES.b], pb)
                        if v in (1, 4):
                            gi = 0 if v == 1 else 1
                            stt(MODF[:, slot, :], pt[:, 0:KT], 1.0, GN[:, gi, :], ALU.add, ALU.mult, pb + [GN.b], [],
                                wp=[MODF.b])
                        else:
                            act(MODF[:, slot, :], pt[:, 0:KT], AF.Copy, pb, [], wp=[MODF.b])
            yield fin

    WM = [sb("wm%d" % i, [128, KT // 2, 512]) for i in range(2)]
    BMB = [sb("bmb%d" % i, [33, 512]) for i in range(2)]
    ROW = sb("row", [33, D])
    for it_ in mod_items((0, 1), WM, BMB, ROW, None, [(0, 0), (1, 0), (2, 0)], (3, 1)):
        it_()
    for kt in range(KT):
        wlA(WKV[:, kt, :], winv[:, kt, 1024:3072], WKVb[kt])

    TMPS = sb("tmps", [128, 16])
    act(TMPS[:, :], DL[:, :], AF.Exp, [DL.b], [TMPS.b], scale=-1.0)
    ts("dve", TMPS[:, :], TMPS[:, :], 1.0, None, ALU.add, None, [TMPS.b], [TMPS.b])
    act(LG[:, :], TMPS[:, :], AF.Ln, [TMPS.b], [LG.b])
    ts("dve", LG[:, :], LG[:, :], -1.0, None, ALU.mult, None, [LG.b], [LG.b])
    TA = sb("ta", [128, 128])
    TB = sb("tb", [128, 128])
    for h in range(NH):
        act(TA[:, :], CTAB[:, 0, :], AF.Exp, [CTAB.b, LG.b], [TA.b], scale=LG[:, h:h + 1])
        tt("dve", TA[:, :], TA[:, :], CTAB[:, 2, :], ALU.mult, [TA.b, CTAB.b], [TA.b])
        act(TB[:, :], CTAB[:, 1, :], AF.Exp, [CTAB.b, LG.b], [TB.b], scale=LG[:, 8 + h:9 + h])
        tt("dve", TB[:, :], TB[:, :], CTAB[:, 3, :], ALU.mult, [TB.b, CTAB.b], [TB.b])
        tt("dve", MT[:, h, :], TA[:, :], TB[:, :], ALU.add, [TA.b, TB.b], [], wp=[MT.b])
        act(QDT[:, 0, h, :], CTAB[:, 4, :], AF.Exp, [CTAB.b, LG.b], [], wp=[QDT.b], scale=LG[:, h:h + 1])
        act(QDT[:, 1, h, :], CTAB[:, 5, :], AF.Exp, [CTAB.b, LG.b], [], wp=[QDT.b], scale=LG[:, 8 + h:9 + h])
    dma(STQ, TMT[:, :], MT[:, :, :].rearrange("p h i -> p (h i)"), [MT.b], [], "st_mt", wp=[DB("TMT")])
    dma(STQ, TQD[:, :], QDT[:, :, :, :].rearrange("p a h i -> p (a h i)"), [QDT.b], [], "st_qd", wp=[DB("TQD")])
    act(KD[:, 0:8], LG[:, 0:8], AF.Exp, [LG.b, PCOL.b], [], wp=[KD.b], scale=PCOL[:, 0:1])
    act(KD[:, 8:16], LG[:, 8:16], AF.Exp, [LG.b, PCOL.b], [], wp=[KD.b], scale=PCOL[:, 1:2])
    act(G128[:, :], LG[:, :], AF.Exp, [LG.b], [G128.b], scale=128.0)
    TC = sb("tc", [128, NO])
    for d_ in range(2):
        for h in range(NH):
            act(TC[:, :], EFB[:, d_, :], AF.Exp, [EFB.b, LG.b], [TC.b], scale=LG[:, 8 * d_ + h:8 * d_ + h + 1])
            tt("dve", COEF[:, d_, :, h], TC[:, :], MFB[:, d_, :], ALU.mult, [TC.b, MFB.b], [], wp=[COEF.b])
    end_phase()

    def rms_n(src_t, ssq, rstd, xn, sqj):
        act(sqj[:, :], src_t[:, :], AF.Square, [src_t.b], [ssq.b], wp=[sqj.b], accum_out=ssq[:, 0:1])
        ts("dve", rstd[:, :], ssq[:, :], 1.0 / D, EPS, ALU.mult, ALU.add, [ssq.b], [rstd.b])
        act(rstd[:, :], rstd[:, :], AF.Sqrt, [rstd.b], [rstd.b])
        P.op("dve", lambda E: E.reciprocal(out=rstd[:, :], in_=rstd[:, :]), r=[rstd.b], w=[rstd.b])
        act(xn[:, :], src_t[:, :], AF.Copy, [src_t.b, rstd.b], [xn.b], scale=rstd[:, 0:1])

    def tr16(xn, psi, ht, so, sh):
        pst, psb = ps_bf16(psi)
        pv = pst.rearrange("p (k t) -> p k t", t=128)
        for kt in range(KT):
            tr(pv[:, kt, :], xn[:, kt * 128:(kt + 1) * 128], [xn.b], [psb[kt // 8]])
        for kt in range(KT):
            if kt % 2 == 0 or not DVE_EVAC:
                act(ht[:, kt, :], pv[:, kt, :], AF.Identity, [psb[kt // 8], MODF.b], [], wp=[ht.b],
                    scale=MODF[:, so, kt:kt + 1], bias=MODF[:, sh, kt:kt + 1])
            else:
                ts("dve", ht[:, kt, :], pv[:, kt, :], MODF[:, so, kt:kt + 1], MODF[:, sh, kt:kt + 1],
                   ALU.mult, ALU.add, [psb[kt // 8], MODF.b], [], wp=[ht.b])

    def rope(pt, pb, rt, outt, outb, nh, R1, R2):
        x4 = pt.rearrange("p (h a b f) -> p h a b f", h=nh, a=2, b=2, f=32)
        o4 = outt.rearrange("p (h a b f) -> p h a b f", h=nh, a=2, b=2, f=32)
        cosv = rt[:, 0:64].rearrange("p (a f) -> p a f", a=2).unsqueeze(1).broadcast_to([128, nh, 2, 32])
        sinv = rt[:, 64:128].rearrange("p (a f) -> p a f", a=2).unsqueeze(1).broadcast_to([128, nh, 2, 32])
        n = nh * 64
        t1 = R1[:, 0:n].rearrange("p (h a f) -> p h a f", h=nh, a=2)
        t2 = R2[:, 0:n].rearrange("p (h a f) -> p h a f", h=nh, a=2)
        x1 = x4[:, :, :, 0, :]
        x2 = x4[:, :, :, 1, :]
        tt("dve", t1, x1, cosv, ALU.mult, pb + [rt.b], [R1.b])
        tt("dve", t2, x2, sinv, ALU.mult, pb + [rt.b], [R2.b])
        tt("dve", o4[:, :, :, 0, :], t1, t2, ALU.subtract, [R1.b, R2.b], [], wp=[outb])
        tt("dve", t1, x1, sinv, ALU.mult, pb + [rt.b], [R1.b])
        tt("dve", t2, x2, cosv, ALU.mult, pb + [rt.b], [R2.b])
        tt("dve", o4[:, :, :, 1, :], t1, t2, ALU.add, [R1.b, R2.b], [], wp=[outb])

    XB = [sb("xb%d" % i, [128, D]) for i in range(2)]
    RT = [sb("rt%d" % i, [128, 128]) for i in range(2)]
    SQJ = sb("sqj", [128, D], BF16)
    SSQ = [sb("ssq%d" % i, [128, 1]) for i in range(2)]
    RSTD = [sb("rstd%d" % i, [128, 1]) for i in range(2)]
    XN = [sb("xn%d" % i, [128, D], BF16) for i in range(2)]
    HTt = [sb("ht%d" % i, [128, KT, 128], BF16) for i in range(2)]
    KR = [sb("kr%d" % i, [128, 1024], BF16) for i in range(2)]
    KF = [sb("kf%d" % i, [128, 1024], BF16) for i in range(2)]
    KB_ = [sb("kb%d" % i, [128, 1024], BF16) for i in range(2)]
    VB = [sb("vb%d" % i, [128, 1024], BF16) for i in range(2)]
    R1 = sb("r1", [128, 512])
    R2 = sb("r2", [128, 512])
    STMP = sb("stmp", [128, 1024])
    KTS = [sb("kts%d" % i, [128, NH, 128], BF16) for i in range(2)]
    SFB = [sb("sfb%d" % i, [128, 1024], BF16) for i in range(2)]
    KVST = [sb("kvst%d" % i, [128, 1024]) for i in range(2)]
    NS = NO + NE

    def A_N(i):
        if i >= NS:
            return
        xb = XB[i % 2]
        src = xo[i] if i < NO else xe[i - NO]
        dma("sp", xb[:, :], src, [], [xb.b], "xb%d" % (i % 2))
        rms_n(xb, SSQ[i % 2], RSTD[i % 2], XN[i % 2], SQJ)

    def A_T(i):
        if i >= NS:
            return
        ht = HTt[i % 2]
        so, sh = (5, 4) if i < 2 else (1, 0)
        tr16(XN[i % 2], 0, ht, so, sh)
        if i >= NO:
            m = i - NO
            store(HT[:, :, m * 128:(m + 1) * 128], ht[:, :, :], [ht.b], [DB("HT")], "st_ht%d" % (i % 2))

    def A_RT(i):
        if i >= NS:
            return
        rt = RT[i % 2]
        dma("sp", rt[:, :], ropek[i], [], [rt.b], "rt%d" % (i % 2))

    def A_MM(i, which):
        if i >= NS:
            return
        ht = HTt[i % 2]
        pt, pb = ps_f32(1 if which == 0 else 2)
        for cgi in range(2):
            cg = which * 2 + cgi
            for kt in range(KT):
                mm(pt[:, cgi * 512:(cgi + 1) * 512], ht[:, kt, :], WKV[:, kt, cg * 512:(cg + 1) * 512],
                   kt == 0, kt == KT - 1, [ht.b, WKVb[kt]], [pb[cgi]])

    def A_s3(i):
        rt = RT[i % 2]
        kr, kf, kb, vb = KR[i % 2], KF[i % 2], KB_[i % 2], VB[i % 2]
        pk, pkb = ps_f32(1)
        pv_, pvb = ps_f32(2)
        act(vb[:, :], pv_, AF.Copy, pvb + [VFL.b], [vb.b], scale=VFL[:, i:i + 1])
        rope(pk, pkb, rt, kr[:, :], kr.b, NH, R1, R2)
        tt("pool", h3(kf[:, :]), h3(kr[:, :]), KD[:, 0:8].to_broadcast([128, NH, 128]), ALU.mult,
           [kr.b, KD.b], [kf.b])
        tt("pool", h3(kb[:, :]), h3(kr[:, :]), KD[:, 8:16].to_broadcast([128, NH, 128]), ALU.mult,
           [kr.b, KD.b], [kb.b])
        if i >= NO:
            m = i - NO
            pst, psb = ps_bf16(0)
            pv3 = pst.rearrange("p (k t) -> p k t", t=128)
            for h in range(NH):
                tr(pv3[:, h, :], kr[:, h * 128:(h + 1) * 128], [kr.b], [psb[0]])
            kts = KTS[i % 2]
            act(kts[:, :, :], pv3[:, 0:NH, :], AF.Copy, [psb[0]], [kts.b])
            store(KTs[m], kts[:, :, :], [kts.b], [DB("KT")], "st_kts%d" % (i % 2))
            store(Vs[m], vb[:, :], [vb.b], [DB("V")], "st_vb%d" % (i % 2))

    def A_s4(i, d_):
        kx = (KF if d_ == 0 else KB_)[i % 2]
        vb = VB[i % 2]
        pt, pb = ps_f32(3)
        for h in range(NH):
            mm(pt[:, h * 128:(h + 1) * 128], kx[:, h * 128:(h + 1) * 128], vb[:, h * 128:(h + 1) * 128], True, True,
               [kx.b, vb.b], [pb[h // 4]])
        if i < NO:
            tt("dve", h3(STMP[:, :]), h3(pt), COEF[:, d_, i, :].to_broadcast([128, NH, 128]), ALU.mult,
               pb + [COEF.b], [STMP.b])
            tt("dve", SNAP[d_][:, :], SNAP[d_][:, :], STMP[:, :], ALU.add, [SNAP[d_].b, STMP.b], [SNAP[d_].b])
        else:
            m = i - NO
            if d_ == 0:
                sfb = SFB[i % 2]
                act(sfb[:, :], SNAP[0][:, :], AF.Copy, [SNAP[0].b], [sfb.b])
                store(SFs[m], sfb[:, :], [sfb.b], [DB("SF")], "st_sfb%d" % (i % 2))
                tt("dve", h3(STMP[:, :]), h3(SNAP[0][:, :]), G128[:, 0:8].to_broadcast([128, NH, 128]), ALU.mult,
                   [SNAP[0].b, G128.b], [STMP.b])
                tt("dve", SNAP[0][:, :], STMP[:, :], pt, ALU.add, [STMP.b] + pb, [SNAP[0].b])
            else:
                kv = KVST[i % 2]
                act(kv[:, :], pt, AF.Copy, pb, [kv.b])
                store(KVB[m], kv[:, :], [kv.b], [DB("KVB%d" % m)], "st_kv%d" % (i % 2))

    A_N(0)
    A_N(1)
    A_T(0)
    A_RT(0)
    A_MM(0, 0)
    A_MM(0, 1)
    A_N(2)
    A_T(1)
    for i in range(NS):
        A_RT(i + 1)
        A_s3(i)
        A_N(i + 3)
        A_T(i + 2)
        A_s4(i, 0)
        A_MM(i + 1, 0)
        A_s4(i, 1)
        A_MM(i + 1, 1)
        tick()
    end_phase()
    stA.close()

    stB = ExitStack()
    HTO = sb("hto", [128, KT, NE * 128], BF16, False, stB)
    for q in range(4):
        dma("sp", HTO[:, q * 4:(q + 1) * 4, :], HT[:, q * 4:(q + 1) * 4, :], [DB("HT")], [], "hto", wp=[HTO.b])
    WB = [sb("wb%d" % i, [128, KT, 512], BF16) for i in range(2)]
    wl = make_wload(3, ("act", "pool", "act", "dve"))
    RT = [sb("rtq%d" % i, [128, 128]) for i in range(2)]
    QR = [sb("qr%d" % i, [128, 512], BF16) for i in range(2)]
    QTS_ = [sb("qts%d" % i, [128, 4, 128], BF16) for i in range(2)]
    GS = [sb("gs%d" % i, [128, 512], BF16) for i in range(2)]
    R1 = sb("r1b", [128, 256])
    R2 = sb("r2b", [128, 256])
    WM = [sb("wmb%d" % i, [128, KT // 4, 512]) for i in range(2)]
    WMB = [sb("wmbb%d" % i, [128, KT // 4, 512], BF16) for i in range(2)]
    BMB = [sb("bmbb%d" % i, [33, 512]) for i in range(2)]
    ROW = sb("rowb", [33, D])
    GB_ = sb("gbb", [128, D])
    modgen = mod_items((2, 3, 4), WM, BMB, ROW, GB_, [(2, 0), (2, 1), (3, 0)], (3, 1), nq=4, WMB=WMB)

    def B_wl(g):
        wb = WB[g % 2]
        col0 = g * 512 if g < 2 else 3072 + (g - 2) * 512
        for q4 in range(4):
            yield (lambda q4=q4: wl(wb[:, q4 * 4:(q4 + 1) * 4, :],
                                    winv[:, q4 * 4:(q4 + 1) * 4, col0:col0 + 512], wb.b, a=4))

    for f in B_wl(0):
        f()
    itn = 0
    qpend = []
    for g in range(4):
        wb = WB[g % 2]
        nxt = list(B_wl(g + 1)) if g + 1 < 4 else []
        for m in range(NE):
            if m in (2, 6, 10, 14) and nxt:
                nxt.pop(0)()
            if g < 2:
                rt = RT[m % 2]
                dma("sp", rt[:, :], ropeq[m], [], [rt.b], "rtq%d" % (m % 2))
            pt, pb = ps_f32(1, m % 2)
            for kt in range(KT):
                mm(pt, HTO[:, kt, m * 128:(m + 1) * 128], wb[:, kt, :], kt == 0, kt == KT - 1, [HTO.b, wb.b], pb)
            if qpend:
                qpend.pop(0)()
            if g < 2:
                qr = QR[m % 2]
                rope(pt, pb, rt, qr[:, :], qr.b, 4, R1, R2)

                def qfin(g=g, m=m, qr=qr):
                    pst, psb = ps_bf16(0)
                    pv3 = pst.rearrange("p (k t) -> p k t", t=128)
                    o8 = (m % 2) * 8
                    for h in range(4):
                        tr(pv3[:, o8 + h, :], qr[:, h * 128:(h + 1) * 128], [qr.b], [psb[m % 2]])
                    qts = QTS_[m % 2]
                    act(qts[:, :, :], pv3[:, o8:o8 + 4, :], AF.Copy, [psb[m % 2]], [qts.b])
                    store(QTs[m][:, g * 4:(g + 1) * 4, :], qts[:, :, :], [qts.b], [DB("QT")],
                          "st_qts%d" % (m % 2))
                qpend.append(qfin)
            else:
                gs = GS[m % 2]
                act(gs[:, :], pt, AF.Silu, pb, [gs.b])
                store(Gs[m][:, (g - 2) * 512:(g - 1) * 512], gs[:, :], [gs.b], [DB("G")], "st_gs%d" % (m % 2))
            if itn % 4 == 1:
                nx = next(modgen, None)
                if nx is not None:
                    nx()
            itn += 1
            tick()
    for nx in modgen:
        nx()
    end_phase()
    WC = [sb("wc%d" % i, [128, 3, KT, 128], BF16) for i in range(2)]
    wl = make_wload(3, ("pool",))
    CSB = [sb("csb%d" % i, [128, 384]) for i in range(2)]
    UU = [sb("uu%d" % i, [128, 384]) for i in range(2)]
    YY = [sb("yy%d" % i, [128, 384]) for i in range(2)]
    CVT = [sb("cvt%d" % i, [128, 384], BF16) for i in range(2)]
    sets = [((2, 0), (2, 1), (3, 0)), ((3, 1), (1, 0), (1, 1))]
    SFBb = [sb("sfbb%d" % i, [128, 1024], BF16) for i in range(2)]
    KVSb = [sb("kvsb%d" % i, [128, 1024]) for i in range(2)]
    STMPb = sb("stmpb", [128, 1024])

    def bwd_steps():
        for m in range(NE - 1, -1, -1):
            def stp(m=m):
                sfb = SFBb[m % 2]
                act(sfb[:, :], SNAP[1][:, :], AF.Copy, [SNAP[1].b], [sfb.b])
                store(SBs[m], sfb[:, :], [sfb.b], [DB("SB")], "st_sfbb%d" % (m % 2))
                if m > 0:
                    kv = KVSb[m % 2]
                    dma("sp", kv[:, :], KVB[m], [DB("KVB%d" % m)], [kv.b], "ld_kvb%d" % (m % 2))
                    tt("dve", h3(STMPb[:, :]), h3(SNAP[1][:, :]), G128[:, 8:16].to_broadcast([128, NH, 128]),
                       ALU.mult, [SNAP[1].b, G128.b], [STMPb.b])
                    tt("dve", SNAP[1][:, :], STMPb[:, :], kv[:, :], ALU.add, [STMPb.b, kv.b], [SNAP[1].b])
            yield stp
    bwdgen = bwd_steps()

    def B2_wl(c):
        wc = WC[c % 2]
        for j3 in range(3):
            cb = 4096 + j3 * 1024 + c * 128
            yield (lambda j3=j3, cb=cb: wl(wc[:, j3, :, :], winv[:, :, cb:cb + 128], wc.b, a=KT))

    for f in B2_wl(0):
        f()
    it = 0
    for c in range(8):
        wc = WC[c % 2]
        nxt = list(B2_wl(c + 1)) if c + 1 < 8 else []
        for tb in range(6):
            if tb in (1, 2, 3) and nxt:
                nxt.pop(0)()
            bk = [ps_f32(a_, b_) for (a_, b_) in sets[it % 2]]
            for j3 in range(3):
                pt, pb = bk[j3]
                for kt in range(KT):
                    mm(pt[:, 0:384], wc[:, j3, kt, :], HTO[:, kt, tb * 384:(tb + 1) * 384], kt == 0, kt == KT - 1,
                       [wc.b, HTO.b], pb)
            (pB, pBb), (pC, pCb), (pH, pHb) = bk
            csb, uu, yy, cvt = CSB[it % 2], UU[it % 2], YY[it % 2], CVT[it % 2]
            act(csb[:, :], pC[:, 0:384], AF.Copy, pCb, [csb.b])
            tt("dve", uu[:, :], csb[:, :], pH[:, 0:384], ALU.mult, [csb.b] + pHb, [uu.b])
            act(yy[:, :], uu[:, :], AF.Copy, [uu.b, CW.b], [yy.b], scale=CW[:, c, 1:2])
            u3 = uu[:, :].rearrange("p (r c) -> p r c", c=64)
            y3 = yy[:, :].rearrange("p (r c) -> p r c", c=64)
            stt(y3[:, :, 1:64], u3[:, :, 0:63], CW[:, c, 0:1], y3[:, :, 1:64], ALU.mult, ALU.add,
                [uu.b, yy.b, CW.b], [yy.b])
            stt(y3[:, :, 0:63], u3[:, :, 1:64], CW[:, c, 2:3], y3[:, :, 0:63], ALU.mult, ALU.add,
                [uu.b, yy.b, CW.b], [yy.b])
            tt("dve", cvt[:, :], pB[:, 0:384], yy[:, :], ALU.mult, pBb + [yy.b], [cvt.b])
            store(CTs[c][:, tb * 384:(tb + 1) * 384], cvt[:, :], [cvt.b], [DB("CT")], "st_cvt%d" % (it % 2))
            it += 1
            nx = next(bwdgen, None)
            if nx is not None:
                nx()
            tick()
    for nx in bwdgen:
        nx()
    end_phase()
    stB.close()

    stC = ExitStack()
    WO = sb("wo", [128, KT, D], BF16, False, stC)
    WOb = [Buf("wo%d" % i) for i in range(KT)]
    wov = w_out.rearrange("(kt p) c -> p kt c", p=128)
    wlC = make_wload(2, ("act",), stC)
    MT = sb("mt2", [128, 1024])
    QDT = sb("qdt2", [128, 2, NH, 128])
    dma("sp", MT[:, :], TMT[:, :], [DB("TMT")], [MT.b], "ld_mt")
    dma("sp", QDT[:, :, :, :].rearrange("p a h i -> p (a h i)"), TQD[:, :], [DB("TQD")], [QDT.b], "ld_qd")
    L = {}
    for nm, shp, dt in [("qt", [128, NH, 128], BF16), ("kt", [128, NH, 128], BF16), ("v", [128, 1024], BF16),
                        ("g", [128, 1024], BF16), ("sf", [128, 1024], BF16), ("sb", [128, 1024], BF16),
                        ("qf", [128, NH, 128], BF16), ("qb", [128, NH, 128], BF16), ("pt", [128, 1024], BF16),
                        ("ret", [128, 1024], BF16), ("rett", [128, NH, 128], BF16)]:
        L[nm] = [sb("c1%s%d" % (nm, i), shp, dt) for i in range(2)]
    OSQ = sb("osq", [128, 1024])
    R1c = sb("r1c", [128, 1024])
    SSO = [sb("sso%d" % i, [128, NH]) for i in range(2)]

    def C1_a(m):
        if m >= NE:
            return
        j = m % 2
        qt, kt_, vv, gg, sf, sbb = L["qt"][j], L["kt"][j], L["v"][j], L["g"][j], L["sf"][j], L["sb"][j]
        dma("sp", qt[:, :, :], QTs[m], [DB("QT")], [qt.b], "l_qt%d" % j)
        dma("sp", kt_[:, :, :], KTs[m], [DB("KT")], [kt_.b], "l_kt%d" % j)
        dma("sp", vv[:, :], Vs[m], [DB("V")], [vv.b], "l_v%d" % j)
        dma("sp", gg[:, :], Gs[m], [DB("G")], [gg.b], "l_g%d" % j)
        dma("sp", sf[:, :], SFs[m], [DB("SF")], [sf.b], "l_sf%d" % j)
        dma("sp", sbb[:, :], SBs[m], [DB("SB")], [sbb.b], "l_sb%d" % j)
        qf, qb = L["qf"][j], L["qb"][j]
        tt("pool", qf[:, :, :], qt[:, :, :], QDT[:, 0, :, :], ALU.mult, [qt.b, QDT.b], [qf.b])
        tt("pool", qb[:, :, :], qt[:, :, :], QDT[:, 1, :, :], ALU.mult, [qt.b, QDT.b], [qb.b])
        pA, pAb = ps_f32(0)
        for h in range(NH):
            mm(pA[:, h * 128:(h + 1) * 128], kt_[:, h, :], qt[:, h, :], True, True, [kt_.b, qt.b], [pAb[h // 4]])
        ptt = L["pt"][j]
        tt("dve", ptt[:, :], pA, MT[:, :], ALU.mult, pAb + [MT.b], [ptt.b])

    def C1_b(m):
        j = m % 2
        vv, gg, sf, sbb = L["v"][j], L["g"][j], L["sf"][j], L["sb"][j]
        qf, qb, ptt = L["qf"][j], L["qb"][j], L["pt"][j]
        pB, pBb = ps_f32(1 + j)
        for h in range(NH):
            hs = slice(h * 128, (h + 1) * 128)
            mm(pB[:, hs], ptt[:, hs], vv[:, hs], True, False, [ptt.b, vv.b], [pBb[h // 4]])
            mm(pB[:, hs], qf[:, h, :], sf[:, hs], False, False, [qf.b, sf.b], [pBb[h // 4]])
            mm(pB[:, hs], qb[:, h, :], sbb[:, hs], False, True, [qb.b, sbb.b], [pBb[h // 4]])
        act(OSQ[:, :], pB, AF.Square, pBb, [OSQ.b])
        sso = SSO[j]
        P.op("dve", lambda E, sso=sso: E.tensor_reduce(out=sso[:, :], in_=h3(OSQ[:, :]), axis=AX.X, op=ALU.add),
             r=[OSQ.b], w=[sso.b])
        rsq(sso[:, :], sso[:, :], sso.b, 1.0 / 128)

    def C1_b2(m):
        if m < 0:
            return
        j = m % 2
        gg = L["g"][j]
        sso = SSO[j]
        pB, pBb = ps_f32(1 + j)
        tt("dve", h3(R1c[:, :]), h3(pB), sso[:, :].to_broadcast([128, NH, 128]), ALU.mult, pBb + [sso.b], [R1c.b])
        ret = L["ret"][j]
        tt("dve", ret[:, :], R1c[:, :], gg[:, :], ALU.mult, [R1c.b, gg.b], [ret.b])

    def C1_c(m):
        if m < 0:
            return
        j = m % 2
        ret = L["ret"][j]
        pst, psb = ps_bf16(3)
        pv3 = pst.rearrange("p (k t) -> p k t", t=128)
        for h in range(NH):
            tr(pv3[:, h, :], ret[:, h * 128:(h + 1) * 128], [ret.b], [psb[0]])
        rett = L["rett"][j]
        act(rett[:, :, :], pv3[:, 0:NH, :], AF.Copy, [psb[0]], [rett.b])
        store(RTs[m], rett[:, :, :], [rett.b], [DB("RT")], "st_rett%d" % j)

    WM = [sb("wmc%d" % i, [128, KT // 4, 512]) for i in range(2)]
    BMB = [sb("bmbc%d" % i, [33, 512]) for i in range(2)]
    ROW = sb("rowc", [33, D])
    GB_ = sb("gbc", [128, D])
    WMB = [sb("wmcb%d" % i, [128, KT // 4, 512], BF16) for i in range(2)]
    modgen = mod_items((5,), WM, BMB, ROW, GB_, [(3, 1)], (3, 1), nq=4, WMB=WMB)
    C1_a(0)
    for m in range(NE + 2):
        C1_b2(m - 1 if m - 1 < NE else -1)
        C1_c(m - 2)
        C1_a(m + 1)
        if m < NE:
            C1_b(m)
        if m < KT:
            wlC(WO[:, m, :], wov[:, m, :], WOb[m])
        nx = next(modgen, None)
        if nx is not None:
            nx()
        tick()
    for nx in modgen:
        nx()
    end_phase()

    G1B = sb("g1b", [128, D])
    dma("sp", G1B[:, :], TG1[:, :], [DB("TG2")], [G1B.b], "ld_g1")
    MXT = [sb("mxt%d" % i, [128, KT, 128], BF16) for i in range(2)]
    XMc = [sb("xmc%d" % i, [128, D]) for i in range(3)]
    TMPX = sb("tmpx", [128, D])
    SQJ = sb("sqj2", [128, D], BF16)
    SSQ = [sb("ssq2%d" % i, [128, 1]) for i in range(2)]
    RSTD = [sb("rstd2%d" % i, [128, 1]) for i in range(2)]
    XN = [sb("xn2%d" % i, [128, D], BF16) for i in range(2)]
    H2c = [sb("h2c%d" % i, [128, KT, 128], BF16) for i in range(2)]
    ctv = CTs.rearrange("c p t -> p c t")

    def C2_l(m):
        if m >= NE:
            return
        j = m % 2
        mxt, xm = MXT[j], XMc[m % 3]
        dma("sp", mxt[:, 0:8, :], RTs[m], [DB("RT")], [], "l_mxa%d" % j, wp=[mxt.b])
        dma("sp", mxt[:, 8:16, :], ctv[:, :, m * 128:(m + 1) * 128], [DB("CT")], [], "l_mxa%d" % j, wp=[mxt.b])
        dma("sp", xm[:, :], xe[m], [], [xm.b], "l_xm%d" % (m % 3))

    def C2_a(m):
        if m >= NE:
            return
        j = m % 2
        mxt, xm = MXT[j], XMc[m % 3]
        pC, pCb = ps_f32(2)
        pD, pDb = ps_f32(3)
        for cg in range(4):
            tgt = (pC if cg < 2 else pD)[:, (cg % 2) * 512:(cg % 2 + 1) * 512]
            tb_ = (pCb if cg < 2 else pDb)[cg % 2]
            for kt in range(KT):
                mm(tgt, mxt[:, kt, :], WO[:, kt, cg * 512:(cg + 1) * 512], kt == 0, kt == KT - 1,
                   [mxt.b, WOb[kt]], [tb_])
        tt("dve", TMPX[:, 0:1024], pC, G1B[:, 0:1024], ALU.mult, pCb + [G1B.b], [], wp=[TMPX.b])
        tt("dve", TMPX[:, 1024:2048], pD, G1B[:, 1024:2048], ALU.mult, pDb + [G1B.b], [], wp=[TMPX.b])
        tt("pool", xm[:, :], xm[:, :], TMPX[:, :], ALU.add, [xm.b, TMPX.b], [xm.b])
        store(XM[m], xm[:, :], [xm.b], [DB("XM")], "st_xm%d" % (m % 3))
        rms_n(xm, SSQ[j], RSTD[j], XN[j], SQJ)

    def C2_b(m):
        if m < 0:
            return
        j = m % 2
        h2 = H2c[j]
        tr16(XN[j], j, h2, 3, 2)
        store(H2T[:, :, m * 128:(m + 1) * 128], h2[:, :, :], [h2.b], [DB("H2T")], "st_h2%d" % j)

    C2_l(0)
    C2_l(1)
    for m in range(NE):
        C2_a(m)
        C2_b(m - 1)
        tick()
        C2_l(m + 2)
    C2_b(NE - 1)
    end_phase()
    stC.close()

    H2O = sb("h2o", [128, KT, NE * 128], BF16)
    for q in range(4):
        dma("sp", H2O[:, q * 4:(q + 1) * 4, :], H2T[:, q * 4:(q + 1) * 4, :], [DB("H2T")], [], "h2o", wp=[H2O.b])
    WU = [sb("wu%d" % i, [128, 2, KT, 128], BF16) for i in range(2)]
    wl = make_wload(3, ("pool",))
    wuv = w_up.rearrange("(kt p) c -> p kt c", p=128)
    ASB = [sb("asb%d" % i, [128, 2176]) for i in range(2)]
    YD = [sb("yd%d" % i, [128, 2048]) for i in range(2)]
    SD = [sb("sd%d" % i, [128, 2048], BF16) for i in range(2)]
    UD = [sb("ud%d" % i, [128, 2048], BF16) for i in range(2)]
    utv = UT.rearrange("n p c t -> p n c t")
    ablk = [(64, 512), (576, 512), (1088, 512), (1600, 512), (2112, 128)]
    pa_i = 0
    pb_i = 0

    def D_wl(c):
        if c >= NFT:
            return
        wu = WU[c % 2]
        wl(wu[:, 0, :, :], wuv[:, :, c * 128:(c + 1) * 128], wu.b, a=KT)
        wl(wu[:, 1, :, :], wuv[:, :, DFF + c * 128:DFF + (c + 1) * 128], wu.b, a=KT)

    D_wl(0)
    for c in range(NFT):
        wu = WU[c % 2]
        D_wl(c + 1)
        asb, yd, sd, ud = ASB[c % 2], YD[c % 2], SD[c % 2], UD[c % 2]
        for bi, (t0, n) in enumerate(ablk):
            pt, pb = ps_f32(pa_i % 2, (pa_i // 2) % 2)
            pa_i += 1
            for kt in range(KT):
                mm(pt[:, 0:n], wu[:, 0, kt, :], H2O[:, kt, t0:t0 + n], kt == 0, kt == KT - 1, [wu.b, H2O.b], pb)
            a0 = t0 - 64
            if bi == 0:
                act(asb[:, 0:64], pt[:, 0:64], AF.Copy, pb + [AFL.b], [], wp=[asb.b], scale=AFL[:, 0:1])
                act(asb[:, 64:512], pt[:, 64:512], AF.Copy, pb, [], wp=[asb.b])
            elif bi == 4:
                act(asb[:, a0:a0 + 64], pt[:, 0:64], AF.Copy, pb, [], wp=[asb.b])
                act(asb[:, a0 + 64:a0 + 128], pt[:, 64:128], AF.Copy, pb + [AFL.b], [], wp=[asb.b],
                    scale=AFL[:, 1:2])
            else:
                act(asb[:, a0:a0 + n], pt[:, 0:n], AF.Copy, pb, [], wp=[asb.b])
        act(yd[:, :], asb[:, 64:2112], AF.Identity, [asb.b, CW2.b], [yd.b], scale=CW2[:, c, 1:2], bias=CW2[:, c, 3:4])
        stt(yd[:, :], asb[:, 0:2048], CW2[:, c, 0:1], yd[:, :], ALU.mult, ALU.add, [asb.b, yd.b, CW2.b], [yd.b])
        stt(yd[:, :], asb[:, 128:2176], CW2[:, c, 2:3], yd[:, :], ALU.mult, ALU.add, [asb.b, yd.b, CW2.b], [yd.b])
        act(sd[:, :], yd[:, :], AF.Silu, [yd.b], [sd.b])
        for bi in range(4):
            pt, pb = ps_f32(2 + pb_i % 2, (pb_i // 2) % 2)
            pb_i += 1
            t0 = 128 + bi * 512
            for kt in range(KT):
                mm(pt, wu[:, 1, kt, :], H2O[:, kt, t0:t0 + 512], kt == 0, kt == KT - 1, [wu.b, H2O.b], pb)
            tt("dve", ud[:, bi * 512:(bi + 1) * 512], sd[:, bi * 512:(bi + 1) * 512], pt, ALU.mult, [sd.b] + pb, [],
               wp=[ud.b])
        store(utv[:, :, c, :], ud[:, :].rearrange("p (n t) -> p n t", t=128), [ud.b], [DB("UT")], "st_ud%d" % (c % 2))
        tick()
    end_phase()

    WD = [sb("wd%d" % i, [128, NFT, 512], BF16) for i in range(2)]
    wl = make_wload(3, ("act", "dve", "act", "pool"))
    wdv = w_down.rearrange("(kt p) c -> p kt c", p=128)
    G2B = sb("g2b", [128, D])
    FGB = sb("fgb_s", [128, D])
    dma("sp", G2B[:, :], TG2[:, :], [DB("TG5")], [G2B.b], "ld_g2")
    dma("sp", FGB[:, :], fgb[:, :], [], [FGB.b], "ld_fg")
    UTc = [sb("utc%d" % i, [128, NFT, 128], BF16) for i in range(2)]
    XMq = [sb("xmq%d" % i, [128, 512]) for i in range(2)]
    ZC = [sb("zc%d" % i, [128, 512]) for i in range(2)]
    ZJ = sb("zj", [128, 512], BF16)
    SSF = sb("ssf", [128, 16, 4])

    def E_wl(cg):
        wd = WD[cg % 2]
        for q in range(NFT // 4):
            yield (lambda q=q: wl(wd[:, q * 4:(q + 1) * 4, :], wdv[:, q * 4:(q + 1) * 4, cg * 512:(cg + 1) * 512],
                                  wd.b, a=4))

    def E_l(it):
        if it >= 64:
            return
        cg, n = it // 16, it % 16
        j = it % 2
        utc, xmq = UTc[j], XMq[j]
        dma("sp", utc[:, :, :], UT[n], [DB("UT")], [utc.b], "l_utc%d" % j)
        dma("sp", xmq[:, :], XM[n + 1][:, cg * 512:(cg + 1) * 512], [DB("XM")], [xmq.b], "l_xmq%d" % j)

    for f in E_wl(0):
        f()
    E_l(0)
    it = 0
    for cg in range(4):
        wd = WD[cg % 2]
        nxt = list(E_wl(cg + 1)) if cg + 1 < 4 else []
        for n in range(16):
            E_l(it + 1)
            if nxt and n >= 2:
                nxt.pop(0)()
            j = it % 2
            utc, xmq, zc = UTc[j], XMq[j], ZC[j]
            pt, pb = ps_f32((it // 2) % 4, it % 2)
            for kt in range(NFT):
                mm(pt, utc[:, kt, :], wd[:, kt, :], kt == 0, kt == NFT - 1, [utc.b, wd.b], pb)
            tt("dve", zc[:, :], pt, G2B[:, cg * 512:(cg + 1) * 512], ALU.mult, pb + [G2B.b], [zc.b])
            tt("dve", zc[:, :], zc[:, :], xmq[:, :], ALU.add, [zc.b, xmq.b], [zc.b])
            act(ZJ[:, :], zc[:, :], AF.Square, [zc.b], [], wp=[ZJ.b, SSF.b], accum_out=SSF[:, n, cg:cg + 1])
            store(XO[n][:, cg * 512:(cg + 1) * 512], zc[:, :], [zc.b], [DB("XO%d" % n)], "st_zc%d" % j)
            it += 1
            tick()
        while nxt:
            nxt.pop(0)()
    tick()
    tick()
    RF = sb("rf", [128, 16])
    P.op("dve", lambda E: E.tensor_reduce(out=RF[:, :], in_=SSF[:, :, :], axis=AX.X, op=ALU.add), r=[SSF.b], w=[RF.b])
    rsq(RF[:, :], RF[:, :], RF.b, 1.0 / D)
    ZF = [sb("zf%d" % i, [128, D]) for i in range(2)]
    OF = [sb("of%d" % i, [128, 1024]) for i in range(2)]
    dma("sp", ZF[0][:, :], XO[0], [DB("XO0")], [ZF[0].b], "l_zf0")
    k = 0
    for n in range(16):
        zf = ZF[n % 2]
        if n + 1 < 16:
            dma("sp", ZF[(n + 1) % 2][:, :], XO[n + 1], [DB("XO%d" % (n + 1))], [ZF[(n + 1) % 2].b],
                "l_zf%d" % ((n + 1) % 2))
        for hf in range(2):
            of = OF[k % 2]
            stt(of[:, :], zf[:, hf * 1024:(hf + 1) * 1024], RF[:, n:n + 1], FGB[:, hf * 1024:(hf + 1) * 1024],
                ALU.mult, ALU.mult, [zf.b, RF.b, FGB.b], [of.b])
            dma(STQ, out[n][:, hf * 1024:(hf + 1) * 1024], of[:, :], [of.b], [], "st_of%d" % (k % 2),
                wp=[DB("out")])
            k += 1
    P.barrier()
    P.emit()
    pstack[0].close()
    gstack.close()
    return nc


DEBUG_OUT = ()
_NC_CACHE = {}


def _host_tables(s):
    f32 = np.float32
    G0 = 16 * s - 1
    G1 = 16 * s + 16
    ext = [G0 + m for m in range(NE)]
    extset = set(g for g in ext if 0 <= g < 64)
    others = [g for g in range(64) if g not in extset]
    efb = np.zeros((2, NO), f32)
    mfb = np.zeros((2, NO), f32)
    for c in range(2):
        efb[0, c] = 128.0 * G0 + 128.0 * (1 - c)
        efb[1, c] = 128.0 * (63 - G1) + 128.0 * c
        mfb[:, c] = 1.0
    olist = []
    for k in range(NO - 2):
        if k < len(others):
            g = others[k]
            olist.append(g)
            if g < G0:
                efb[0, 2 + k] = 128.0 * (G0 - 1 - g)
                mfb[0, 2 + k] = 1.0
            if g > G1:
                efb[1, 2 + k] = 128.0 * (g - G1 - 1)
                mfb[1, 2 + k] = 1.0
        else:
            olist.append(others[0])
    freqs = (np.float32(10000.0) ** (-np.arange(0, 64, 2, dtype=f32) / np.float32(64))).astype(f32)

    def table(g, scale):
        g = min(max(g, 0), 63)
        pos = g * 128 + np.arange(128)
        row = (pos // 64).astype(f32)
        col = (pos % 64).astype(f32)
        ar = row[:, None] * freqs[None, :]
        ac = col[:, None] * freqs[None, :]
        t = np.concatenate([np.cos(ar), np.cos(ac), np.sin(ar), np.sin(ac)], axis=1).astype(f32)
        return (t * f32(scale)).astype(f32)

    ks = f32(128.0 ** -0.5)
    ctx_t = np.concatenate([np.full((128, 64), ks, f32), np.zeros((128, 64), f32)], axis=1)
    ropek = np.stack([ctx_t, ctx_t] + [table(g, ks) for g in olist] + [table(g, ks) for g in ext]).astype(f32)
    ropeq = np.stack([table(g, 1.0) for g in ext]).astype(f32)
    vfl = np.ones((NO + NE,), f32)
    for m, g in enumerate(ext):
        if not (0 <= g < 64):
            vfl[NO + m] = 0.0
    afl = np.array([0.0 if s == 0 else 1.0, 0.0 if s == 3 else 1.0], f32)
    return ext, olist, efb, mfb, ropek, ropeq, vfl, afl


def _fm(v, nt):
    return np.ascontiguousarray(np.asarray(v, np.float32).reshape(nt, 128).T)


def kernel(x, c, ctx, c_ctx, w_mod, b_mod, norm1_g, w_in, ret_decay_fwd, ret_decay_bwd,
           conv_w, w_out, norm2_g, w_up, ffn_conv_w, ffn_conv_b, w_down, final_g):
    f32 = np.float32
    x = np.asarray(x, f32)
    ctx = np.asarray(ctx, f32)
    c = np.asarray(c, f32)
    c_ctx = np.asarray(c_ctx, f32)
    if "nc" not in _NC_CACHE:
        _NC_CACHE["nc"] = build_nc()
    nc = _NC_CACHE["nc"]
    rep = lambda v: np.ascontiguousarray(np.broadcast_to(np.asarray(v, f32)[None], (128,) + np.asarray(v).shape))
    jj = np.arange(128, dtype=f32)
    rel = jj[None, :] - jj[:, None]
    ctab = np.stack([np.maximum(rel, 0), np.maximum(-rel, 0), (rel >= 0).astype(f32), (rel < 0).astype(f32),
                     np.broadcast_to(jj[None, :] + 1, (128, 128)), np.broadcast_to(128 - jj[None, :], (128, 128))],
                    axis=1).astype(f32)
    pcol = np.stack([127 - jj, jj], axis=1).astype(f32)
    shared = {
        "w_mod": np.ascontiguousarray(np.asarray(w_mod, f32)[0]),
        "w_in": np.ascontiguousarray(np.asarray(w_in, f32)[0]),
        "w_out": np.ascontiguousarray(np.asarray(w_out, f32)[0]),
        "w_up": np.ascontiguousarray(np.asarray(w_up, f32)[0]),
        "w_down": np.ascontiguousarray(np.asarray(w_down, f32)[0]),
        "gn": np.ascontiguousarray(np.stack([_fm(norm1_g[0], KT), _fm(norm2_g[0], KT)], axis=1)),
        "fgb": rep(final_g),
        "dlog": rep(np.concatenate([np.asarray(ret_decay_fwd, f32)[0], np.asarray(ret_decay_bwd, f32)[0]])),
        "cw": np.ascontiguousarray(np.stack([_fm(np.asarray(conv_w, f32)[0, t], 8) for t in range(3)], axis=2)),
        "cw2": np.ascontiguousarray(np.stack([_fm(np.asarray(ffn_conv_w, f32)[0, t], NFT) for t in range(3)]
                                             + [_fm(np.asarray(ffn_conv_b, f32)[0], NFT)], axis=2)),
        "ctab": ctab, "pcol": pcol, "ident": np.eye(128, dtype=f32),
    }
    bm = np.zeros((33, 6 * D), f32)
    bm[0] = np.asarray(b_mod, f32)[0]
    bm[32] = np.asarray(b_mod, f32)[0]
    shared["bm"] = bm
    in_maps = []
    for core in range(8):
        b, s = core // 4, core % 4
        ext, olist, efb, mfb, ropek, ropeq, vfl, afl = _host_tables(s)
        xc = x[b].reshape(64, 128, D)
        xo = np.concatenate([ctx[b].reshape(2, 128, D), xc[olist]], axis=0)
        xe = np.zeros((NE, 128, D), f32)
        for m, g in enumerate(ext):
            if 0 <= g < 64:
                xe[m] = xc[g]
        cT = np.zeros((128, KT, 33), f32)
        cT[:, :, 0] = _fm(c[b], KT)
        cT[:, :, 32] = _fm(c_ctx, KT)
        d = dict(shared)
        d.update({"xo": np.ascontiguousarray(xo), "xe": xe, "ropek": ropek, "ropeq": ropeq,
                  "efb": rep(efb), "mfb": rep(mfb), "vfl": rep(vfl), "afl": rep(afl), "cT": cT})
        in_maps.append(d)
    res = run_bass_kernel_spmd(nc, in_maps, core_ids=list(range(8)))
    _NC_CACHE["res"] = res
    outp = np.zeros((2, 8192, D), f32)
    for core in range(8):
        b, s = core // 4, core % 4
        outp[b, s * 2048:(s + 1) * 2048] = np.asarray(res.results[core]["out"]).reshape(2048, D)
    return outp
```
